# Optimizing a Trainium2 kernel written in Bass

```python
import math
import jax, jax.numpy as jnp
from jax import lax
import numpy as np

D_MODEL = 1024
BATCH = 8
SEQ = 4096
DEPTH = 1
DEC_BATCH = 16
DEC_SEQ = 2048
PAST_LEN = 128

HEAD_DIM = 64
N_HEADS_A = 8
N_HEADS_B = 4
N_HEADS_TOTAL = N_HEADS_A + N_HEADS_B
DILATED_BRANCHES = ((128, 1), (512, 4), (2048, 16))
A_WIDTH = N_HEADS_A * HEAD_DIM
B_QK_WIDTH = N_HEADS_B * 2 * HEAD_DIM
B_V_WIDTH = N_HEADS_B * 2 * HEAD_DIM
IN_WIDTH = 3 * A_WIDTH + 2 * B_QK_WIDTH + B_V_WIDTH
MIX_WIDTH = A_WIDTH + B_V_WIDTH
NUM_BUCKETS = 32
MAX_DISTANCE = 1024
Q_BLOCK = 128
N_KEYS = 128
N_EXPERTS = N_KEYS * N_KEYS
PEER_HEADS = 8
D_KEY = 256
D_KEY_HALF = D_KEY // 2
PEER_TOPK = 16
TOKEN_CHUNK = 128
NORM_EPS = 1e-6
NEG_INF = -1e30
ATTN_SCALE = HEAD_DIM ** -0.5

kernel_name = 'hymba_dilated_diff_peer_encoder'


def rms_norm(x, g):
    x32 = x.astype(jnp.float32)
    y = x32 * lax.rsqrt(jnp.mean(x32 * x32, axis=-1, keepdims=True) + NORM_EPS)
    return (y * g.astype(jnp.float32)).astype(x.dtype)


def rel_bucket(rel):
    nb = NUM_BUCKETS // 2
    max_exact = nb // 2
    n = jnp.abs(rel)
    large = max_exact + (jnp.log(jnp.maximum(n, 1).astype(jnp.float32) / max_exact)
                         / math.log(MAX_DISTANCE / max_exact) * (nb - max_exact)).astype(jnp.int32)
    large = jnp.minimum(large, nb - 1)
    return (rel > 0).astype(jnp.int32) * nb + jnp.where(n < max_exact, n, large)


def dilated_branch(q, k, v, bias_tab, dilation, half):
    b, s, h, c = q.shape
    blk = half
    n_len = s // dilation
    nb = -(-n_len // blk)
    pad = nb * blk - n_len

    def to_classes(t):
        return t.reshape(b, n_len, dilation, h, t.shape[-1]).transpose(0, 2, 1, 3, 4)

    qc = jnp.pad(to_classes(q), ((0, 0), (0, 0), (0, pad), (0, 0), (0, 0))).reshape(b, dilation, nb, blk, h, c)

    def band(t):
        tp = jnp.pad(to_classes(t), ((0, 0), (0, 0), (blk, blk + pad), (0, 0), (0, 0)))
        tp = tp.reshape(b, dilation, nb + 2, blk, h, t.shape[-1])
        return jnp.concatenate([tp[:, :, :-2], tp[:, :, 1:-1], tp[:, :, 2:]], axis=3)

    kb, vb = band(k), band(v)
    off = jnp.arange(3 * blk)[None, :] - blk - jnp.arange(blk)[:, None]
    key_idx = jnp.arange(nb)[:, None] * blk - blk + jnp.arange(3 * blk)[None, :]
    mask = (jnp.abs(off) <= half)[None] & ((key_idx >= 0) & (key_idx < n_len))[:, None, :]
    bias = bias_tab[rel_bucket(off * dilation)].transpose(2, 0, 1).astype(jnp.float32)
    scores = jnp.einsum('bdnqhc,bdnkhc->bdnhqk', qc, kb).astype(jnp.float32) * ATTN_SCALE + bias
    scores = jnp.where(mask[:, None], scores, NEG_INF)
    lse = jax.nn.logsumexp(scores, axis=-1)
    p = jnp.exp(scores - lse[..., None])
    o = jnp.einsum('bdnhqk,bdnkhc->bdnqhc', p.astype(v.dtype), vb)

    def from_classes(t):
        t = t.reshape(b, dilation, nb * blk, h, t.shape[-1])[:, :, :n_len]
        return t.transpose(0, 2, 1, 3, 4).reshape(b, s, h, t.shape[-1])

    o = from_classes(o)
    lse_t = from_classes(lse.transpose(0, 1, 2, 4, 3)[..., None])[..., 0]
    return o, lse_t


def dilated_attention(q, k, v, bias_tab):
    outs, lses = [], []
    for window, dilation in DILATED_BRANCHES:
        o, l = dilated_branch(q, k, v, bias_tab, dilation, window // dilation // 2)
        outs.append(o)
        lses.append(l)
    w = jax.nn.softmax(jnp.stack(lses, 0), axis=0)
    return jnp.sum(w[..., None] * jnp.stack(outs, 0).astype(jnp.float32), axis=0)


def diff_attention(q1, q2, k1, k2, v, bias_tab, lam):
    b, s, h, c = q1.shape
    nq = s // Q_BLOCK
    qs = jnp.stack([q1, q2], 0).reshape(2, b, nq, Q_BLOCK, h, c).transpose(2, 0, 1, 3, 4, 5)
    kpos = jnp.arange(s)

    def block(args):
        qblk, start = args
        qpos = start + jnp.arange(Q_BLOCK)
        bias = bias_tab[rel_bucket(kpos[None, :] - qpos[:, None])].transpose(2, 0, 1).astype(jnp.float32)
        s1 = jnp.einsum('bqhc,bkhc->bhqk', qblk[0], k1).astype(jnp.float32) * ATTN_SCALE + bias
        s2 = jnp.einsum('bqhc,bkhc->bhqk', qblk[1], k2).astype(jnp.float32) * ATTN_SCALE + bias
        a = jax.nn.softmax(s1, axis=-1) - lam * jax.nn.softmax(s2, axis=-1)
        return jnp.einsum('bhqk,bkhc->bqhc', a.astype(v.dtype), v)

    o = lax.map(block, (qs, jnp.arange(nq) * Q_BLOCK))
    return o.transpose(1, 0, 2, 3, 4).reshape(b, s, h, v.shape[-1])


def peer(h, w_q, sub_keys, u, v):
    b, s, d = h.shape
    hc = h.reshape(-1, TOKEN_CHUNK, d)

    def chunk(xc):
        q = (xc @ w_q).reshape(TOKEN_CHUNK, PEER_HEADS, 2, D_KEY_HALF)
        sc = jnp.einsum('cpzk,znk->cpzn', q, sub_keys).astype(jnp.float32)
        top_v, top_i = lax.top_k(sc, PEER_TOPK)
        cand = (top_v[:, :, 0, :, None] + top_v[:, :, 1, None, :]).reshape(TOKEN_CHUNK, PEER_HEADS, -1)
        cand_idx = (top_i[:, :, 0, :, None] * N_KEYS + top_i[:, :, 1, None, :]).reshape(TOKEN_CHUNK, PEER_HEADS, -1)
        best, pos = lax.top_k(cand, PEER_TOPK)
        eidx = jnp.take_along_axis(cand_idx, pos, axis=-1)
        g = jax.nn.softmax(best, axis=-1)
        act = jax.nn.gelu(jnp.einsum('cd,cpkd->cpk', xc, u[eidx]).astype(jnp.float32), approximate=False)
        return jnp.einsum('cpk,cpkd->cd', (g * act).astype(xc.dtype), v[eidx])

    return lax.map(chunk, hc).reshape(b, s, d)


def encoder(x, rel_bias, attn_norm_g, w_in, q_norm_a, k_norm_a, q_norm_b, k_norm_b,
            lambda_q1, lambda_k1, lambda_q2, lambda_k2, diff_norm_g, w_out, ffn_norm_g,
            peer_w_q, peer_sub_keys, peer_u, peer_v):
    b, s, _ = x.shape
    bias_a = rel_bias[:, :N_HEADS_A]
    bias_b = rel_bias[:, N_HEADS_A:]
    for l in range(DEPTH):
        lambda_init = 0.8 - 0.6 * math.exp(-0.3 * l)
        n = rms_norm(x, attn_norm_g[l])
        proj = n @ w_in[l]
        qa, ka, va, qb, kb, vb = jnp.split(proj, np.cumsum(
            [A_WIDTH, A_WIDTH, A_WIDTH, B_QK_WIDTH, B_QK_WIDTH]).tolist(), axis=-1)
        qa = rms_norm(qa.reshape(b, s, N_HEADS_A, HEAD_DIM), q_norm_a[l])
        ka = rms_norm(ka.reshape(b, s, N_HEADS_A, HEAD_DIM), k_norm_a[l])
        va = va.reshape(b, s, N_HEADS_A, HEAD_DIM)
        o_a = dilated_attention(qa, ka, va, bias_a).reshape(b, s, A_WIDTH).astype(x.dtype)

        qb = rms_norm(qb.reshape(b, s, N_HEADS_B, 2, HEAD_DIM), q_norm_b[l])
        kb = rms_norm(kb.reshape(b, s, N_HEADS_B, 2, HEAD_DIM), k_norm_b[l])
        vb = vb.reshape(b, s, N_HEADS_B, 2 * HEAD_DIM)
        lam = (jnp.exp(jnp.sum(lambda_q1[l].astype(jnp.float32) * lambda_k1[l].astype(jnp.float32)))
               - jnp.exp(jnp.sum(lambda_q2[l].astype(jnp.float32) * lambda_k2[l].astype(jnp.float32)))
               + lambda_init)
        o_b = diff_attention(qb[..., 0, :], qb[..., 1, :], kb[..., 0, :], kb[..., 1, :], vb, bias_b, lam)
        o_b = (rms_norm(o_b, diff_norm_g[l]) * (1.0 - lambda_init)).reshape(b, s, B_V_WIDTH).astype(x.dtype)

        x = x + jnp.concatenate([o_a, o_b], axis=-1) @ w_out[l]
        x = x + peer(rms_norm(x, ffn_norm_g[l]), peer_w_q[l], peer_sub_keys[l], peer_u[l], peer_v[l])
    return x


def setup_inputs(seed: int = 0) -> dict:
    key = jax.random.key(seed)
    ks = jax.random.split(key, 20)
    f32 = jnp.float32

    def gain(k, shape):
        return 1.0 + 0.01 * jax.random.normal(k, shape, f32)

    return {
        'x_prompt': jax.random.normal(ks[0], (BATCH, SEQ, D_MODEL), f32),
        'x_sample': jax.random.normal(ks[1], (DEC_BATCH, DEC_SEQ, D_MODEL), f32),
        'rel_bias': 0.1 * jax.random.normal(ks[2], (NUM_BUCKETS, N_HEADS_TOTAL), f32),
        'attn_norm_g': gain(ks[3], (DEPTH, D_MODEL)),
        'w_in': jax.random.normal(ks[4], (DEPTH, D_MODEL, IN_WIDTH), f32) * D_MODEL ** -0.5,
        'q_norm_a': gain(ks[5], (DEPTH, HEAD_DIM)),
        'k_norm_a': gain(ks[6], (DEPTH, HEAD_DIM)),
        'q_norm_b': gain(ks[7], (DEPTH, HEAD_DIM)),
        'k_norm_b': gain(ks[8], (DEPTH, HEAD_DIM)),
        'lambda_q1': 0.1 * jax.random.normal(ks[9], (DEPTH, HEAD_DIM), f32),
        'lambda_k1': 0.1 * jax.random.normal(ks[10], (DEPTH, HEAD_DIM), f32),
        'lambda_q2': 0.1 * jax.random.normal(ks[11], (DEPTH, HEAD_DIM), f32),
        'lambda_k2': 0.1 * jax.random.normal(ks[12], (DEPTH, HEAD_DIM), f32),
        'diff_norm_g': gain(ks[13], (DEPTH, 2 * HEAD_DIM)),
        'w_out': jax.random.normal(ks[14], (DEPTH, MIX_WIDTH, D_MODEL), f32) * MIX_WIDTH ** -0.5,
        'ffn_norm_g': gain(ks[15], (DEPTH, D_MODEL)),
        'peer_w_q': jax.random.normal(ks[16], (DEPTH, D_MODEL, PEER_HEADS * D_KEY), f32) * D_MODEL ** -0.5,
        'peer_sub_keys': jax.random.normal(ks[17], (DEPTH, 2, N_KEYS, D_KEY_HALF), f32) * D_KEY_HALF ** -0.5,
        'peer_u': jax.random.normal(ks[18], (DEPTH, N_EXPERTS, D_MODEL), f32) * D_MODEL ** -0.5,
        'peer_v': jax.random.normal(ks[19], (DEPTH, N_EXPERTS, D_MODEL), f32) * D_MODEL ** -0.5,
    }


def reference(x_prompt, x_sample, rel_bias, attn_norm_g, w_in, q_norm_a, k_norm_a, q_norm_b, k_norm_b,
              lambda_q1, lambda_k1, lambda_q2, lambda_k2, diff_norm_g, w_out, ffn_norm_g,
              peer_w_q, peer_sub_keys, peer_u, peer_v):
    y_prompt = encoder(x_prompt, rel_bias, attn_norm_g, w_in, q_norm_a, k_norm_a, q_norm_b, k_norm_b,
                       lambda_q1, lambda_k1, lambda_q2, lambda_k2, diff_norm_g, w_out, ffn_norm_g,
                       peer_w_q, peer_sub_keys, peer_u, peer_v)
    y_sample = encoder(x_sample, rel_bias, attn_norm_g, w_in, q_norm_a, k_norm_a, q_norm_b, k_norm_b,
                       lambda_q1, lambda_k1, lambda_q2, lambda_k2, diff_norm_g, w_out, ffn_norm_g,
                       peer_w_q, peer_sub_keys, peer_u, peer_v)
    return (y_prompt, y_sample)
```

```python
import math
from contextlib import ExitStack
import numpy as np
import concourse.bass as bass
import concourse.mybir as mybir
from concourse.bass_utils import run_bass_kernel_spmd

F32 = mybir.dt.float32
BF16 = mybir.dt.bfloat16
U32 = mybir.dt.uint32
AF = mybir.ActivationFunctionType
ALU = mybir.AluOpType
AX = mybir.AxisListType

D = 1024
KD = 8
EPS = 1e-6
NEG = -30000.0
NDS = 56


class Dep:
    __slots__ = ("w", "r")

    def __init__(self):
        self.w = None
        self.r = {}


class Eng:
    def __init__(self, name, eng, sem):
        self.name = name
        self.eng = eng
        self.sem = sem
        self.count = 0
        self.seen = {}


class Sched:
    def __init__(self, nc, stack):
        self.nc = nc
        self.E = {}
        for name, eng in [("pe", nc.tensor), ("act", nc.scalar), ("dve", nc.vector),
                          ("pool", nc.gpsimd), ("sp", nc.sync)]:
            sem = stack.enter_context(nc.semaphore(f"s_{name}"))
            self.E[name] = Eng(name, eng, sem)
        self.dsems = []
        self.dpool = {"sp": [], "pool": []}
        for i in range(NDS):
            sem = stack.enter_context(nc.semaphore(f"dq{i}"))
            self.dsems.append([sem, 0])
            self.dpool["pool" if i >= NDS - 16 else "sp"].append(i)
        self.dnext = {"sp": 0, "pool": 0}
        self.ninst = 0

    def _wait(self, e, deps):
        best = {}
        for (key, sem, val) in deps:
            if key == "pe" and e.name == "pe":
                continue
            if val > best.get(key, (None, 0))[1]:
                best[key] = (sem, val)
        for key, (sem, val) in best.items():
            if e.seen.get(key, 0) < val:
                e.eng.wait_ge(sem, val)
                e.seen[key] = val

    @staticmethod
    def _deps(reads, writes):
        d = []
        for t in reads:
            if t.w is not None:
                d.append(t.w)
        for t in writes:
            if t.w is not None:
                d.append(t.w)
            d.extend(t.r.values())
        return d

    @staticmethod
    def _mark(tok, reads, writes):
        for t in writes:
            t.w = tok
            t.r = {}
        for t in reads:
            t.r[tok[0]] = tok

    def op(self, en, fn, reads=(), writes=()):
        e = self.E[en]
        self._wait(e, self._deps(reads, writes))
        ins = fn(e.eng)
        e.count += 1
        ins.then_inc(e.sem, 1)
        self._mark((en, e.sem, e.count), reads, writes)
        self.ninst += 1

    def group(self, en, fns, reads=(), writes=()):
        e = self.E[en]
        self._wait(e, self._deps(reads, writes))
        ins = None
        for fn in fns:
            ins = fn(e.eng)
            self.ninst += 1
        e.count += 1
        ins.then_inc(e.sem, 1)
        self._mark((en, e.sem, e.count), reads, writes)

    def dma(self, qn, out, in_, reads=(), writes=(), **kw):
        e = self.E[qn]
        pl = self.dpool[qn]
        idx = pl[self.dnext[qn]]
        self.dnext[qn] = (self.dnext[qn] + 1) % len(pl)
        slot = self.dsems[idx]
        deps = self._deps(reads, writes)
        key = ("d", idx)
        if slot[1] > 0:
            deps.append((key, slot[0], slot[1]))
        self._wait(e, deps)
        ins = e.eng.dma_start(out=out, in_=in_, **kw)
        slot[1] += 16
        ins.then_inc(slot[0], 16)
        self._mark((key, slot[0], slot[1]), reads, writes)
        self.ninst += 1

    def barrier(self):
        for e in self.E.values():
            for o in self.E.values():
                if o.count == 0:
                    continue
                if e.seen.get(o.name, 0) < o.count:
                    e.eng.wait_ge(o.sem, o.count)
                    e.seen[o.name] = o.count
            for idx, slot in enumerate(self.dsems):
                key = ("d", idx)
                if slot[1] > 0 and e.seen.get(key, 0) < slot[1]:
                    e.eng.wait_ge(slot[0], slot[1])
                    e.seen[key] = slot[1]

    def finish(self):
        e = self.E["sp"]
        for idx, slot in enumerate(self.dsems):
            if slot[1] > 0 and e.seen.get(("d", idx), 0) < slot[1]:
                e.eng.wait_ge(slot[0], slot[1])
        for name in ("pe", "act", "dve", "pool"):
            o = self.E[name]
            if o.count > 0:
                e.eng.wait_ge(o.sem, o.count)


def pipeline(items, first, second, depth=1):
    items = list(items)
    n = len(items)
    for i in range(n + depth):
        if i < n:
            first(items[i])
        if i - depth >= 0:
            second(items[i - depth])


def rel_bucket_np(rel):
    nb = 16
    max_exact = 8
    n = np.abs(rel)
    with np.errstate(divide="ignore"):
        large = max_exact + (np.log(np.maximum(n, 1).astype(np.float32) / np.float32(max_exact))
                             / np.float32(math.log(1024 / max_exact)) * (nb - max_exact)).astype(np.int32)
    large = np.minimum(large, nb - 1)
    return (rel > 0).astype(np.int32) * nb + np.where(n < max_exact, n, large)


RB = 4095
FAR = 559


def host_consts():
    c = {}
    c["ident"] = np.eye(128, dtype=np.float32)
    c["antiI"] = np.ascontiguousarray(np.eye(128, dtype=np.float32)[::-1])
    bo = np.zeros((128, 128), np.float32)
    bo[:64, :64] = 1
    bo[64:, 64:] = 1
    c["blockones"] = bo
    c["iota"] = np.tile(np.arange(128, dtype=np.float32)[None, :], (128, 1))
    i = np.arange(8192)
    ohb = np.zeros((32, 8192), np.float32)
    ohb[rel_bucket_np(RB - i), i] = 1.0
    c["ohb"] = ohb
    oha = np.zeros((3, 33, 384), np.float32)
    for di, d in enumerate((1, 4, 16)):
        for ii in range(384):
            rm = 191 - ii
            if abs(rm) <= 64:
                oha[di, rel_bucket_np(np.array(rm * d)), ii] = 1.0
            else:
                oha[di, 32, ii] = 1.0
    c["oha"] = oha
    sel = np.zeros((128, 64), np.float32)
    sel[64, :] = 1.0
    c["sel65"] = sel
    return c


def build(seq_lens, n_exp_chunks=128, debug=False):
    nc = bass.Bass("TRN2", target_bir_lowering=False)
    NSEQ = len(seq_lens)
    TTOT = sum(seq_lens)
    SMAX = max(seq_lens)

    def din(name, shape, dt=F32):
        return nc.dram_tensor(name, list(shape), dt, kind="ExternalInput").ap()

    xs = [din(f"x{i}", [S, D]) for i, S in enumerate(seq_lens)]
    ys = [nc.dram_tensor(f"y{i}", [S, D], F32, kind="ExternalOutput").ap() for i, S in enumerate(seq_lens)]
    rel_bias = din("rel_bias", [32, 12])
    attn_g = din("attn_norm_g", [D])
    w_in = din("w_in", [D, 3072])
    qn_a = din("q_norm_a", [64])
    kn_a = din("k_norm_a", [64])
    qn_b = din("q_norm_b", [64])
    kn_b = din("k_norm_b", [64])
    lq1 = din("lambda_q1", [64])
    lk1 = din("lambda_k1", [64])
    lq2 = din("lambda_q2", [64])
    lk2 = din("lambda_k2", [64])
    dn_g = din("diff_norm_g", [128])
    w_out = din("w_out", [D, D])
    ffn_g = din("ffn_norm_g", [D])
    w_q = din("peer_w_q", [D, 2048])
    subk = din("peer_sub_keys", [2, 128, 128])
    pu = din("peer_u", [16384, D])
    pv = din("peer_v", [16384, D])
    c_ident = din("c_ident", [128, 128])
    c_anti = din("c_antiI", [128, 128])
    c_bo = din("c_blockones", [128, 128])
    c_iota = din("c_iota", [128, 128])
    c_ohb = din("c_ohb", [32, 8192])
    c_oha = din("c_oha", [3, 33, 384])
    c_sel = din("c_sel65", [128, 64])

    def dscr(name, shape, dt):
        return nc.dram_tensor(name, list(shape), dt, kind="Internal")

    win_bf = dscr("win_bf", [128, KD, 3072], BF16).ap()
    GB_h = dscr("GBseq", [4, 8192], BF16)
    GA_h = dscr("GAseq", [8, 3, 384], BF16)
    oT_scr = [dscr(f"oT{i}", [D, S], BF16).ap() for i, S in enumerate(seq_lens)]
    wq_scr = dscr("wq_bf", [128, KD, 2048], BF16).ap()
    UT_scr = dscr("UTs", [128, 128, KD, 128], BF16).ap()
    V_scr = dscr("Vs", [16384, D], BF16).ap()
    dbg = {}
    if debug:
        for i, S in enumerate(seq_lens):
            dbg[f"dbg_x1_{i}"] = nc.dram_tensor(f"dbg_x1_{i}", [S, D], F32, kind="ExternalOutput").ap()

    with ExitStack() as st:
        S_ = Sched(nc, st)

        def sb(name, shape, dt):
            return st.enter_context(nc.sbuf_tensor(name, list(shape), dt))

        ident_f = sb("ident_f", [128, 128], F32)
        ident_b = sb("ident_b", [128, 128], BF16)
        anti_b = sb("anti_b", [128, 128], BF16)
        bo_b = sb("bo_b", [128, 128], BF16)
        ones_b = sb("ones_b", [128, 128], BF16)
        ones_f = sb("ones_f", [128, 128], F32)
        iota_f = sb("iota_f", [128, 128], F32)
        sel_f = sb("sel_f", [128, 64], F32)
        gq_a = sb("gq_a", [128, 1], F32)
        gk_a = sb("gk_a", [128, 1], F32)
        gq_b = sb("gq_b", [128, 1], F32)
        gk_b = sb("gk_b", [128, 1], F32)
        gdiff = sb("gdiff", [128, 1], F32)
        neglam = sb("neglam", [128, 1], F32)
        biasfar = sb("biasfar", [128, 4, 2], F32)
        gA = sb("gA", [128, KD], F32)
        gF = sb("gF", [128, KD], F32)
        T_const = Dep()
        psum = [st.enter_context(nc.psum_tensor(f"ps{i}", [128, 512], F32)) for i in range(8)]
        PS = [Dep() for _ in range(8)]
        PSA = [PS[4], PS[6], PS[7]]
        T_oTs = [Dep() for _ in seq_lens]
        T_y = Dep()
        stA = ExitStack()
        BMA = stA.enter_context(nc.sbuf_tensor('BMA', [128, 8, 3, 2, 128], BF16))

        with ExitStack() as s0:
            def sb0(name, shape, dt):
                return s0.enter_context(nc.sbuf_tensor(name, list(shape), dt))
            stage = sb0("stage0", [128, 4096], F32)
            T_stage = Dep()
            tmp = sb0("tmp0", [128, 512], F32)
            T_tmp = Dep()
            S_.dma("sp", ident_f[:], c_ident, writes=[T_const])
            S_.dma("sp", iota_f[:], c_iota, writes=[T_const])
            S_.dma("sp", sel_f[:], c_sel, writes=[T_const])
            S_.dma("sp", stage[:, 0:128], c_anti, writes=[T_stage])
            S_.dma("sp", stage[:, 128:256], c_bo, writes=[T_stage])
            S_.op("dve", lambda e: e.tensor_copy(out=ident_b[:], in_=ident_f[:]), reads=[T_const], writes=[T_const])
            S_.op("dve", lambda e: e.tensor_copy(out=anti_b[:], in_=stage[:, 0:128]), reads=[T_stage], writes=[T_const])
            S_.op("dve", lambda e: e.tensor_copy(out=bo_b[:], in_=stage[:, 128:256]), reads=[T_stage], writes=[T_const])
            S_.op("dve", lambda e: e.memset(ones_b[:], 1.0), writes=[T_const])
            S_.op("dve", lambda e: e.memset(ones_f[:], 1.0), writes=[T_const])
            T_g = Dep()
            graw = sb0("graw", [128, 8], F32)
            for j, src in enumerate((qn_a, kn_a, qn_b, kn_b)):
                for hh in range(2):
                    S_.dma("sp", graw[hh * 64:(hh + 1) * 64, j:j + 1], src.rearrange("(c o) -> c o", o=1), writes=[T_g])
            S_.dma("sp", graw[:, 4:5], dn_g.rearrange("(c o) -> c o", o=1), writes=[T_g])
            S_.op("dve", lambda e: e.tensor_scalar_mul(out=gq_a[:], in0=graw[:, 0:1], scalar1=0.125), reads=[T_g], writes=[T_const])
            S_.op("dve", lambda e: e.tensor_copy(out=gk_a[:], in_=graw[:, 1:2]), reads=[T_g], writes=[T_const])
            S_.op("dve", lambda e: e.tensor_scalar_mul(out=gq_b[:], in0=graw[:, 2:3], scalar1=0.125), reads=[T_g], writes=[T_const])
            S_.op("dve", lambda e: e.tensor_copy(out=gk_b[:], in_=graw[:, 3:4]), reads=[T_g], writes=[T_const])
            S_.op("dve", lambda e: e.tensor_scalar_mul(out=gdiff[:], in0=graw[:, 4:5], scalar1=0.8), reads=[T_g], writes=[T_const])
            S_.dma("sp", gA[:], attn_g.rearrange("(k p) -> p k", p=128), writes=[T_const], allow_slow_non_contiguous=True)
            S_.dma("sp", gF[:], ffn_g.rearrange("(k p) -> p k", p=128), writes=[T_const], allow_slow_non_contiguous=True)
            lam4 = sb0("lam4", [128, 4, 64], F32)
            T_l = Dep()
            for j, src in enumerate((lq1, lk1, lq2, lk2)):
                S_.dma("sp", lam4[:, j, :], src.partition_broadcast(128), writes=[T_l])
            lamw = sb0("lamw", [128, 8], F32)
            T_lw = Dep()
            prod = sb0("lprod", [128, 2, 64], F32)
            S_.op("dve", lambda e: e.tensor_tensor(out=prod[:, 0, :], in0=lam4[:, 0, :], in1=lam4[:, 1, :], op=ALU.mult), reads=[T_l], writes=[T_lw])
            S_.op("dve", lambda e: e.tensor_tensor(out=prod[:, 1, :], in0=lam4[:, 2, :], in1=lam4[:, 3, :], op=ALU.mult), reads=[T_l, T_lw], writes=[T_lw])
            S_.op("dve", lambda e: e.reduce_sum(out=lamw[:, 0:2], in_=prod[:], axis=AX.X), reads=[T_lw], writes=[T_lw])
            S_.op("act", lambda e: e.activation(out=lamw[:, 2:4], in_=lamw[:, 0:2], func=AF.Exp), reads=[T_lw], writes=[T_lw])
            S_.op("dve", lambda e: e.tensor_tensor(out=lamw[:, 4:5], in0=lamw[:, 3:4], in1=lamw[:, 2:3], op=ALU.subtract), reads=[T_lw], writes=[T_lw])
            S_.op("dve", lambda e: e.tensor_scalar_add(out=neglam[:], in0=lamw[:, 4:5], scalar1=-0.2), reads=[T_lw], writes=[T_const])
            for hb in range(4):
                for sg, row in enumerate((15, 31)):
                    S_.dma("sp", biasfar[:, hb, sg:sg + 1],
                           rel_bias[row:row + 1, 8 + hb:9 + hb].rearrange("a b -> (a b)").partition_broadcast(128), writes=[T_const])
            tabA = sb0("tabA", [33, 12], F32)
            T_tab = Dep()
            S_.op("dve", lambda e: e.memset(tabA[:], NEG), writes=[T_tab])
            S_.dma("sp", tabA[0:32, :], rel_bias, reads=[], writes=[T_tab])
            ohb_sb = sb0("ohb_sb", [32, 8192], F32)
            oha_sb = sb0("oha_sb", [33, 3, 384], F32)
            T_oh = Dep()
            S_.dma("sp", ohb_sb[:], c_ohb, writes=[T_oh])
            S_.dma("sp", oha_sb[:], c_oha.rearrange("d b i -> b d i"), writes=[T_oh])
            seqb = sb0("seqb", [8, 8192], BF16)
            T_sq = Dep()
            for g in range(16):
                S_.op("pe", lambda e, g=g: e.matmul(psum[g % 2][0:4, :], lhsT=tabA[0:32, 8:12], rhs=ohb_sb[:, g * 512:(g + 1) * 512], start=True, stop=True),
                      reads=[T_tab, T_oh], writes=[PS[g % 2]])
                S_.op("act", lambda e, g=g: e.copy(out=seqb[0:4, g * 512:(g + 1) * 512], in_=psum[g % 2][0:4, :]), reads=[PS[g % 2]], writes=[T_sq])
            T_GB = Dep()
            S_.dma("sp", GB_h.ap(), seqb[0:4, :], reads=[T_sq], writes=[T_GB])
            seqa = sb0("seqa", [8, 3, 384], BF16)
            T_sa = Dep()
            for di in range(3):
                S_.op("pe", lambda e, di=di: e.matmul(psum[2][0:8, 0:384], lhsT=tabA[0:33, 0:8], rhs=oha_sb[:, di, :], start=True, stop=True),
                      reads=[T_tab, T_oh], writes=[PS[2]])
                S_.op("act", lambda e, di=di: e.copy(out=seqa[:, di, :], in_=psum[2][0:8, 0:384]), reads=[PS[2]], writes=[T_sa])
            T_GA = Dep()
            S_.dma("sp", GA_h.ap(), seqa[:], reads=[T_sa], writes=[T_GA])
            for h in range(8):
                for di in range(3):
                    for c in range(2):
                        src = bass.AP(GA_h, (h * 3 + di) * 384 + 128 * (1 - c), [[1, 128], [1, 128]])
                        S_.dma("sp", BMA[:, h, di, c, :], src, reads=[T_GA], writes=[T_const])
            T_win = Dep()
            wst = sb0("wst", [128, KD, 512], BF16)
            T_wst = Dep()
            for cb in range(6):
                S_.dma("sp", stage[:].rearrange("p (k c) -> p k c", k=KD),
                       w_in[:, cb * 512:(cb + 1) * 512].rearrange("(k p) c -> p k c", p=128), writes=[T_stage])
                for k in range(KD):
                    S_.op("dve" if k % 2 else "pool", lambda e, k=k: e.tensor_scalar(out=wst[:, k, :], in0=stage[:, k * 512:(k + 1) * 512], scalar1=gA[:, k:k + 1], scalar2=None, op0=ALU.mult),
                          reads=[T_stage, T_const], writes=[T_wst])
                S_.dma("sp", win_bf[:, :, cb * 512:(cb + 1) * 512], wst[:], reads=[T_wst], writes=[T_win])
            T_UV = Dep()
            gFrow = sb0("gFrow", [128, D], F32)
            T_gfr = Dep()
            S_.dma("sp", gFrow[:], ffn_g.partition_broadcast(128), writes=[T_gfr])
            NB0 = 4
            ub = [sb0(f"ub{i}", [128, D], BF16) for i in range(NB0)]
            T_ub = [Dep() for _ in range(NB0)]
            ut = [sb0(f"ut{i}", [128, KD, 128], BF16) for i in range(NB0)]
            T_ut = [Dep() for _ in range(NB0)]
            vb16 = [sb0(f"vb16{i}", [128, D], BF16) for i in range(NB0)]
            T_vb = [Dep() for _ in range(NB0)]
            ust = [sb0(f"ust{i}", [128, D], F32) for i in range(NB0)]
            T_ust = [Dep() for _ in range(NB0)]
            vst = [sb0(f"vst{i}", [128, D], F32) for i in range(NB0)]
            T_vst = [Dep() for _ in range(NB0)]
            def ld_uv(i):
                b = i % NB0
                S_.dma("sp", ust[b][:], pu[i * 128:(i + 1) * 128, :], writes=[T_ust[b]])
                S_.dma("sp", vst[b][:], pv[i * 128:(i + 1) * 128, :], writes=[T_vst[b]])
            for i in range(min(NB0 - 1, n_exp_chunks)):
                ld_uv(i)
            for i in range(n_exp_chunks):
                b = i % NB0
                pb2 = 4 + (i % 2)
                if i + NB0 - 1 < n_exp_chunks:
                    ld_uv(i + NB0 - 1)
                S_.op("dve", lambda e, b=b: e.tensor_tensor(out=ub[b][:], in0=ust[b][:], in1=gFrow[:], op=ALU.mult),
                      reads=[T_ust[b], T_gfr], writes=[T_ub[b]])
                pst = psum[pb2][:].bitcast(BF16)
                S_.group("pe", [lambda e, k=k, b=b, pst=pst: e.transpose(out=pst[:, k * 128:(k + 1) * 128], in_=ub[b][:, k * 128:(k + 1) * 128], identity=ident_b[:]) for k in range(KD)],
                         reads=[T_ub[b], T_const], writes=[PS[pb2]])
                S_.op("act", lambda e, b=b, pst=pst: e.copy(out=ut[b][:].rearrange("p k e -> p (k e)"), in_=pst), reads=[PS[pb2]], writes=[T_ut[b]])
                S_.dma("sp", UT_scr[i], ut[b][:], reads=[T_ut[b]], writes=[T_UV])
                S_.op("pool", lambda e, b=b: e.tensor_copy(out=vb16[b][:], in_=vst[b][:]), reads=[T_vst[b]], writes=[T_vb[b]])
                S_.dma("sp", V_scr[i * 128:(i + 1) * 128, :], vb16[b][:], reads=[T_vb[b]], writes=[T_UV])

        S_.barrier()
        for si, S in enumerate(seq_lens):
            S_.barrier()
            NT = S // 128
            NG = S // 512
            with ExitStack() as s1:
                def sb1(name, shape, dt):
                    return s1.enter_context(nc.sbuf_tensor(f"{name}_{si}", list(shape), dt))
                xnT = sb1("xnT", [128, KD, S], BF16)
                T_xnT = Dep()
                xt = [sb1(f"xt{i}", [128, D], F32) for i in range(2)]
                T_xt = [Dep(), Dep()]
                junk = sb1("junk", [128, D], BF16)
                T_junk = Dep()
                xnb = [sb1(f"xnb{i}", [128, D], BF16) for i in range(2)]
                T_xnb = [Dep(), Dep()]
                st4 = sb1("st4", [128, 8], F32)
                T_st4 = Dep()
                for i in range(NT):
                    b = i % 2
                    S_.dma("sp", xt[b][:], xs[si][i * 128:(i + 1) * 128, :], writes=[T_xt[b]])
                    S_.op("act", lambda e, b=b: e.activation(out=junk[:], in_=xt[b][:], func=AF.Square, accum_out=st4[:, 0:1]),
                          reads=[T_xt[b]], writes=[T_junk, T_st4])
                    S_.op("act", lambda e: e.activation(out=st4[:, 1:2], in_=st4[:, 0:1], func=AF.Sqrt, scale=1.0 / D, bias=EPS), reads=[T_st4], writes=[T_st4])
                    S_.op("dve", lambda e: e.reciprocal(out=st4[:, 2:3], in_=st4[:, 1:2]), reads=[T_st4], writes=[T_st4])
                    S_.op("dve", lambda e, b=b: e.tensor_scalar(out=xnb[b][:], in0=xt[b][:], scalar1=st4[:, 2:3], scalar2=None, op0=ALU.mult),
                          reads=[T_xt[b], T_st4], writes=[T_xnb[b]])
                    pst = psum[b][:].bitcast(BF16)
                    S_.group("pe", [lambda e, k=k, b=b, pst=pst: e.transpose(out=pst[:, k * 128:(k + 1) * 128], in_=xnb[b][:, k * 128:(k + 1) * 128], identity=ident_b[:]) for k in range(KD)],
                             reads=[T_xnb[b], T_const], writes=[PS[b]])
                    S_.op("act", lambda e, i=i, pst=pst: e.copy(out=xnT[:, :, i * 128:(i + 1) * 128], in_=pst.rearrange("p (k t) -> p k t", k=KD)),
                          reads=[PS[b]], writes=[T_xnT])

                wsl = sb1("wsl", [128, KD, 384], BF16)
                T_wsl = Dep()
                qT = sb1("qT", [128, 2, S], BF16)
                kT = sb1("kT", [128, S], BF16)
                T_qT, T_kT = Dep(), Dep()
                sq = [sb1(f"sq{i}", [128, 512], BF16) for i in range(2)]
                T_sq2 = [Dep(), Dep()]
                sd = [sb1(f"sd{i}", [128, 512], F32) for i in range(2)]
                T_sd = [Dep(), Dep()]

                S_.op("pool", lambda e: e.memset(qT[:], 0.0), writes=[T_qT])

                def qk_proj(dst, T_dst, wcol, gain, split=False):
                    def f1(g):
                        b = g % 2
                        pa = psum[b]
                        S_.group("pe", [lambda e, k=k, g=g, pa=pa: e.matmul(pa[:], lhsT=wsl[:, k, wcol:wcol + 128], rhs=xnT[:, k, g * 512:(g + 1) * 512], start=(k == 0), stop=(k == KD - 1)) for k in range(KD)],
                                 reads=[T_wsl, T_xnT], writes=[PS[b]])
                        S_.op("act", lambda e, b=b, pa=pa: e.activation(out=sq[b][:], in_=pa[:], func=AF.Square), reads=[PS[b]], writes=[T_sq2[b]])

                    def f2(g):
                        b = g % 2
                        pa, pb_ = psum[b], psum[2 + b]
                        S_.op("pe", lambda e, b=b, pb_=pb_: e.matmul(pb_[:], lhsT=bo_b[:], rhs=sq[b][:], start=True, stop=True), reads=[T_sq2[b], T_const], writes=[PS[2 + b]])
                        S_.op("act", lambda e, b=b, pb_=pb_: e.activation(out=sd[b][:], in_=pb_[:], func=AF.Sqrt, scale=1.0 / 64, bias=EPS), reads=[PS[2 + b]], writes=[T_sd[b]])
                        S_.op("dve", lambda e, b=b: e.reciprocal(out=sd[b][:], in_=sd[b][:]), reads=[T_sd[b]], writes=[T_sd[b]])
                        if split:
                            for hh_ in range(2):
                                rs_ = slice(hh_ * 64, hh_ * 64 + 64)
                                S_.op("dve", lambda e, b=b, g=g, pa=pa, rs_=rs_, hh_=hh_: e.scalar_tensor_tensor(out=dst[rs_, hh_, g * 512:(g + 1) * 512], in0=pa[rs_, :], scalar=gain[rs_, 0:1], in1=sd[b][rs_, :], op0=ALU.mult, op1=ALU.mult),
                                      reads=[PS[b], T_sd[b], T_const], writes=[T_dst])
                        else:
                            S_.op("dve", lambda e, b=b, g=g, pa=pa: e.scalar_tensor_tensor(out=dst[:, g * 512:(g + 1) * 512], in0=pa[:], scalar=gain[:, 0:1], in1=sd[b][:], op0=ALU.mult, op1=ALU.mult),
                                  reads=[PS[b], T_sd[b], T_const], writes=[T_dst])
                    pipeline(range(NG), f1, f2)

                dils = (1, 4, 16)
                with ExitStack() as s2:
                    def sb2(name, shape, dt):
                        return s2.enter_context(nc.sbuf_tensor(f"{name}_{si}", list(shape), dt))
                    Vp = [sb2(f"Vp{di}", [128, NT, 2, 65], BF16) for di in range(3)]
                    T_Vp = [Dep() for _ in range(3)]
                    acc = sb2("accA", [128, S], F32)
                    T_acc = Dep()
                    pT = [sb2(f"pTa{i}", [128, 2, 128], BF16) for i in range(3)]
                    T_pT = [Dep(), Dep(), Dep()]
                    rden = sb2("rdenA", [64, 512], F32)
                    T_rden = Dep()
                    oTt = [sb2(f"oTtA{i}", [64, 512], BF16) for i in range(2)]
                    T_oTt = [Dep(), Dep()]
                    for di in range(3):
                        S_.op("pool", lambda e, di=di: e.memset(Vp[di][:, :, :, 64:65], 1.0), writes=[T_Vp[di]])
                    for hp in range(4):
                        for j, c0 in enumerate((hp * 128, 512 + hp * 128, 1024 + hp * 128)):
                            S_.dma("sp", wsl[:, :, j * 128:(j + 1) * 128], win_bf[:, :, c0:c0 + 128], reads=[T_win], writes=[T_wsl])
                        qk_proj(qT, T_qT, 0, gq_a, split=True)
                        qk_proj(kT, T_kT, 128, gk_a)
                        cnt = 0
                        for di, d in enumerate(dils):
                            L = S // d
                            ntc = L // 128
                            for r in range(d):
                                for j0 in range(0, ntc, 4):
                                    nj = min(4, ntc - j0)
                                    b = cnt % 2
                                    cnt += 1
                                    pv_ = psum[4 + b]
                                    fns = []
                                    for jj in range(nj):
                                        j = j0 + jj
                                        t0 = r + d * 128 * j
                                        for k in range(KD):
                                            fns.append(lambda e, k=k, jj=jj, t0=t0, d=d, pv_=pv_: e.matmul(
                                                pv_[:, jj * 128:(jj + 1) * 128], lhsT=xnT[:, k, t0:t0 + d * 127 + 1:d], rhs=wsl[:, k, 256:384],
                                                start=(k == 0), stop=(k == KD - 1)))
                                    S_.group("pe", fns, reads=[T_xnT, T_wsl], writes=[PS[4 + b]])
                                    ti = r * ntc + j0
                                    S_.op("act", lambda e, di=di, ti=ti, nj=nj, pv_=pv_: e.copy(
                                        out=Vp[di][:, ti:ti + nj, :, 0:64],
                                        in_=pv_[:, 0:nj * 128].rearrange("p (j h c) -> p j h c", j=nj, h=2)),
                                        reads=[PS[4 + b]], writes=[T_Vp[di]])
                        for hh in range(2):
                            h = hp * 2 + hh
                            ro = slice(hh * 64, hh * 64 + 64)
                            allb = []
                            for di, d in enumerate(dils):
                                L = S // d
                                ntc = L // 128
                                for r in range(d):
                                    blocks = [(0, 64, [(0, 1, 64)])]
                                    for bb in range(ntc - 1):
                                        blocks.append((128 * bb + 64, 128, [(bb, 0, 0), (bb + 1, 1, 0)]))
                                    blocks.append((L - 64, 64, [(ntc - 1, 0, 0)]))
                                    for (qm0, nq, kts) in blocks:
                                        allb.append((len(allb) % 3, di, d, r, ntc, qm0, nq, kts))

                            def fA1(it):
                                b, di, d, r, ntc, qm0, nq, kts = it
                                ps_s = psum[b]
                                q0 = r + d * qm0
                                qsl = slice(q0, q0 + d * (nq - 1) + 1, d)
                                fns = []
                                for ci, (kt, ch, c0) in enumerate(kts):
                                    k0 = r + d * 128 * kt
                                    fns.append(lambda e, ci=ci, k0=k0, d=d, qsl=qsl, nq=nq, ps_s=ps_s: e.matmul(
                                        ps_s[:, ci * 128:ci * 128 + nq], lhsT=kT[:, k0:k0 + d * 127 + 1:d], rhs=qT[:, hh, qsl], start=True, stop=False))
                                    fns.append(lambda e, ci=ci, ch=ch, c0=c0, nq=nq, di=di, ps_s=ps_s: e.matmul(
                                        ps_s[:, ci * 128:ci * 128 + nq], lhsT=anti_b[:], rhs=BMA[:, h, di, ch, c0:c0 + nq], start=False, stop=True))
                                S_.group("pe", fns, reads=[T_qT, T_kT, T_const], writes=[PS[b]])
                                if nq == 128:
                                    S_.op("act", lambda e, b=b, ps_s=ps_s: e.activation(out=pT[b][:].rearrange("p c q -> p (c q)"), in_=ps_s[:, 0:256], func=AF.Exp),
                                          reads=[PS[b]], writes=[T_pT[b]])
                                else:
                                    S_.op("act", lambda e, b=b, ps_s=ps_s: e.activation(out=pT[b][:, 0, 0:64], in_=ps_s[:, 0:64], func=AF.Exp),
                                          reads=[PS[b]], writes=[T_pT[b]])

                            def fA2(it):
                                b, di, d, r, ntc, qm0, nq, kts = it
                                ps_o = psum[3 + b]
                                q0 = r + d * qm0
                                qsl = slice(q0, q0 + d * (nq - 1) + 1, d)
                                nk = len(kts)
                                fns = []
                                for ci, (kt, ch, c0) in enumerate(kts):
                                    ti = r * ntc + kt
                                    fns.append(lambda e, ci=ci, ti=ti, di=di, nq=nq, b=b, nk=nk, ps_o=ps_o: e.matmul(
                                        ps_o[0:65, 0:nq], lhsT=Vp[di][:, ti, hh, :], rhs=pT[b][:, ci, 0:nq], start=(ci == 0), stop=(ci == nk - 1)))
                                S_.group("pe", fns, reads=[T_Vp[di], T_pT[b]], writes=[PS[3 + b]])
                                if di == 0:
                                    S_.op("dve", lambda e, qsl=qsl, nq=nq, ps_o=ps_o: e.tensor_copy(out=acc[0:65, qsl], in_=ps_o[0:65, 0:nq]),
                                          reads=[PS[3 + b]], writes=[T_acc])
                                else:
                                    S_.op("dve", lambda e, qsl=qsl, nq=nq, ps_o=ps_o: e.tensor_tensor(out=acc[0:65, qsl], in0=ps_o[0:65, 0:nq], in1=acc[0:65, qsl], op=ALU.add),
                                          reads=[PS[3 + b], T_acc], writes=[T_acc])
                            pipeline(allb, fA1, fA2, depth=2)
                            for g in range(NG):
                                b = g % 2
                                pd = psum[4 + b]
                                S_.op("pe", lambda e, g=g, pd=pd: e.matmul(pd[0:64, :], lhsT=sel_f[0:65, :], rhs=acc[0:65, g * 512:(g + 1) * 512], start=True, stop=True),
                                      reads=[T_acc, T_const], writes=[PS[4 + b]])
                                S_.op("dve", lambda e, pd=pd: e.reciprocal(out=rden[:], in_=pd[0:64, :]), reads=[PS[4 + b]], writes=[T_rden])
                                S_.op("dve", lambda e, g=g, b=b: e.tensor_tensor(out=oTt[b][:], in0=acc[0:64, g * 512:(g + 1) * 512], in1=rden[:], op=ALU.mult),
                                      reads=[T_acc, T_rden], writes=[T_oTt[b]])
                                S_.dma("pool", oT_scr[si][h * 64:(h + 1) * 64, g * 512:(g + 1) * 512], oTt[b][:], reads=[T_oTt[b]], writes=[T_oTs[si]])

                S_.barrier()
                with ExitStack() as s3:
                    def sb3(name, shape, dt):
                        return s3.enter_context(nc.sbuf_tensor(f"{name}_{si}", list(shape), dt))
                    offs = list(range(-640, 1025, 128))
                    BMB = sb3("BMB", [128, len(offs), 512], BF16)
                    T_BMB = Dep()
                    vB = sb3("vB", [128, NT, 128], BF16)
                    T_vB = Dep()
                    pTb = [sb3(f"pTb{i}", [128, 512], BF16) for i in range(4)]
                    T_pTb = [Dep() for _ in range(4)]
                    fa = [sb3(f"fa{i}", [128, 512], F32) for i in range(4)]
                    T_fa = [Dep() for _ in range(4)]
                    oTb = [sb3(f"oTb{i}", [128, 512], BF16) for i in range(2)]
                    T_oTb = [Dep(), Dep()]
                    for hb in range(4):
                        for j, c0 in enumerate((1536 + hb * 128, 2048 + hb * 128, 2560 + hb * 128)):
                            S_.dma("sp", wsl[:, :, j * 128:(j + 1) * 128], win_bf[:, :, c0:c0 + 128], reads=[T_win], writes=[T_wsl])
                        for oi, off in enumerate(offs):
                            base = RB - off - 127
                            if base < 0 or base + 127 + 511 >= 8192:
                                continue
                            src = bass.AP(GB_h, hb * 8192 + base, [[1, 128], [1, 512]])
                            S_.dma("sp", BMB[:, oi, :], src, reads=[T_GB], writes=[T_BMB])
                        qk_proj(qT, T_qT, 0, gq_b, split=True)
                        qk_proj(kT, T_kT, 128, gk_b)
                        for j0 in range(0, NT, 4):
                            b = (j0 // 4) % 2
                            pv_ = psum[4 + b]
                            fns = []
                            for jj in range(4):
                                j = j0 + jj
                                for k in range(KD):
                                    fns.append(lambda e, k=k, jj=jj, j=j, pv_=pv_: e.matmul(
                                        pv_[:, jj * 128:(jj + 1) * 128], lhsT=xnT[:, k, j * 128:(j + 1) * 128], rhs=wsl[:, k, 256:384],
                                        start=(k == 0), stop=(k == KD - 1)))
                            S_.group("pe", fns, reads=[T_xnT, T_wsl], writes=[PS[4 + b]])
                            S_.op("act", lambda e, j0=j0, pv_=pv_: e.copy(out=vB[:, j0:j0 + 4, :], in_=pv_[:].rearrange("p (j c) -> p j c", j=4)),
                                  reads=[PS[4 + b]], writes=[T_vB])
                        scnt = 0
                        for qg in range(NG):
                            itemsB = []
                            for kc in range(NT):
                                for mp in range(2):
                                    itemsB.append((kc, mp, scnt % 4))
                                    scnt += 1

                            def fB1(it, qg=qg):
                                kc, mp, sl = it
                                off = kc * 128 - qg * 512
                                near = not (off + 127 <= -FAR or off - 511 >= FAR)
                                ps_s = psum[sl]
                                rr = slice(mp * 64, mp * 64 + 64)
                                fns = [lambda e, rr=rr, kc=kc, qg=qg, ps_s=ps_s, near=near: e.matmul(
                                    ps_s[:], lhsT=kT[:, kc * 128:(kc + 1) * 128], rhs=qT[:, mp, qg * 512:(qg + 1) * 512], start=True, stop=not near)]
                                rds = [T_qT, T_kT]
                                if near:
                                    oi = offs.index(off)
                                    fns.append(lambda e, oi=oi, ps_s=ps_s: e.matmul(ps_s[:], lhsT=anti_b[:], rhs=BMB[:, oi, :], start=False, stop=True))
                                    rds += [T_BMB, T_const]
                                S_.group("pe", fns, reads=rds, writes=[PS[sl]])
                                if near:
                                    S_.op("act", lambda e, sl=sl, ps_s=ps_s: e.activation(out=pTb[sl][:], in_=ps_s[:], func=AF.Exp), reads=[PS[sl]], writes=[T_pTb[sl]])
                                else:
                                    sg = 0 if off < 0 else 1
                                    S_.op("act", lambda e, sl=sl, ps_s=ps_s, sg=sg: e.activation(out=pTb[sl][:], in_=ps_s[:], func=AF.Exp, bias=biasfar[:, hb, sg:sg + 1]),
                                          reads=[PS[sl], T_const], writes=[T_pTb[sl]])

                            def fB2(it):
                                kc, mp, sl = it
                                S_.group("pe", [
                                    lambda e, sl=sl, kc=kc, mp=mp: e.matmul(psum[4 + 2 * mp][:], lhsT=vB[:, kc, :], rhs=pTb[sl][:], start=(kc == 0), stop=(kc == NT - 1)),
                                    lambda e, sl=sl, kc=kc, mp=mp: e.matmul(psum[5 + 2 * mp][:], lhsT=ones_b[:], rhs=pTb[sl][:], start=(kc == 0), stop=(kc == NT - 1)),
                                ], reads=[T_vB, T_pTb[sl], T_const], writes=[PS[4 + 2 * mp], PS[5 + 2 * mp]])
                            pipeline(itemsB, fB1, fB2, depth=2)
                            S_.op("dve", lambda e: e.reciprocal(out=fa[0][:], in_=psum[5][:]), reads=[PS[5]], writes=[T_fa[0]])
                            S_.op("dve", lambda e: e.tensor_tensor(out=fa[1][:], in0=psum[4][:], in1=fa[0][:], op=ALU.mult), reads=[PS[4], T_fa[0]], writes=[T_fa[1]])
                            S_.op("dve", lambda e: e.reciprocal(out=fa[0][:], in_=psum[7][:]), reads=[PS[7], T_fa[0]], writes=[T_fa[0]])
                            S_.op("dve", lambda e: e.tensor_tensor(out=fa[2][:], in0=psum[6][:], in1=fa[0][:], op=ALU.mult), reads=[PS[6], T_fa[0]], writes=[T_fa[2]])
                            S_.op("dve", lambda e: e.scalar_tensor_tensor(out=fa[3][:], in0=fa[2][:], scalar=neglam[:, 0:1], in1=fa[1][:], op0=ALU.mult, op1=ALU.add),
                                  reads=[T_fa[1], T_fa[2], T_const], writes=[T_fa[3]])
                            S_.op("act", lambda e: e.activation(out=fa[1][:], in_=fa[3][:], func=AF.Square), reads=[T_fa[3], T_fa[1]], writes=[T_fa[1]])
                            S_.op("pe", lambda e: e.matmul(psum[0][:], lhsT=ones_f[:], rhs=fa[1][:], start=True, stop=True), reads=[T_fa[1], T_const], writes=[PS[0]])
                            S_.op("act", lambda e: e.activation(out=fa[2][:], in_=psum[0][:], func=AF.Sqrt, scale=1.0 / 128, bias=EPS), reads=[PS[0], T_fa[2]], writes=[T_fa[2]])
                            S_.op("dve", lambda e: e.reciprocal(out=fa[2][:], in_=fa[2][:]), reads=[T_fa[2]], writes=[T_fa[2]])
                            ob = qg % 2
                            S_.op("dve", lambda e, ob=ob: e.scalar_tensor_tensor(out=oTb[ob][:], in0=fa[3][:], scalar=gdiff[:, 0:1], in1=fa[2][:], op0=ALU.mult, op1=ALU.mult),
                                  reads=[T_fa[3], T_fa[2], T_const], writes=[T_oTb[ob]])
                            S_.dma("pool", oT_scr[si][512 + hb * 128:512 + (hb + 1) * 128, qg * 512:(qg + 1) * 512], oTb[ob][:], reads=[T_oTb[ob]], writes=[T_oTs[si]])

        S_.barrier()
        stA.close()
        with ExitStack() as s4:
            def sb4(name, shape, dt):
                return s4.enter_context(nc.sbuf_tensor(name, list(shape), dt))
            wout_b = sb4("wout_b", [128, KD, D], BF16)
            wqs = [sb4(f"wqs{i}", [128, KD, 128], BF16) for i in range(2)]
            T_wqs = [Dep(), Dep()]
            T_wqscr = Dep()
            KzT = sb4("KzT", [128, 2, 128], BF16)
            T_w4 = Dep()
            s4t = ExitStack()
            stg = s4t.enter_context(nc.sbuf_tensor("stg4", [128, KD, 512], F32))
            T_stg = Dep()
            wqst = s4t.enter_context(nc.sbuf_tensor("wqst", [128, KD, 512], BF16))
            T_wqst = Dep()
            for cb in range(2):
                S_.dma("sp", stg[:], w_out[:, cb * 512:(cb + 1) * 512].rearrange("(k p) c -> p k c", p=128), writes=[T_stg])
                S_.op("dve", lambda e, cb=cb: e.tensor_copy(out=wout_b[:, :, cb * 512:(cb + 1) * 512], in_=stg[:]), reads=[T_stg], writes=[T_w4])
            for cb in range(4):
                S_.dma("sp", stg[:], w_q[:, cb * 512:(cb + 1) * 512].rearrange("(k p) c -> p k c", p=128), writes=[T_stg])
                for k in range(KD):
                    S_.op("dve", lambda e, k=k, cb=cb: e.tensor_scalar(out=wqst[:, k, :], in0=stg[:, k, :], scalar1=gF[:, k:k + 1], scalar2=None, op0=ALU.mult),
                          reads=[T_stg, T_const], writes=[T_wqst])
                S_.dma("sp", wq_scr[:, :, cb * 512:(cb + 1) * 512], wqst[:], reads=[T_wqst], writes=[T_wqscr])
            for z in range(2):
                S_.dma("sp", stg[:, 0, 0:128], subk[z], writes=[T_stg])
                S_.op("pe", lambda e: e.transpose(out=psum[0][:, 0:128], in_=stg[:, 0, 0:128], identity=ident_f[:]), reads=[T_stg, T_const], writes=[PS[0]])
                S_.op("act", lambda e, z=z: e.copy(out=KzT[:, z, :], in_=psum[0][:, 0:128]), reads=[PS[0]], writes=[T_w4])

            S_.barrier()
            s4t.close()
            TB = 256
            oTl = sb4("oTl", [128, KD, TB], BF16)
            T_oTl = Dep()
            xt4 = sb4("xt4", [128, D], F32)
            T_xt4 = Dep()
            x1 = [[sb4(f"x1_{bf}_{i}", [128, D], F32) for i in range(2)] for bf in range(2)]
            T_x1 = [[Dep(), Dep()] for _ in range(2)]
            st5 = sb4("st5", [128, 8], F32)
            T_st5 = Dep()
            S_.op("pool", lambda e: e.memset(st5[:, 4:5], -0.5), writes=[T_st5])
            xtmp = sb4("xtmp", [128, 512], F32)
            T_xtmp = Dep()
            hnb = sb4("hnb", [128, D], BF16)
            T_hnb = Dep()
            hT = [sb4(f"hT{bf}", [128, KD, TB], BF16) for bf in range(2)]
            T_hT = [Dep(), Dep()]
            qpT = sb4("qpT", [128, 16, TB], BF16)
            T_qpT = Dep()
            sc = [sb4(f"sc{i}", [128, 2048], F32) for i in range(2)]
            T_sc = [[Dep() for _ in range(4)] for _ in range(2)]
            sc2 = sb4("sc2", [128, 128], F32)
            T_sc2 = Dep()
            top = sb4("top", [128, 16, 16], F32)
            idx = sb4("idx", [128, 16, 16], U32)
            idxf = sb4("idxf", [128, 16, 16], F32)
            T_top, T_idx = Dep(), Dep()
            T_cand = Dep()
            cand2 = sb4("cand2", [128, 256], F32)
            T_cand2 = Dep()
            best = sb4("best", [128, 8, 16], F32)
            pos = sb4("pos", [128, 8, 16], U32)
            k12 = sb4("k12", [128, 2, 8, 16], U32)
            k12f = sb4("k12f", [128, 2, 8, 16], F32)
            T_best, T_pos, T_k12 = Dep(), Dep(), Dep()
            T_eq = T_cand
            IJg2 = sb4("IJg", [128, 2, 3, 128], F32)
            T_IJg = Dep()
            gw = sb4("gw", [128, 8, 16], F32)
            zz = sb4("zz", [128, 8], F32)
            T_gw = Dep()
            IJgT = sb4("IJgT", [128, 3, TB], F32)
            T_IJgT = Dep()
            NRING = 4
            AB = sb4("AB", [128, NRING, 2, 128], BF16)
            T_AB = [Dep() for _ in range(NRING)]
            T_BB = [Dep() for _ in range(NRING)]
            Gsb = sb4("Gsb", [128, TB, 128], BF16)
            T_G = Dep()
            NUV = 8
            UTb = [sb4(f"UTb{i}", [128, KD, 128], BF16) for i in range(NUV)]
            Vb = [sb4(f"Vb{i}", [128, D], BF16) for i in range(NUV)]
            T_UTb = [Dep() for _ in range(NUV)]
            T_Vb = [Dep() for _ in range(NUV)]
            asb = [sb4(f"asb{i}", [128, TB], BF16) for i in range(3)]
            wsb = [sb4(f"wsb{i}", [128, TB], BF16) for i in range(3)]
            T_asb = [Dep(), Dep(), Dep()]
            T_wsb = [Dep(), Dep(), Dep()]
            yt = sb4("yt", [128, 512], F32)
            T_yt = Dep()
            iota16 = iota_f[:, 0:16]

            blocks = [(si, blk * TB) for si, S in enumerate(seq_lens) for blk in range(S // TB)]

            def gen_pe(bi, pbk):
                si, t0 = blocks[bi]
                bf = bi % 2
                S_.dma("sp", oTl[:], oT_scr[si][:, t0:t0 + TB].rearrange("(k p) t -> p k t", p=128), reads=[T_oTs[si]], writes=[T_oTl])
                for t2 in range(2):
                    S_.dma("sp", xt4[:], xs[si][t0 + t2 * 128:t0 + (t2 + 1) * 128, :], writes=[T_xt4])
                    for hf in range(2):
                        S_.group("pe", [lambda e, k=k, t2=t2, hf=hf: e.matmul(psum[pbk][:], lhsT=oTl[:, k, t2 * 128:(t2 + 1) * 128], rhs=wout_b[:, k, hf * 512:(hf + 1) * 512], start=(k == 0), stop=(k == KD - 1)) for k in range(KD)],
                                 reads=[T_oTl, T_w4], writes=[PS[pbk]])
                        S_.op("act", lambda e: e.copy(out=xtmp[:], in_=psum[pbk][:]), reads=[PS[pbk]], writes=[T_xtmp])
                        S_.op("pool", lambda e, t2=t2, hf=hf: e.tensor_tensor(out=x1[bf][t2][:, hf * 512:(hf + 1) * 512], in0=xtmp[:], in1=xt4[:, hf * 512:(hf + 1) * 512], op=ALU.add),
                              reads=[T_xtmp, T_xt4], writes=[T_x1[bf][t2]])
                        yield
                    if debug:
                        S_.dma("pool", dbg[f"dbg_x1_{si}"][t0 + t2 * 128:t0 + (t2 + 1) * 128, :], x1[bf][t2][:], reads=[T_x1[bf][t2]], writes=[Dep()])
                    S_.op("act", lambda e, t2=t2: e.activation(out=hnb[:], in_=x1[bf][t2][:], func=AF.Square, accum_out=st5[:, 0:1]), reads=[T_x1[bf][t2]], writes=[T_hnb, T_st5])
                    S_.op("pool", lambda e: e.tensor_scalar(out=st5[:, 1:2], in0=st5[:, 0:1], scalar1=1.0 / D, scalar2=EPS, op0=ALU.mult, op1=ALU.add), reads=[T_st5], writes=[T_st5])
                    S_.op("pool", lambda e: e.tensor_tensor(out=st5[:, 2:3], in0=st5[:, 1:2], in1=st5[:, 4:5], op=ALU.pow), reads=[T_st5], writes=[T_st5])
                    S_.op("act", lambda e, t2=t2: e.activation(out=hnb[:], in_=x1[bf][t2][:], func=AF.Copy, scale=st5[:, 2:3]), reads=[T_x1[bf][t2], T_st5], writes=[T_hnb])
                    yield
                    pst = psum[pbk][:].bitcast(BF16)
                    S_.group("pe", [lambda e, k=k, pst=pst: e.transpose(out=pst[:, k * 128:(k + 1) * 128], in_=hnb[:, k * 128:(k + 1) * 128], identity=ident_b[:]) for k in range(KD)],
                             reads=[T_hnb, T_const], writes=[PS[pbk]])
                    S_.op("act", lambda e, t2=t2, pst=pst: e.copy(out=hT[bf][:, :, t2 * 128:(t2 + 1) * 128], in_=pst.rearrange("p (k t) -> p k t", k=KD)), reads=[PS[pbk]], writes=[T_hT[bf]])
                    yield
                for pz in range(16):
                    b = pz % 2
                    S_.dma("sp", wqs[b][:], wq_scr[:, :, pz * 128:(pz + 1) * 128], reads=[T_wqscr], writes=[T_wqs[b]])
                    S_.group("pe", [lambda e, k=k, pz=pz, b=b: e.matmul(psum[pbk][:, 0:TB], lhsT=wqs[b][:, k, :], rhs=hT[bf][:, k, :], start=(k == 0), stop=(k == KD - 1)) for k in range(KD)],
                             reads=[T_hT[bf], T_wqs[b]], writes=[PS[pbk]])
                    S_.op("act", lambda e, pz=pz, b=b: e.copy(out=qpT[:, pz, :], in_=psum[pbk][:, 0:TB]), reads=[PS[pbk]], writes=[T_qpT])
                    yield
                for t2 in range(2):
                    for bk in range(4):
                        S_.group("pe", [lambda e, pz=pz, t2=t2: e.matmul(psum[pbk][:, (pz % 4) * 128:(pz % 4 + 1) * 128], lhsT=qpT[:, pz, t2 * 128:(t2 + 1) * 128], rhs=KzT[:, pz % 2, :], start=True, stop=True) for pz in range(bk * 4, bk * 4 + 4)],
                                 reads=[T_qpT, T_w4], writes=[PS[pbk]])
                        S_.op("act", lambda e, t2=t2, bk=bk: e.copy(out=sc[t2][:, bk * 512:(bk + 1) * 512], in_=psum[pbk][:]), reads=[PS[pbk]], writes=[T_sc[t2][bk], T_cand])
                        yield

            def gen_dve(bi):
                si, t0 = blocks[bi]
                bf = bi % 2
                for t2 in range(2):
                    for bk in range(4):
                        for g in range(bk * 4, bk * 4 + 4):
                            gs = slice(g * 128, (g + 1) * 128)
                            S_.op("dve", lambda e, g=g, gs=gs, t2=t2: e.max(out=top[:, g, 0:8], in_=sc[t2][:, gs]), reads=[T_sc[t2][bk]], writes=[T_top])
                            S_.op("dve", lambda e, g=g, gs=gs, t2=t2: e.max_index(out=idx[:, g, 0:8], in_max=top[:, g, 0:8], in_values=sc[t2][:, gs]), reads=[T_sc[t2][bk], T_top], writes=[T_idx])
                            S_.op("dve", lambda e, g=g, gs=gs, t2=t2: e.match_replace(out=sc2[:], in_to_replace=top[:, g, 0:8], in_values=sc[t2][:, gs], imm_value=-1e30), reads=[T_sc[t2][bk], T_top], writes=[T_sc2])
                            yield
                            S_.op("dve", lambda e, g=g: e.max(out=top[:, g, 8:16], in_=sc2[:]), reads=[T_sc2], writes=[T_top])
                            S_.op("dve", lambda e, g=g: e.max_index(out=idx[:, g, 8:16], in_max=top[:, g, 8:16], in_values=sc2[:]), reads=[T_sc2, T_top], writes=[T_idx])
                            yield
                    S_.op("dve", lambda e: e.tensor_copy(out=idxf[:], in_=idx[:]), reads=[T_idx], writes=[T_idx])
                    t4v = top[:].rearrange("p (h z) k -> p h z k", z=2)
                    i4v = idxf[:].rearrange("p (h z) k -> p h z k", z=2)
                    cand = sc[t2][:].rearrange("p (h c) -> p h c", h=8)
                    eq = sc[t2][:].rearrange("p (h s k) -> p h s k", h=8, s=16)
                    c4 = sc[t2][:].rearrange("p (h a b) -> p h a b", h=8, a=16)
                    S_.op("dve", lambda e: e.tensor_tensor(out=c4, in0=t4v[:, :, 0, :].unsqueeze(3).to_broadcast([128, 8, 16, 16]), in1=t4v[:, :, 1, :].unsqueeze(2).to_broadcast([128, 8, 16, 16]), op=ALU.add),
                          reads=[T_top], writes=[T_cand] + T_sc[t2])
                    yield
                    for p in range(8):
                        S_.op("dve", lambda e, p=p: e.max(out=best[:, p, 0:8], in_=cand[:, p, :]), reads=[T_cand], writes=[T_best])
                        S_.op("dve", lambda e, p=p: e.max_index(out=pos[:, p, 0:8], in_max=best[:, p, 0:8], in_values=cand[:, p, :]), reads=[T_cand, T_best], writes=[T_pos])
                        S_.op("dve", lambda e, p=p: e.match_replace(out=cand2[:], in_to_replace=best[:, p, 0:8], in_values=cand[:, p, :], imm_value=-1e30), reads=[T_cand, T_best], writes=[T_cand2])
                        yield
                        S_.op("dve", lambda e, p=p: e.max(out=best[:, p, 8:16], in_=cand2[:]), reads=[T_cand2], writes=[T_best])
                        S_.op("dve", lambda e, p=p: e.max_index(out=pos[:, p, 8:16], in_max=best[:, p, 8:16], in_values=cand2[:]), reads=[T_cand2, T_best], writes=[T_pos])
                        yield
                    S_.op("dve", lambda e: e.tensor_tensor(out=gw[:], in0=best[:], in1=best[:, :, 0:1].to_broadcast([128, 8, 16]), op=ALU.subtract), reads=[T_best], writes=[T_gw])
                    S_.op("act", lambda e: e.activation(out=gw[:], in_=gw[:], func=AF.Exp), reads=[T_gw], writes=[T_gw])
                    S_.op("dve", lambda e: e.reduce_sum(out=zz[:], in_=gw[:], axis=AX.X), reads=[T_gw], writes=[T_gw])
                    S_.op("dve", lambda e: e.reciprocal(out=zz[:], in_=zz[:]), reads=[T_gw], writes=[T_gw])
                    S_.op("dve", lambda e, t2=t2: e.tensor_tensor(out=IJg2[:, t2, 2, :].rearrange("p (h s) -> p h s", h=8), in0=gw[:], in1=zz[:].unsqueeze(2).to_broadcast([128, 8, 16]), op=ALU.mult),
                          reads=[T_gw], writes=[T_IJg])
                    yield
                    S_.op("dve", lambda e: e.tensor_single_scalar(out=k12[:, 0], in_=pos[:], scalar=4, op=ALU.logical_shift_right), reads=[T_pos], writes=[T_k12])
                    S_.op("dve", lambda e: e.tensor_single_scalar(out=k12[:, 1], in_=pos[:], scalar=15, op=ALU.bitwise_and), reads=[T_pos, T_k12], writes=[T_k12])
                    S_.op("dve", lambda e: e.tensor_copy(out=k12f[:], in_=k12[:]), reads=[T_k12], writes=[T_k12])
                    yield
                    for z in range(2):
                        S_.op("dve", lambda e, z=z: e.tensor_tensor(out=eq[:], in0=k12f[:, z].unsqueeze(3).to_broadcast([128, 8, 16, 16]),
                                                                      in1=iota16.unsqueeze(1).unsqueeze(1).to_broadcast([128, 8, 16, 16]), op=ALU.is_equal),
                              reads=[T_k12, T_const, T_eq], writes=[T_eq])
                        yield
                        S_.op("dve", lambda e, z=z: e.tensor_tensor(out=eq[:], in0=eq[:], in1=i4v[:, :, z, :].unsqueeze(2).to_broadcast([128, 8, 16, 16]), op=ALU.mult),
                              reads=[T_eq, T_idx], writes=[T_eq])
                        yield
                        S_.op("dve", lambda e, z=z, t2=t2: e.reduce_sum(out=IJg2[:, t2, z, :], in_=eq[:].rearrange("p h s k -> p (h s) k"), axis=AX.X), reads=[T_eq], writes=[T_IJg])
                        yield

            def emit_B6(gpe=None):
                for t2 in range(2):
                    S_.group("pe", [lambda e, j=j, t2=t2: e.transpose(out=psum[5][:, j * 128:(j + 1) * 128], in_=IJg2[:, t2, j, :], identity=ident_f[:]) for j in range(3)],
                             reads=[T_IJg, T_const], writes=[PS[5]])
                    S_.op("act", lambda e, t2=t2: e.copy(out=IJgT[:, :, t2 * 128:(t2 + 1) * 128], in_=psum[5][:, 0:384].rearrange("p (j t) -> p j t", j=3)), reads=[PS[5]], writes=[T_IJgT])
                for tq in range(TB // 4):
                    gb = 5 + (tq % 2)
                    for tt in range(4):
                        t = tq * 4 + tt
                        rs = t % NRING
                        S_.op("dve", lambda e, t=t, rs=rs: e.tensor_scalar(out=AB[:, rs, 0, :], in0=iota_f[:], scalar1=IJgT[:, 0, t:t + 1], scalar2=IJgT[:, 2, t:t + 1], op0=ALU.is_equal, op1=ALU.mult),
                              reads=[T_IJgT, T_const], writes=[T_AB[rs]])
                        S_.op("dve", lambda e, t=t, rs=rs: e.tensor_scalar(out=AB[:, rs, 1, :], in0=iota_f[:], scalar1=IJgT[:, 1, t:t + 1], scalar2=None, op0=ALU.is_equal),
                              reads=[T_IJgT, T_const], writes=[T_BB[rs]])
                        S_.op("pe", lambda e, tt=tt, rs=rs, gb=gb: e.matmul(psum[gb][:, tt * 128:(tt + 1) * 128], lhsT=AB[:, rs, 1, :], rhs=AB[:, rs, 0, :], start=True, stop=True),
                              reads=[T_AB[rs], T_BB[rs]], writes=[PS[gb]])
                    S_.op("act", lambda e, tq=tq, gb=gb: e.copy(out=Gsb[:, tq * 4:tq * 4 + 4, :].rearrange("j t i -> j (t i)"), in_=psum[gb][:]),
                          reads=[PS[gb]], writes=[T_G])
                    if gpe is not None:
                        next(gpe, None)
                if gpe is not None:
                    for _ in gpe:
                        pass

            def emit_B7(bi, filler):
                bf = bi % 2

                def f71(i):
                    u = i % NUV
                    a = i % 3
                    S_.dma("sp", UTb[u][:], UT_scr[i], reads=[T_UV], writes=[T_UTb[u]])
                    S_.dma("sp", Vb[u][:], V_scr[i * 128:(i + 1) * 128, :], reads=[T_UV], writes=[T_Vb[u]])
                    pa_ = psum[(4, 6, 7)[a]][:, 0:TB]
                    S_.group("pe", [lambda e, k=k, u=u, pa_=pa_: e.matmul(pa_, lhsT=UTb[u][:, k, :], rhs=hT[bf][:, k, :], start=(k == 0), stop=(k == KD - 1)) for k in range(KD)],
                             reads=[T_UTb[u], T_hT[bf]], writes=[PSA[a]])
                    S_.op("act", lambda e, a=a, pa_=pa_: e.activation(out=asb[a][:], in_=pa_, func=AF.Gelu), reads=[PSA[a]], writes=[T_asb[a]])
                    S_.op("pool", lambda e, a=a, i=i: e.tensor_tensor(out=wsb[a][:], in0=asb[a][:], in1=Gsb[:, :, i], op=ALU.mult), reads=[T_asb[a], T_G], writes=[T_wsb[a]])

                def f72(i):
                    u = i % NUV
                    a = i % 3
                    S_.group("pe", [lambda e, t2=t2, hf=hf, a=a, u=u, i=i: e.matmul(psum[t2 * 2 + hf][:], lhsT=wsb[a][:, t2 * 128:(t2 + 1) * 128], rhs=Vb[u][:, hf * 512:(hf + 1) * 512], start=(i == 0), stop=(i == n_exp_chunks - 1)) for t2 in range(2) for hf in range(2)],
                             reads=[T_wsb[a], T_Vb[u]], writes=[PS[0], PS[1], PS[2], PS[3]])
                    if filler is not None:
                        next(filler, None)
                pipeline(range(n_exp_chunks), f71, f72, depth=2)
                if filler is not None:
                    for _ in filler:
                        pass

            def emit_B8(bi):
                si, t0 = blocks[bi]
                bf = bi % 2
                for t2 in range(2):
                    for hf in range(2):
                        S_.op("dve", lambda e, t2=t2, hf=hf: e.tensor_tensor(out=yt[:], in0=psum[t2 * 2 + hf][:], in1=x1[bf][t2][:, hf * 512:(hf + 1) * 512], op=ALU.add),
                              reads=[PS[t2 * 2 + hf], T_x1[bf][t2]], writes=[T_yt])
                        S_.dma("pool", ys[si][t0 + t2 * 128:t0 + (t2 + 1) * 128, hf * 512:(hf + 1) * 512], yt[:], reads=[T_yt], writes=[T_y])

            for _ in gen_pe(0, 5):
                pass
            for _ in gen_dve(0):
                pass
            for bi in range(len(blocks)):
                emit_B6(gen_pe(bi + 1, 7) if bi + 1 < len(blocks) else None)
                filler = gen_dve(bi + 1) if bi + 1 < len(blocks) else None
                emit_B7(bi, filler)
                emit_B8(bi)
        S_.finish()
    print("instructions:", S_.ninst)
    return nc


_CONSTS = None


def _in_maps(inputs, seq_lens_per_core, core_seqs):
    global _CONSTS
    if _CONSTS is None:
        _CONSTS = host_consts()
    c = _CONSTS
    base = {
        "rel_bias": inputs["rel_bias"], "attn_norm_g": inputs["attn_norm_g"][0], "w_in": inputs["w_in"][0],
        "q_norm_a": inputs["q_norm_a"][0], "k_norm_a": inputs["k_norm_a"][0], "q_norm_b": inputs["q_norm_b"][0],
        "k_norm_b": inputs["k_norm_b"][0], "lambda_q1": inputs["lambda_q1"][0], "lambda_k1": inputs["lambda_k1"][0],
        "lambda_q2": inputs["lambda_q2"][0], "lambda_k2": inputs["lambda_k2"][0], "diff_norm_g": inputs["diff_norm_g"][0],
        "w_out": inputs["w_out"][0], "ffn_norm_g": inputs["ffn_norm_g"][0], "peer_w_q": inputs["peer_w_q"][0],
        "peer_sub_keys": inputs["peer_sub_keys"][0], "peer_u": inputs["peer_u"][0], "peer_v": inputs["peer_v"][0],
        "c_ident": c["ident"], "c_antiI": c["antiI"], "c_blockones": c["blockones"], "c_iota": c["iota"],
        "c_ohb": c["ohb"], "c_oha": c["oha"], "c_sel65": c["sel65"],
    }
    base = {k: np.ascontiguousarray(np.asarray(v, dtype=np.float32)) for k, v in base.items()}
    maps = []
    for seqs in core_seqs:
        m = dict(base)
        for i, xarr in enumerate(seqs):
            m[f"x{i}"] = np.ascontiguousarray(np.asarray(xarr, dtype=np.float32))
        maps.append(m)
    return maps


def kernel(**inputs):
    inputs = {k: np.asarray(v) for k, v in inputs.items()}
    xp = inputs["x_prompt"]
    xsmp = inputs["x_sample"]
    n = 8
    seq_lens = [xp.shape[1], xsmp.shape[1], xsmp.shape[1]]
    core_seqs = [[xp[c], xsmp[2 * c], xsmp[2 * c + 1]] for c in range(n)]
    nc = build(seq_lens)
    maps = _in_maps(inputs, seq_lens, core_seqs)
    res = run_bass_kernel_spmd(nc, maps, core_ids=list(range(n)))
    yp = np.stack([res.results[c]["y0"] for c in range(n)], axis=0).astype(np.float32)
    ysm = np.empty(xsmp.shape, np.float32)
    for c in range(n):
        ysm[2 * c] = res.results[c]["y1"]
        ysm[2 * c + 1] = res.results[c]["y2"]
    return (yp, ysm)
```

```python
import math
from contextlib import ExitStack
import numpy as np
import concourse.bass as bass
import concourse.mybir as mybir
from concourse.bass_utils import run_bass_kernel_spmd

F32 = mybir.dt.float32
BF16 = mybir.dt.bfloat16
U32 = mybir.dt.uint32
AF = mybir.ActivationFunctionType
ALU = mybir.AluOpType
AX = mybir.AxisListType

D = 1024
KD = 8
EPS = 1e-6
NEG = -30000.0
NDS = 56


class Dep:
    __slots__ = ("w", "r")

    def __init__(self):
        self.w = None
        self.r = {}


class Eng:
    def __init__(self, name, eng, sem):
        self.name = name
        self.eng = eng
        self.sem = sem
        self.count = 0
        self.seen = {}


class Sched:
    def __init__(self, nc, stack):
        self.nc = nc
        self.E = {}
        for name, eng in [("pe", nc.tensor), ("act", nc.scalar), ("dve", nc.vector),
                          ("pool", nc.gpsimd), ("sp", nc.sync)]:
            sem = stack.enter_context(nc.semaphore(f"s_{name}"))
            self.E[name] = Eng(name, eng, sem)
        self.dsems = []
        self.dpool = {"sp": [], "pool": []}
        for i in range(NDS):
            sem = stack.enter_context(nc.semaphore(f"dq{i}"))
            self.dsems.append([sem, 0])
            self.dpool["pool" if i >= NDS - 16 else "sp"].append(i)
        self.dnext = {"sp": 0, "pool": 0}
        self.ninst = 0

    def _wait(self, e, deps):
        best = {}
        for (key, sem, val) in deps:
            if key == "pe" and e.name == "pe":
                continue
            if val > best.get(key, (None, 0))[1]:
                best[key] = (sem, val)
        for key, (sem, val) in best.items():
            if e.seen.get(key, 0) < val:
                e.eng.wait_ge(sem, val)
                e.seen[key] = val

    @staticmethod
    def _deps(reads, writes):
        d = []
        for t in reads:
            if t.w is not None:
                d.append(t.w)
        for t in writes:
            if t.w is not None:
                d.append(t.w)
            d.extend(t.r.values())
        return d

    @staticmethod
    def _mark(tok, reads, writes):
        for t in writes:
            t.w = tok
            t.r = {}
        for t in reads:
            t.r[tok[0]] = tok

    def op(self, en, fn, reads=(), writes=()):
        e = self.E[en]
        self._wait(e, self._deps(reads, writes))
        ins = fn(e.eng)
        e.count += 1
        ins.then_inc(e.sem, 1)
        self._mark((en, e.sem, e.count), reads, writes)
        self.ninst += 1

    def group(self, en, fns, reads=(), writes=()):
        e = self.E[en]
        self._wait(e, self._deps(reads, writes))
        ins = None
        for fn in fns:
            ins = fn(e.eng)
            self.ninst += 1
        e.count += 1
        ins.then_inc(e.sem, 1)
        self._mark((en, e.sem, e.count), reads, writes)

    def dma(self, qn, out, in_, reads=(), writes=(), **kw):
        e = self.E[qn]
        pl = self.dpool[qn]
        idx = pl[self.dnext[qn]]
        self.dnext[qn] = (self.dnext[qn] + 1) % len(pl)
        slot = self.dsems[idx]
        deps = self._deps(reads, writes)
        key = ("d", idx)
        if slot[1] > 0:
            deps.append((key, slot[0], slot[1]))
        self._wait(e, deps)
        ins = e.eng.dma_start(out=out, in_=in_, **kw)
        slot[1] += 16
        ins.then_inc(slot[0], 16)
        self._mark((key, slot[0], slot[1]), reads, writes)
        self.ninst += 1

    def barrier(self):
        for e in self.E.values():
            for o in self.E.values():
                if o.count == 0:
                    continue
                if e.seen.get(o.name, 0) < o.count:
                    e.eng.wait_ge(o.sem, o.count)
                    e.seen[o.name] = o.count
            for idx, slot in enumerate(self.dsems):
                key = ("d", idx)
                if slot[1] > 0 and e.seen.get(key, 0) < slot[1]:
                    e.eng.wait_ge(slot[0], slot[1])
                    e.seen[key] = slot[1]

    def finish(self):
        e = self.E["sp"]
        for idx, slot in enumerate(self.dsems):
            if slot[1] > 0 and e.seen.get(("d", idx), 0) < slot[1]:
                e.eng.wait_ge(slot[0], slot[1])
        for name in ("pe", "act", "dve", "pool"):
            o = self.E[name]
            if o.count > 0:
                e.eng.wait_ge(o.sem, o.count)


def pipeline(items, first, second, depth=1):
    items = list(items)
    n = len(items)
    for i in range(n + depth):
        if i < n:
            first(items[i])
        if i - depth >= 0:
            second(items[i - depth])


def rel_bucket_np(rel):
    nb = 16
    max_exact = 8
    n = np.abs(rel)
    with np.errstate(divide="ignore"):
        large = max_exact + (np.log(np.maximum(n, 1).astype(np.float32) / np.float32(max_exact))
                             / np.float32(math.log(1024 / max_exact)) * (nb - max_exact)).astype(np.int32)
    large = np.minimum(large, nb - 1)
    return (rel > 0).astype(np.int32) * nb + np.where(n < max_exact, n, large)


RB = 4095
FAR = 559


def host_consts():
    c = {}
    c["ident"] = np.eye(128, dtype=np.float32)
    c["antiI"] = np.ascontiguousarray(np.eye(128, dtype=np.float32)[::-1])
    bo = np.zeros((128, 128), np.float32)
    bo[:64, :64] = 1
    bo[64:, 64:] = 1
    c["blockones"] = bo
    c["iota"] = np.tile(np.arange(128, dtype=np.float32)[None, :], (128, 1))
    i = np.arange(8192)
    ohb = np.zeros((32, 8192), np.float32)
    ohb[rel_bucket_np(RB - i), i] = 1.0
    c["ohb"] = ohb
    oha = np.zeros((3, 33, 384), np.float32)
    for di, d in enumerate((1, 4, 16)):
        for ii in range(384):
            rm = 191 - ii
            if abs(rm) <= 64:
                oha[di, rel_bucket_np(np.array(rm * d)), ii] = 1.0
            else:
                oha[di, 32, ii] = 1.0
    c["oha"] = oha
    sel = np.zeros((128, 64), np.float32)
    sel[64, :] = 1.0
    c["sel65"] = sel
    return c


def build(seq_lens, n_exp_chunks=128, debug=False):
    nc = bass.Bass("TRN2", target_bir_lowering=False)
    NSEQ = len(seq_lens)
    TTOT = sum(seq_lens)
    SMAX = max(seq_lens)

    def din(name, shape, dt=F32):
        return nc.dram_tensor(name, list(shape), dt, kind="ExternalInput").ap()

    xs = [din(f"x{i}", [S, D]) for i, S in enumerate(seq_lens)]
    ys = [nc.dram_tensor(f"y{i}", [S, D], F32, kind="ExternalOutput").ap() for i, S in enumerate(seq_lens)]
    rel_bias = din("rel_bias", [32, 12])
    attn_g = din("attn_norm_g", [D])
    w_in = din("w_in", [D, 3072])
    qn_a = din("q_norm_a", [64])
    kn_a = din("k_norm_a", [64])
    qn_b = din("q_norm_b", [64])
    kn_b = din("k_norm_b", [64])
    lq1 = din("lambda_q1", [64])
    lk1 = din("lambda_k1", [64])
    lq2 = din("lambda_q2", [64])
    lk2 = din("lambda_k2", [64])
    dn_g = din("diff_norm_g", [128])
    w_out = din("w_out", [D, D])
    ffn_g = din("ffn_norm_g", [D])
    w_q = din("peer_w_q", [D, 2048])
    subk = din("peer_sub_keys", [2, 128, 128])
    pu = din("peer_u", [16384, D])
    pv = din("peer_v", [16384, D])
    c_ident = din("c_ident", [128, 128])
    c_anti = din("c_antiI", [128, 128])
    c_bo = din("c_blockones", [128, 128])
    c_iota = din("c_iota", [128, 128])
    c_ohb = din("c_ohb", [32, 8192])
    c_oha = din("c_oha", [3, 33, 384])
    c_sel = din("c_sel65", [128, 64])

    def dscr(name, shape, dt):
        return nc.dram_tensor(name, list(shape), dt, kind="Internal")

    win_bf = dscr("win_bf", [128, KD, 3072], BF16).ap()
    GB_h = dscr("GBseq", [4, 8192], BF16)
    GA_h = dscr("GAseq", [8, 3, 384], BF16)
    oT_scr = [dscr(f"oT{i}", [D, S], BF16).ap() for i, S in enumerate(seq_lens)]
    wq_scr = dscr("wq_bf", [128, KD, 2048], BF16).ap()
    UT_scr = dscr("UTs", [128, 128, KD, 128], BF16).ap()
    V_scr = dscr("Vs", [16384, D], BF16).ap()
    dbg = {}
    if debug:
        for i, S in enumerate(seq_lens):
            dbg[f"dbg_x1_{i}"] = nc.dram_tensor(f"dbg_x1_{i}", [S, D], F32, kind="ExternalOutput").ap()

    with ExitStack() as st:
        S_ = Sched(nc, st)

        def sb(name, shape, dt):
            return st.enter_context(nc.sbuf_tensor(name, list(shape), dt))

        ident_f = sb("ident_f", [128, 128], F32)
        ident_b = sb("ident_b", [128, 128], BF16)
        anti_b = sb("anti_b", [128, 128], BF16)
        bo_b = sb("bo_b", [128, 128], BF16)
        ones_b = sb("ones_b", [128, 128], BF16)
        ones_f = sb("ones_f", [128, 128], F32)
        iota_f = sb("iota_f", [128, 128], F32)
        sel_f = sb("sel_f", [128, 64], F32)
        gq_a = sb("gq_a", [128, 1], F32)
        gk_a = sb("gk_a", [128, 1], F32)
        gq_b = sb("gq_b", [128, 1], F32)
        gk_b = sb("gk_b", [128, 1], F32)
        gdiff = sb("gdiff", [128, 1], F32)
        neglam = sb("neglam", [128, 1], F32)
        biasfar = sb("biasfar", [128, 4, 2], F32)
        gA = sb("gA", [128, KD], F32)
        gF = sb("gF", [128, KD], F32)
        T_const = Dep()
        psum = [st.enter_context(nc.psum_tensor(f"ps{i}", [128, 512], F32)) for i in range(8)]
        PS = [Dep() for _ in range(8)]
        PSA = [PS[4], PS[6], PS[7]]
        T_oTs = [Dep() for _ in seq_lens]
        T_y = Dep()
        stA = ExitStack()
        BMA = stA.enter_context(nc.sbuf_tensor('BMA', [128, 8, 3, 2, 128], BF16))

        with ExitStack() as s0:
            def sb0(name, shape, dt):
                return s0.enter_context(nc.sbuf_tensor(name, list(shape), dt))
            stage = sb0("stage0", [128, 4096], F32)
            T_stage = Dep()
            tmp = sb0("tmp0", [128, 512], F32)
            T_tmp = Dep()
            S_.dma("sp", ident_f[:], c_ident, writes=[T_const])
            S_.dma("sp", iota_f[:], c_iota, writes=[T_const])
            S_.dma("sp", sel_f[:], c_sel, writes=[T_const])
            S_.dma("sp", stage[:, 0:128], c_anti, writes=[T_stage])
            S_.dma("sp", stage[:, 128:256], c_bo, writes=[T_stage])
            S_.op("dve", lambda e: e.tensor_copy(out=ident_b[:], in_=ident_f[:]), reads=[T_const], writes=[T_const])
            S_.op("dve", lambda e: e.tensor_copy(out=anti_b[:], in_=stage[:, 0:128]), reads=[T_stage], writes=[T_const])
            S_.op("dve", lambda e: e.tensor_copy(out=bo_b[:], in_=stage[:, 128:256]), reads=[T_stage], writes=[T_const])
            S_.op("dve", lambda e: e.memset(ones_b[:], 1.0), writes=[T_const])
            S_.op("dve", lambda e: e.memset(ones_f[:], 1.0), writes=[T_const])
            T_g = Dep()
            graw = sb0("graw", [128, 8], F32)
            for j, src in enumerate((qn_a, kn_a, qn_b, kn_b)):
                for hh in range(2):
                    S_.dma("sp", graw[hh * 64:(hh + 1) * 64, j:j + 1], src.rearrange("(c o) -> c o", o=1), writes=[T_g])
            S_.dma("sp", graw[:, 4:5], dn_g.rearrange("(c o) -> c o", o=1), writes=[T_g])
            S_.op("dve", lambda e: e.tensor_scalar_mul(out=gq_a[:], in0=graw[:, 0:1], scalar1=0.125), reads=[T_g], writes=[T_const])
            S_.op("dve", lambda e: e.tensor_copy(out=gk_a[:], in_=graw[:, 1:2]), reads=[T_g], writes=[T_const])
            S_.op("dve", lambda e: e.tensor_scalar_mul(out=gq_b[:], in0=graw[:, 2:3], scalar1=0.125), reads=[T_g], writes=[T_const])
            S_.op("dve", lambda e: e.tensor_copy(out=gk_b[:], in_=graw[:, 3:4]), reads=[T_g], writes=[T_const])
            S_.op("dve", lambda e: e.tensor_scalar_mul(out=gdiff[:], in0=graw[:, 4:5], scalar1=0.8), reads=[T_g], writes=[T_const])
            S_.dma("sp", gA[:], attn_g.rearrange("(k p) -> p k", p=128), writes=[T_const], allow_slow_non_contiguous=True)
            S_.dma("sp", gF[:], ffn_g.rearrange("(k p) -> p k", p=128), writes=[T_const], allow_slow_non_contiguous=True)
            lam4 = sb0("lam4", [128, 4, 64], F32)
            T_l = Dep()
            for j, src in enumerate((lq1, lk1, lq2, lk2)):
                S_.dma("sp", lam4[:, j, :], src.partition_broadcast(128), writes=[T_l])
            lamw = sb0("lamw", [128, 8], F32)
            T_lw = Dep()
            prod = sb0("lprod", [128, 2, 64], F32)
            S_.op("dve", lambda e: e.tensor_tensor(out=prod[:, 0, :], in0=lam4[:, 0, :], in1=lam4[:, 1, :], op=ALU.mult), reads=[T_l], writes=[T_lw])
            S_.op("dve", lambda e: e.tensor_tensor(out=prod[:, 1, :], in0=lam4[:, 2, :], in1=lam4[:, 3, :], op=ALU.mult), reads=[T_l, T_lw], writes=[T_lw])
            S_.op("dve", lambda e: e.reduce_sum(out=lamw[:, 0:2], in_=prod[:], axis=AX.X), reads=[T_lw], writes=[T_lw])
            S_.op("act", lambda e: e.activation(out=lamw[:, 2:4], in_=lamw[:, 0:2], func=AF.Exp), reads=[T_lw], writes=[T_lw])
            S_.op("dve", lambda e: e.tensor_tensor(out=lamw[:, 4:5], in0=lamw[:, 3:4], in1=lamw[:, 2:3], op=ALU.subtract), reads=[T_lw], writes=[T_lw])
            S_.op("dve", lambda e: e.tensor_scalar_add(out=neglam[:], in0=lamw[:, 4:5], scalar1=-0.2), reads=[T_lw], writes=[T_const])
            for hb in range(4):
                for sg, row in enumerate((15, 31)):
                    S_.dma("sp", biasfar[:, hb, sg:sg + 1],
                           rel_bias[row:row + 1, 8 + hb:9 + hb].rearrange("a b -> (a b)").partition_broadcast(128), writes=[T_const])
            tabA = sb0("tabA", [33, 12], F32)
            T_tab = Dep()
            S_.op("dve", lambda e: e.memset(tabA[:], NEG), writes=[T_tab])
            S_.dma("sp", tabA[0:32, :], rel_bias, reads=[], writes=[T_tab])
            ohb_sb = sb0("ohb_sb", [32, 8192], F32)
            oha_sb = sb0("oha_sb", [33, 3, 384], F32)
            T_oh = Dep()
            S_.dma("sp", ohb_sb[:], c_ohb, writes=[T_oh])
            S_.dma("sp", oha_sb[:], c_oha.rearrange("d b i -> b d i"), writes=[T_oh])
            seqb = sb0("seqb", [8, 8192], BF16)
            T_sq = Dep()
            for g in range(16):
                S_.op("pe", lambda e, g=g: e.matmul(psum[g % 2][0:4, :], lhsT=tabA[0:32, 8:12], rhs=ohb_sb[:, g * 512:(g + 1) * 512], start=True, stop=True),
                      reads=[T_tab, T_oh], writes=[PS[g % 2]])
                S_.op("act", lambda e, g=g: e.copy(out=seqb[0:4, g * 512:(g + 1) * 512], in_=psum[g % 2][0:4, :]), reads=[PS[g % 2]], writes=[T_sq])
            T_GB = Dep()
            S_.dma("sp", GB_h.ap(), seqb[0:4, :], reads=[T_sq], writes=[T_GB])
            seqa = sb0("seqa", [8, 3, 384], BF16)
            T_sa = Dep()
            for di in range(3):
                S_.op("pe", lambda e, di=di: e.matmul(psum[2][0:8, 0:384], lhsT=tabA[0:33, 0:8], rhs=oha_sb[:, di, :], start=True, stop=True),
                      reads=[T_tab, T_oh], writes=[PS[2]])
                S_.op("act", lambda e, di=di: e.copy(out=seqa[:, di, :], in_=psum[2][0:8, 0:384]), reads=[PS[2]], writes=[T_sa])
            T_GA = Dep()
            S_.dma("sp", GA_h.ap(), seqa[:], reads=[T_sa], writes=[T_GA])
            for h in range(8):
                for di in range(3):
                    for c in range(2):
                        src = bass.AP(GA_h, (h * 3 + di) * 384 + 128 * (1 - c), [[1, 128], [1, 128]])
                        S_.dma("sp", BMA[:, h, di, c, :], src, reads=[T_GA], writes=[T_const])
            T_win = Dep()
            wst = sb0("wst", [128, KD, 512], BF16)
            T_wst = Dep()
            for cb in range(6):
                S_.dma("sp", stage[:].rearrange("p (k c) -> p k c", k=KD),
                       w_in[:, cb * 512:(cb + 1) * 512].rearrange("(k p) c -> p k c", p=128), writes=[T_stage])
                for k in range(KD):
                    S_.op("dve" if k % 2 else "pool", lambda e, k=k: e.tensor_scalar(out=wst[:, k, :], in0=stage[:, k * 512:(k + 1) * 512], scalar1=gA[:, k:k + 1], scalar2=None, op0=ALU.mult),
                          reads=[T_stage, T_const], writes=[T_wst])
                S_.dma("sp", win_bf[:, :, cb * 512:(cb + 1) * 512], wst[:], reads=[T_wst], writes=[T_win])
            T_UV = Dep()
            gFrow = sb0("gFrow", [128, D], F32)
            T_gfr = Dep()
            S_.dma("sp", gFrow[:], ffn_g.partition_broadcast(128), writes=[T_gfr])
            NB0 = 4
            ub = [sb0(f"ub{i}", [128, D], BF16) for i in range(NB0)]
            T_ub = [Dep() for _ in range(NB0)]
            ut = [sb0(f"ut{i}", [128, KD, 128], BF16) for i in range(NB0)]
            T_ut = [Dep() for _ in range(NB0)]
            vb16 = [sb0(f"vb16{i}", [128, D], BF16) for i in range(NB0)]
            T_vb = [Dep() for _ in range(NB0)]
            ust = [sb0(f"ust{i}", [128, D], F32) for i in range(NB0)]
            T_ust = [Dep() for _ in range(NB0)]
            vst = [sb0(f"vst{i}", [128, D], F32) for i in range(NB0)]
            T_vst = [Dep() for _ in range(NB0)]
            def ld_uv(i):
                b = i % NB0
                S_.dma("sp", ust[b][:], pu[i * 128:(i + 1) * 128, :], writes=[T_ust[b]])
                S_.dma("sp", vst[b][:], pv[i * 128:(i + 1) * 128, :], writes=[T_vst[b]])
            for i in range(min(NB0 - 1, n_exp_chunks)):
                ld_uv(i)
            for i in range(n_exp_chunks):
                b = i % NB0
                pb2 = 4 + (i % 2)
                if i + NB0 - 1 < n_exp_chunks:
                    ld_uv(i + NB0 - 1)
                S_.op("dve", lambda e, b=b: e.tensor_tensor(out=ub[b][:], in0=ust[b][:], in1=gFrow[:], op=ALU.mult),
                      reads=[T_ust[b], T_gfr], writes=[T_ub[b]])
                pst = psum[pb2][:].bitcast(BF16)
                S_.group("pe", [lambda e, k=k, b=b, pst=pst: e.transpose(out=pst[:, k * 128:(k + 1) * 128], in_=ub[b][:, k * 128:(k + 1) * 128], identity=ident_b[:]) for k in range(KD)],
                         reads=[T_ub[b], T_const], writes=[PS[pb2]])
                S_.op("act", lambda e, b=b, pst=pst: e.copy(out=ut[b][:].rearrange("p k e -> p (k e)"), in_=pst), reads=[PS[pb2]], writes=[T_ut[b]])
                S_.dma("sp", UT_scr[i], ut[b][:], reads=[T_ut[b]], writes=[T_UV])
                S_.op("pool", lambda e, b=b: e.tensor_copy(out=vb16[b][:], in_=vst[b][:]), reads=[T_vst[b]], writes=[T_vb[b]])
                S_.dma("sp", V_scr[i * 128:(i + 1) * 128, :], vb16[b][:], reads=[T_vb[b]], writes=[T_UV])

        S_.barrier()
        for si, S in enumerate(seq_lens):
            S_.barrier()
            NT = S // 128
            NG = S // 512
            with ExitStack() as s1:
                def sb1(name, shape, dt):
                    return s1.enter_context(nc.sbuf_tensor(f"{name}_{si}", list(shape), dt))
                xnT = sb1("xnT", [128, KD, S], BF16)
                T_xnT = Dep()
                xt = [sb1(f"xt{i}", [128, D], F32) for i in range(2)]
                T_xt = [Dep(), Dep()]
                junk = sb1("junk", [128, D], BF16)
                T_junk = Dep()
                xnb = [sb1(f"xnb{i}", [128, D], BF16) for i in range(2)]
                T_xnb = [Dep(), Dep()]
                st4 = sb1("st4", [128, 8], F32)
                T_st4 = Dep()
                for i in range(NT):
                    b = i % 2
                    S_.dma("sp", xt[b][:], xs[si][i * 128:(i + 1) * 128, :], writes=[T_xt[b]])
                    S_.op("act", lambda e, b=b: e.activation(out=junk[:], in_=xt[b][:], func=AF.Square, accum_out=st4[:, 0:1]),
                          reads=[T_xt[b]], writes=[T_junk, T_st4])
                    S_.op("act", lambda e: e.activation(out=st4[:, 1:2], in_=st4[:, 0:1], func=AF.Sqrt, scale=1.0 / D, bias=EPS), reads=[T_st4], writes=[T_st4])
                    S_.op("dve", lambda e: e.reciprocal(out=st4[:, 2:3], in_=st4[:, 1:2]), reads=[T_st4], writes=[T_st4])
                    S_.op("dve", lambda e, b=b: e.tensor_scalar(out=xnb[b][:], in0=xt[b][:], scalar1=st4[:, 2:3], scalar2=None, op0=ALU.mult),
                          reads=[T_xt[b], T_st4], writes=[T_xnb[b]])
                    pst = psum[b][:].bitcast(BF16)
                    S_.group("pe", [lambda e, k=k, b=b, pst=pst: e.transpose(out=pst[:, k * 128:(k + 1) * 128], in_=xnb[b][:, k * 128:(k + 1) * 128], identity=ident_b[:]) for k in range(KD)],
                             reads=[T_xnb[b], T_const], writes=[PS[b]])
                    S_.op("act", lambda e, i=i, pst=pst: e.copy(out=xnT[:, :, i * 128:(i + 1) * 128], in_=pst.rearrange("p (k t) -> p k t", k=KD)),
                          reads=[PS[b]], writes=[T_xnT])

                wsl = sb1("wsl", [128, KD, 384], BF16)
                T_wsl = Dep()
                qT = sb1("qT", [128, 2, S], BF16)
                kT = sb1("kT", [128, S], BF16)
                T_qT, T_kT = Dep(), Dep()
                sq = [sb1(f"sq{i}", [128, 512], BF16) for i in range(2)]
                T_sq2 = [Dep(), Dep()]
                sd = [sb1(f"sd{i}", [128, 512], F32) for i in range(2)]
                T_sd = [Dep(), Dep()]

                S_.op("pool", lambda e: e.memset(qT[:], 0.0), writes=[T_qT])

                def qk_proj(dst, T_dst, wcol, gain, split=False):
                    def f1(g):
                        b = g % 2
                        pa = psum[b]
                        S_.group("pe", [lambda e, k=k, g=g, pa=pa: e.matmul(pa[:], lhsT=wsl[:, k, wcol:wcol + 128], rhs=xnT[:, k, g * 512:(g + 1) * 512], start=(k == 0), stop=(k == KD - 1)) for k in range(KD)],
                                 reads=[T_wsl, T_xnT], writes=[PS[b]])
                        S_.op("act", lambda e, b=b, pa=pa: e.activation(out=sq[b][:], in_=pa[:], func=AF.Square), reads=[PS[b]], writes=[T_sq2[b]])

                    def f2(g):
                        b = g % 2
                        pa, pb_ = psum[b], psum[2 + b]
                        S_.op("pe", lambda e, b=b, pb_=pb_: e.matmul(pb_[:], lhsT=bo_b[:], rhs=sq[b][:], start=True, stop=True), reads=[T_sq2[b], T_const], writes=[PS[2 + b]])
                        S_.op("act", lambda e, b=b, pb_=pb_: e.activation(out=sd[b][:], in_=pb_[:], func=AF.Sqrt, scale=1.0 / 64, bias=EPS), reads=[PS[2 + b]], writes=[T_sd[b]])
                        S_.op("dve", lambda e, b=b: e.reciprocal(out=sd[b][:], in_=sd[b][:]), reads=[T_sd[b]], writes=[T_sd[b]])
                        if split:
                            for hh_ in range(2):
                                rs_ = slice(hh_ * 64, hh_ * 64 + 64)
                                S_.op("dve", lambda e, b=b, g=g, pa=pa, rs_=rs_, hh_=hh_: e.scalar_tensor_tensor(out=dst[rs_, hh_, g * 512:(g + 1) * 512], in0=pa[rs_, :], scalar=gain[rs_, 0:1], in1=sd[b][rs_, :], op0=ALU.mult, op1=ALU.mult),
                                      reads=[PS[b], T_sd[b], T_const], writes=[T_dst])
                        else:
                            S_.op("dve", lambda e, b=b, g=g, pa=pa: e.scalar_tensor_tensor(out=dst[:, g * 512:(g + 1) * 512], in0=pa[:], scalar=gain[:, 0:1], in1=sd[b][:], op0=ALU.mult, op1=ALU.mult),
                                  reads=[PS[b], T_sd[b], T_const], writes=[T_dst])
                    pipeline(range(NG), f1, f2)

                dils = (1, 4, 16)
                with ExitStack() as s2:
                    def sb2(name, shape, dt):
                        return s2.enter_context(nc.sbuf_tensor(f"{name}_{si}", list(shape), dt))
                    Vp = [sb2(f"Vp{di}", [128, NT, 2, 65], BF16) for di in range(3)]
                    T_Vp = [Dep() for _ in range(3)]
                    acc = sb2("accA", [128, S], F32)
                    T_acc = Dep()
                    pT = [sb2(f"pTa{i}", [128, 2, 128], BF16) for i in range(3)]
                    T_pT = [Dep(), Dep(), Dep()]
                    rden = sb2("rdenA", [64, 512], F32)
                    T_rden = Dep()
                    oTt = [sb2(f"oTtA{i}", [64, 512], BF16) for i in range(2)]
                    T_oTt = [Dep(), Dep()]
                    for di in range(3):
                        S_.op("pool", lambda e, di=di: e.memset(Vp[di][:, :, :, 64:65], 1.0), writes=[T_Vp[di]])
                    for hp in range(4):
                        for j, c0 in enumerate((hp * 128, 512 + hp * 128, 1024 + hp * 128)):
                            S_.dma("sp", wsl[:, :, j * 128:(j + 1) * 128], win_bf[:, :, c0:c0 + 128], reads=[T_win], writes=[T_wsl])
                        qk_proj(qT, T_qT, 0, gq_a, split=True)
                        qk_proj(kT, T_kT, 128, gk_a)
                        cnt = 0
                        for di, d in enumerate(dils):
                            L = S // d
                            ntc = L // 128
                            for r in range(d):
                                for j0 in range(0, ntc, 4):
                                    nj = min(4, ntc - j0)
                                    b = cnt % 2
                                    cnt += 1
                                    pv_ = psum[4 + b]
                                    fns = []
                                    for jj in range(nj):
                                        j = j0 + jj
                                        t0 = r + d * 128 * j
                                        for k in range(KD):
                                            fns.append(lambda e, k=k, jj=jj, t0=t0, d=d, pv_=pv_: e.matmul(
                                                pv_[:, jj * 128:(jj + 1) * 128], lhsT=xnT[:, k, t0:t0 + d * 127 + 1:d], rhs=wsl[:, k, 256:384],
                                                start=(k == 0), stop=(k == KD - 1)))
                                    S_.group("pe", fns, reads=[T_xnT, T_wsl], writes=[PS[4 + b]])
                                    ti = r * ntc + j0
                                    S_.op("act", lambda e, di=di, ti=ti, nj=nj, pv_=pv_: e.copy(
                                        out=Vp[di][:, ti:ti + nj, :, 0:64],
                                        in_=pv_[:, 0:nj * 128].rearrange("p (j h c) -> p j h c", j=nj, h=2)),
                                        reads=[PS[4 + b]], writes=[T_Vp[di]])
                        for hh in range(2):
                            h = hp * 2 + hh
                            ro = slice(hh * 64, hh * 64 + 64)
                            allb = []
                            for di, d in enumerate(dils):
                                L = S // d
                                ntc = L // 128
                                for r in range(d):
                                    blocks = [(0, 64, [(0, 1, 64)])]
                                    for bb in range(ntc - 1):
                                        blocks.append((128 * bb + 64, 128, [(bb, 0, 0), (bb + 1, 1, 0)]))
                                    blocks.append((L - 64, 64, [(ntc - 1, 0, 0)]))
                                    for (qm0, nq, kts) in blocks:
                                        allb.append((len(allb) % 3, di, d, r, ntc, qm0, nq, kts))

                            def fA1(it):
                                b, di, d, r, ntc, qm0, nq, kts = it
                                ps_s = psum[b]
                                q0 = r + d * qm0
                                qsl = slice(q0, q0 + d * (nq - 1) + 1, d)
                                fns = []
                                for ci, (kt, ch, c0) in enumerate(kts):
                                    k0 = r + d * 128 * kt
                                    fns.append(lambda e, ci=ci, k0=k0, d=d, qsl=qsl, nq=nq, ps_s=ps_s: e.matmul(
                                        ps_s[:, ci * 128:ci * 128 + nq], lhsT=kT[:, k0:k0 + d * 127 + 1:d], rhs=qT[:, hh, qsl], start=True, stop=False))
                                    fns.append(lambda e, ci=ci, ch=ch, c0=c0, nq=nq, di=di, ps_s=ps_s: e.matmul(
                                        ps_s[:, ci * 128:ci * 128 + nq], lhsT=anti_b[:], rhs=BMA[:, h, di, ch, c0:c0 + nq], start=False, stop=True))
                                S_.group("pe", fns, reads=[T_qT, T_kT, T_const], writes=[PS[b]])
                                if nq == 128:
                                    S_.op("act", lambda e, b=b, ps_s=ps_s: e.activation(out=pT[b][:].rearrange("p c q -> p (c q)"), in_=ps_s[:, 0:256], func=AF.Exp),
                                          reads=[PS[b]], writes=[T_pT[b]])
                                else:
                                    S_.op("act", lambda e, b=b, ps_s=ps_s: e.activation(out=pT[b][:, 0, 0:64], in_=ps_s[:, 0:64], func=AF.Exp),
                                          reads=[PS[b]], writes=[T_pT[b]])

                            def fA2(it):
                                b, di, d, r, ntc, qm0, nq, kts = it
                                ps_o = psum[3 + b]
                                q0 = r + d * qm0
                                qsl = slice(q0, q0 + d * (nq - 1) + 1, d)
                                nk = len(kts)
                                fns = []
                                for ci, (kt, ch, c0) in enumerate(kts):
                                    ti = r * ntc + kt
                                    fns.append(lambda e, ci=ci, ti=ti, di=di, nq=nq, b=b, nk=nk, ps_o=ps_o: e.matmul(
                                        ps_o[0:65, 0:nq], lhsT=Vp[di][:, ti, hh, :], rhs=pT[b][:, ci, 0:nq], start=(ci == 0), stop=(ci == nk - 1)))
                                S_.group("pe", fns, reads=[T_Vp[di], T_pT[b]], writes=[PS[3 + b]])
                                if di == 0:
                                    S_.op("dve", lambda e, qsl=qsl, nq=nq, ps_o=ps_o: e.tensor_copy(out=acc[0:65, qsl], in_=ps_o[0:65, 0:nq]),
                                          reads=[PS[3 + b]], writes=[T_acc])
                                else:
                                    S_.op("dve", lambda e, qsl=qsl, nq=nq, ps_o=ps_o: e.tensor_tensor(out=acc[0:65, qsl], in0=ps_o[0:65, 0:nq], in1=acc[0:65, qsl], op=ALU.add),
                                          reads=[PS[3 + b], T_acc], writes=[T_acc])
                            pipeline(allb, fA1, fA2, depth=2)
                            for g in range(NG):
                                b = g % 2
                                pd = psum[4 + b]
                                S_.op("pe", lambda e, g=g, pd=pd: e.matmul(pd[0:64, :], lhsT=sel_f[0:65, :], rhs=acc[0:65, g * 512:(g + 1) * 512], start=True, stop=True),
                                      reads=[T_acc, T_const], writes=[PS[4 + b]])
                                S_.op("dve", lambda e, pd=pd: e.reciprocal(out=rden[:], in_=pd[0:64, :]), reads=[PS[4 + b]], writes=[T_rden])
                                S_.op("dve", lambda e, g=g, b=b: e.tensor_tensor(out=oTt[b][:], in0=acc[0:64, g * 512:(g + 1) * 512], in1=rden[:], op=ALU.mult),
                                      reads=[T_acc, T_rden], writes=[T_oTt[b]])
                                S_.dma("pool", oT_scr[si][h * 64:(h + 1) * 64, g * 512:(g + 1) * 512], oTt[b][:], reads=[T_oTt[b]], writes=[T_oTs[si]])

                S_.barrier()
                with ExitStack() as s3:
                    def sb3(name, shape, dt):
                        return s3.enter_context(nc.sbuf_tensor(f"{name}_{si}", list(shape), dt))
                    offs = list(range(-640, 1025, 128))
                    BMB = sb3("BMB", [128, len(offs), 512], BF16)
                    T_BMB = Dep()
                    vB = sb3("vB", [128, NT, 128], BF16)
                    T_vB = Dep()
                    pTb = [sb3(f"pTb{i}", [128, 512], BF16) for i in range(4)]
                    T_pTb = [Dep() for _ in range(4)]
                    fa = [sb3(f"fa{i}", [128, 512], F32) for i in range(4)]
                    T_fa = [Dep() for _ in range(4)]
                    oTb = [sb3(f"oTb{i}", [128, 512], BF16) for i in range(2)]
                    T_oTb = [Dep(), Dep()]
                    for hb in range(4):
                        for j, c0 in enumerate((1536 + hb * 128, 2048 + hb * 128, 2560 + hb * 128)):
                            S_.dma("sp", wsl[:, :, j * 128:(j + 1) * 128], win_bf[:, :, c0:c0 + 128], reads=[T_win], writes=[T_wsl])
                        for oi, off in enumerate(offs):
                            base = RB - off - 127
                            if base < 0 or base + 127 + 511 >= 8192:
                                continue
                            src = bass.AP(GB_h, hb * 8192 + base, [[1, 128], [1, 512]])
                            S_.dma("sp", BMB[:, oi, :], src, reads=[T_GB], writes=[T_BMB])
                        qk_proj(qT, T_qT, 0, gq_b, split=True)
                        qk_proj(kT, T_kT, 128, gk_b)
                        for j0 in range(0, NT, 4):
                            b = (j0 // 4) % 2
                            pv_ = psum[4 + b]
                            fns = []
                            for jj in range(4):
                                j = j0 + jj
                                for k in range(KD):
                                    fns.append(lambda e, k=k, jj=jj, j=j, pv_=pv_: e.matmul(
                                        pv_[:, jj * 128:(jj + 1) * 128], lhsT=xnT[:, k, j * 128:(j + 1) * 128], rhs=wsl[:, k, 256:384],
                                        start=(k == 0), stop=(k == KD - 1)))
                            S_.group("pe", fns, reads=[T_xnT, T_wsl], writes=[PS[4 + b]])
                            S_.op("act", lambda e, j0=j0, pv_=pv_: e.copy(out=vB[:, j0:j0 + 4, :], in_=pv_[:].rearrange("p (j c) -> p j c", j=4)),
                                  reads=[PS[4 + b]], writes=[T_vB])
                        scnt = 0
                        for qg in range(NG):
                            itemsB = []
                            for kc in range(NT):
                                for mp in range(2):
                                    itemsB.append((kc, mp, scnt % 4))
                                    scnt += 1

                            def fB1(it, qg=qg):
                                kc, mp, sl = it
                                off = kc * 128 - qg * 512
                                near = not (off + 127 <= -FAR or off - 511 >= FAR)
                                ps_s = psum[sl]
                                rr = slice(mp * 64, mp * 64 + 64)
                                fns = [lambda e, rr=rr, kc=kc, qg=qg, ps_s=ps_s, near=near: e.matmul(
                                    ps_s[:], lhsT=kT[:, kc * 128:(kc + 1) * 128], rhs=qT[:, mp, qg * 512:(qg + 1) * 512], start=True, stop=not near)]
                                rds = [T_qT, T_kT]
                                if near:
                                    oi = offs.index(off)
                                    fns.append(lambda e, oi=oi, ps_s=ps_s: e.matmul(ps_s[:], lhsT=anti_b[:], rhs=BMB[:, oi, :], start=False, stop=True))
                                    rds += [T_BMB, T_const]
                                S_.group("pe", fns, reads=rds, writes=[PS[sl]])
                                if near:
                                    S_.op("act", lambda e, sl=sl, ps_s=ps_s: e.activation(out=pTb[sl][:], in_=ps_s[:], func=AF.Exp), reads=[PS[sl]], writes=[T_pTb[sl]])
                                else:
                                    sg = 0 if off < 0 else 1
                                    S_.op("act", lambda e, sl=sl, ps_s=ps_s, sg=sg: e.activation(out=pTb[sl][:], in_=ps_s[:], func=AF.Exp, bias=biasfar[:, hb, sg:sg + 1]),
                                          reads=[PS[sl], T_const], writes=[T_pTb[sl]])

                            def fB2(it):
                                kc, mp, sl = it
                                S_.group("pe", [
                                    lambda e, sl=sl, kc=kc, mp=mp: e.matmul(psum[4 + 2 * mp][:], lhsT=vB[:, kc, :], rhs=pTb[sl][:], start=(kc == 0), stop=(kc == NT - 1)),
                                    lambda e, sl=sl, kc=kc, mp=mp: e.matmul(psum[5 + 2 * mp][:], lhsT=ones_b[:], rhs=pTb[sl][:], start=(kc == 0), stop=(kc == NT - 1)),
                                ], reads=[T_vB, T_pTb[sl], T_const], writes=[PS[4 + 2 * mp], PS[5 + 2 * mp]])
                            pipeline(itemsB, fB1, fB2, depth=2)
                            S_.op("dve", lambda e: e.reciprocal(out=fa[0][:], in_=psum[5][:]), reads=[PS[5]], writes=[T_fa[0]])
                            S_.op("dve", lambda e: e.tensor_tensor(out=fa[1][:], in0=psum[4][:], in1=fa[0][:], op=ALU.mult), reads=[PS[4], T_fa[0]], writes=[T_fa[1]])
                            S_.op("dve", lambda e: e.reciprocal(out=fa[0][:], in_=psum[7][:]), reads=[PS[7], T_fa[0]], writes=[T_fa[0]])
                            S_.op("dve", lambda e: e.tensor_tensor(out=fa[2][:], in0=psum[6][:], in1=fa[0][:], op=ALU.mult), reads=[PS[6], T_fa[0]], writes=[T_fa[2]])
                            S_.op("dve", lambda e: e.scalar_tensor_tensor(out=fa[3][:], in0=fa[2][:], scalar=neglam[:, 0:1], in1=fa[1][:], op0=ALU.mult, op1=ALU.add),
                                  reads=[T_fa[1], T_fa[2], T_const], writes=[T_fa[3]])
                            S_.op("act", lambda e: e.activation(out=fa[1][:], in_=fa[3][:], func=AF.Square), reads=[T_fa[3], T_fa[1]], writes=[T_fa[1]])
                            S_.op("pe", lambda e: e.matmul(psum[0][:], lhsT=ones_f[:], rhs=fa[1][:], start=True, stop=True), reads=[T_fa[1], T_const], writes=[PS[0]])
                            S_.op("act", lambda e: e.activation(out=fa[2][:], in_=psum[0][:], func=AF.Sqrt, scale=1.0 / 128, bias=EPS), reads=[PS[0], T_fa[2]], writes=[T_fa[2]])
                            S_.op("dve", lambda e: e.reciprocal(out=fa[2][:], in_=fa[2][:]), reads=[T_fa[2]], writes=[T_fa[2]])
                            ob = qg % 2
                            S_.op("dve", lambda e, ob=ob: e.scalar_tensor_tensor(out=oTb[ob][:], in0=fa[3][:], scalar=gdiff[:, 0:1], in1=fa[2][:], op0=ALU.mult, op1=ALU.mult),
                                  reads=[T_fa[3], T_fa[2], T_const], writes=[T_oTb[ob]])
                            S_.dma("pool", oT_scr[si][512 + hb * 128:512 + (hb + 1) * 128, qg * 512:(qg + 1) * 512], oTb[ob][:], reads=[T_oTb[ob]], writes=[T_oTs[si]])

        S_.barrier()
        stA.close()
        with ExitStack() as s4:
            def sb4(name, shape, dt):
                return s4.enter_context(nc.sbuf_tensor(name, list(shape), dt))
            wout_b = sb4("wout_b", [128, KD, D], BF16)
            wqs = [sb4(f"wqs{i}", [128, KD, 128], BF16) for i in range(2)]
            T_wqs = [Dep(), Dep()]
            T_wqscr = Dep()
            KzT = sb4("KzT", [128, 2, 128], BF16)
            T_w4 = Dep()
            s4t = ExitStack()
            stg = s4t.enter_context(nc.sbuf_tensor("stg4", [128, KD, 512], F32))
            T_stg = Dep()
            wqst = s4t.enter_context(nc.sbuf_tensor("wqst", [128, KD, 512], BF16))
            T_wqst = Dep()
            for cb in range(2):
                S_.dma("sp", stg[:], w_out[:, cb * 512:(cb + 1) * 512].rearrange("(k p) c -> p k c", p=128), writes=[T_stg])
                S_.op("dve", lambda e, cb=cb: e.tensor_copy(out=wout_b[:, :, cb * 512:(cb + 1) * 512], in_=stg[:]), reads=[T_stg], writes=[T_w4])
            for cb in range(4):
                S_.dma("sp", stg[:], w_q[:, cb * 512:(cb + 1) * 512].rearrange("(k p) c -> p k c", p=128), writes=[T_stg])
                for k in range(KD):
                    S_.op("dve", lambda e, k=k, cb=cb: e.tensor_scalar(out=wqst[:, k, :], in0=stg[:, k, :], scalar1=gF[:, k:k + 1], scalar2=None, op0=ALU.mult),
                          reads=[T_stg, T_const], writes=[T_wqst])
                S_.dma("sp", wq_scr[:, :, cb * 512:(cb + 1) * 512], wqst[:], reads=[T_wqst], writes=[T_wqscr])
            for z in range(2):
                S_.dma("sp", stg[:, 0, 0:128], subk[z], writes=[T_stg])
                S_.op("pe", lambda e: e.transpose(out=psum[0][:, 0:128], in_=stg[:, 0, 0:128], identity=ident_f[:]), reads=[T_stg, T_const], writes=[PS[0]])
                S_.op("act", lambda e, z=z: e.copy(out=KzT[:, z, :], in_=psum[0][:, 0:128]), reads=[PS[0]], writes=[T_w4])

            S_.barrier()
            s4t.close()
            TB = 256
            oTl = sb4("oTl", [128, KD, TB], BF16)
            T_oTl = Dep()
            xt4 = sb4("xt4", [128, D], F32)
            T_xt4 = Dep()
            x1 = [[sb4(f"x1_{bf}_{i}", [128, D], F32) for i in range(2)] for bf in range(2)]
            T_x1 = [[Dep(), Dep()] for _ in range(2)]
            st5 = sb4("st5", [128, 8], F32)
            T_st5 = Dep()
            S_.op("pool", lambda e: e.memset(st5[:, 4:5], -0.5), writes=[T_st5])
            xtmp = sb4("xtmp", [128, 512], F32)
            T_xtmp = Dep()
            hnb = sb4("hnb", [128, D], BF16)
            T_hnb = Dep()
            hT = [sb4(f"hT{bf}", [128, KD, TB], BF16) for bf in range(2)]
            T_hT = [Dep(), Dep()]
            qpT = sb4("qpT", [128, 16, TB], BF16)
            T_qpT = Dep()
            sc = [sb4(f"sc{i}", [128, 2048], F32) for i in range(2)]
            T_sc = [[Dep() for _ in range(4)] for _ in range(2)]
            sc2 = sb4("sc2", [128, 128], F32)
            T_sc2 = Dep()
            top = sb4("top", [128, 16, 16], F32)
            idx = sb4("idx", [128, 16, 16], U32)
            idxf = sb4("idxf", [128, 16, 16], F32)
            T_top, T_idx = Dep(), Dep()
            T_cand = Dep()
            cand2 = sb4("cand2", [128, 256], F32)
            T_cand2 = Dep()
            best = sb4("best", [128, 8, 16], F32)
            pos = sb4("pos", [128, 8, 16], U32)
            k12 = sb4("k12", [128, 2, 8, 16], U32)
            k12f = sb4("k12f", [128, 2, 8, 16], F32)
            T_best, T_pos, T_k12 = Dep(), Dep(), Dep()
            T_eq = T_cand
            IJg2 = sb4("IJg", [128, 2, 3, 128], F32)
            T_IJg = Dep()
            gw = sb4("gw", [128, 8, 16], F32)
            zz = sb4("zz", [128, 8], F32)
            T_gw = Dep()
            IJgT = sb4("IJgT", [128, 3, TB], F32)
            T_IJgT = Dep()
            NRING = 12
            AB = sb4("AB", [128, NRING, 2, 128], BF16)
            T_AB = [Dep() for _ in range(NRING)]
            T_BB = [Dep() for _ in range(NRING)]
            Gsb = sb4("Gsb", [128, TB, 128], BF16)
            T_G = Dep()
            NUV = 8
            UTb = [sb4(f"UTb{i}", [128, KD, 128], BF16) for i in range(NUV)]
            Vb = [sb4(f"Vb{i}", [128, D], BF16) for i in range(NUV)]
            T_UTb = [Dep() for _ in range(NUV)]
            T_Vb = [Dep() for _ in range(NUV)]
            asb = [sb4(f"asb{i}", [128, TB], BF16) for i in range(3)]
            wsb = [sb4(f"wsb{i}", [128, TB], BF16) for i in range(3)]
            T_asb = [Dep(), Dep(), Dep()]
            T_wsb = [Dep(), Dep(), Dep()]
            yt = sb4("yt", [128, 512], F32)
            T_yt = Dep()
            iota16 = iota_f[:, 0:16]

            blocks = [(si, blk * TB) for si, S in enumerate(seq_lens) for blk in range(S // TB)]

            def gen_pe(bi, pbks):
                si, t0 = blocks[bi]
                bf = bi % 2
                cnt_ = [0]

                def nb_():
                    cnt_[0] += 1
                    return pbks[cnt_[0] % len(pbks)]
                S_.dma("sp", oTl[:], oT_scr[si][:, t0:t0 + TB].rearrange("(k p) t -> p k t", p=128), reads=[T_oTs[si]], writes=[T_oTl])
                for t2 in range(2):
                    S_.dma("sp", xt4[:], xs[si][t0 + t2 * 128:t0 + (t2 + 1) * 128, :], writes=[T_xt4])
                    for hf in range(2):
                        pbk = nb_()
                        S_.group("pe", [lambda e, k=k, t2=t2, hf=hf, pbk=pbk: e.matmul(psum[pbk][:], lhsT=oTl[:, k, t2 * 128:(t2 + 1) * 128], rhs=wout_b[:, k, hf * 512:(hf + 1) * 512], start=(k == 0), stop=(k == KD - 1)) for k in range(KD)],
                                 reads=[T_oTl, T_w4], writes=[PS[pbk]])
                        S_.op("act", lambda e, pbk=pbk: e.copy(out=xtmp[:], in_=psum[pbk][:]), reads=[PS[pbk]], writes=[T_xtmp])
                        S_.op("pool", lambda e, t2=t2, hf=hf: e.tensor_tensor(out=x1[bf][t2][:, hf * 512:(hf + 1) * 512], in0=xtmp[:], in1=xt4[:, hf * 512:(hf + 1) * 512], op=ALU.add),
                              reads=[T_xtmp, T_xt4], writes=[T_x1[bf][t2]])
                        yield
                    if debug:
                        S_.dma("pool", dbg[f"dbg_x1_{si}"][t0 + t2 * 128:t0 + (t2 + 1) * 128, :], x1[bf][t2][:], reads=[T_x1[bf][t2]], writes=[Dep()])
                    S_.op("act", lambda e, t2=t2: e.activation(out=hnb[:], in_=x1[bf][t2][:], func=AF.Square, accum_out=st5[:, 0:1]), reads=[T_x1[bf][t2]], writes=[T_hnb, T_st5])
                    S_.op("pool", lambda e: e.tensor_scalar(out=st5[:, 1:2], in0=st5[:, 0:1], scalar1=1.0 / D, scalar2=EPS, op0=ALU.mult, op1=ALU.add), reads=[T_st5], writes=[T_st5])
                    S_.op("pool", lambda e: e.tensor_tensor(out=st5[:, 2:3], in0=st5[:, 1:2], in1=st5[:, 4:5], op=ALU.pow), reads=[T_st5], writes=[T_st5])
                    S_.op("act", lambda e, t2=t2: e.activation(out=hnb[:], in_=x1[bf][t2][:], func=AF.Copy, scale=st5[:, 2:3]), reads=[T_x1[bf][t2], T_st5], writes=[T_hnb])
                    yield
                    pbk = nb_()
                    pst = psum[pbk][:].bitcast(BF16)
                    S_.group("pe", [lambda e, k=k, pst=pst: e.transpose(out=pst[:, k * 128:(k + 1) * 128], in_=hnb[:, k * 128:(k + 1) * 128], identity=ident_b[:]) for k in range(KD)],
                             reads=[T_hnb, T_const], writes=[PS[pbk]])
                    S_.op("act", lambda e, t2=t2, pst=pst: e.copy(out=hT[bf][:, :, t2 * 128:(t2 + 1) * 128], in_=pst.rearrange("p (k t) -> p k t", k=KD)), reads=[PS[pbk]], writes=[T_hT[bf]])
                    yield
                for pz in range(16):
                    b = pz % 2
                    pbk = nb_()
                    S_.dma("sp", wqs[b][:], wq_scr[:, :, pz * 128:(pz + 1) * 128], reads=[T_wqscr], writes=[T_wqs[b]])
                    S_.group("pe", [lambda e, k=k, pz=pz, b=b, pbk=pbk: e.matmul(psum[pbk][:, 0:TB], lhsT=wqs[b][:, k, :], rhs=hT[bf][:, k, :], start=(k == 0), stop=(k == KD - 1)) for k in range(KD)],
                             reads=[T_hT[bf], T_wqs[b]], writes=[PS[pbk]])
                    S_.op("act", lambda e, pz=pz, b=b, pbk=pbk: e.copy(out=qpT[:, pz, :], in_=psum[pbk][:, 0:TB]), reads=[PS[pbk]], writes=[T_qpT])
                    yield
                for t2 in range(2):
                    for bk in range(4):
                        pbk = nb_()
                        S_.group("pe", [lambda e, pz=pz, t2=t2, pbk=pbk: e.matmul(psum[pbk][:, (pz % 4) * 128:(pz % 4 + 1) * 128], lhsT=qpT[:, pz, t2 * 128:(t2 + 1) * 128], rhs=KzT[:, pz % 2, :], start=True, stop=True) for pz in range(bk * 4, bk * 4 + 4)],
                                 reads=[T_qpT, T_w4], writes=[PS[pbk]])
                        S_.op("act", lambda e, t2=t2, bk=bk, pbk=pbk: e.copy(out=sc[t2][:, bk * 512:(bk + 1) * 512], in_=psum[pbk][:]), reads=[PS[pbk]], writes=[T_sc[t2][bk], T_cand])
                        yield

            def gen_dve(bi):
                si, t0 = blocks[bi]
                bf = bi % 2
                for t2 in range(2):
                    for bk in range(4):
                        for g in range(bk * 4, bk * 4 + 4):
                            gs = slice(g * 128, (g + 1) * 128)
                            S_.op("dve", lambda e, g=g, gs=gs, t2=t2: e.max(out=top[:, g, 0:8], in_=sc[t2][:, gs]), reads=[T_sc[t2][bk]], writes=[T_top])
                            S_.op("dve", lambda e, g=g, gs=gs, t2=t2: e.max_index(out=idx[:, g, 0:8], in_max=top[:, g, 0:8], in_values=sc[t2][:, gs]), reads=[T_sc[t2][bk], T_top], writes=[T_idx])
                            S_.op("dve", lambda e, g=g, gs=gs, t2=t2: e.match_replace(out=sc2[:], in_to_replace=top[:, g, 0:8], in_values=sc[t2][:, gs], imm_value=-1e30), reads=[T_sc[t2][bk], T_top], writes=[T_sc2])
                            yield
                            S_.op("dve", lambda e, g=g: e.max(out=top[:, g, 8:16], in_=sc2[:]), reads=[T_sc2], writes=[T_top])
                            S_.op("dve", lambda e, g=g: e.max_index(out=idx[:, g, 8:16], in_max=top[:, g, 8:16], in_values=sc2[:]), reads=[T_sc2, T_top], writes=[T_idx])
                            yield
                    S_.op("dve", lambda e: e.tensor_copy(out=idxf[:], in_=idx[:]), reads=[T_idx], writes=[T_idx])
                    t4v = top[:].rearrange("p (h z) k -> p h z k", z=2)
                    i4v = idxf[:].rearrange("p (h z) k -> p h z k", z=2)
                    cand = sc[t2][:].rearrange("p (h c) -> p h c", h=8)
                    eq = sc[t2][:].rearrange("p (h s k) -> p h s k", h=8, s=16)
                    c4 = sc[t2][:].rearrange("p (h a b) -> p h a b", h=8, a=16)
                    S_.op("dve", lambda e: e.tensor_tensor(out=c4, in0=t4v[:, :, 0, :].unsqueeze(3).to_broadcast([128, 8, 16, 16]), in1=t4v[:, :, 1, :].unsqueeze(2).to_broadcast([128, 8, 16, 16]), op=ALU.add),
                          reads=[T_top], writes=[T_cand] + T_sc[t2])
                    yield
                    for p in range(8):
                        S_.op("dve", lambda e, p=p: e.max(out=best[:, p, 0:8], in_=cand[:, p, :]), reads=[T_cand], writes=[T_best])
                        S_.op("dve", lambda e, p=p: e.max_index(out=pos[:, p, 0:8], in_max=best[:, p, 0:8], in_values=cand[:, p, :]), reads=[T_cand, T_best], writes=[T_pos])
                        S_.op("dve", lambda e, p=p: e.match_replace(out=cand2[:], in_to_replace=best[:, p, 0:8], in_values=cand[:, p, :], imm_value=-1e30), reads=[T_cand, T_best], writes=[T_cand2])
                        yield
                        S_.op("dve", lambda e, p=p: e.max(out=best[:, p, 8:16], in_=cand2[:]), reads=[T_cand2], writes=[T_best])
                        S_.op("dve", lambda e, p=p: e.max_index(out=pos[:, p, 8:16], in_max=best[:, p, 8:16], in_values=cand2[:]), reads=[T_cand2, T_best], writes=[T_pos])
                        yield
                    S_.op("dve", lambda e: e.tensor_tensor(out=gw[:], in0=best[:], in1=best[:, :, 0:1].to_broadcast([128, 8, 16]), op=ALU.subtract), reads=[T_best], writes=[T_gw])
                    S_.op("act", lambda e: e.activation(out=gw[:], in_=gw[:], func=AF.Exp), reads=[T_gw], writes=[T_gw])
                    S_.op("dve", lambda e: e.reduce_sum(out=zz[:], in_=gw[:], axis=AX.X), reads=[T_gw], writes=[T_gw])
                    S_.op("dve", lambda e: e.reciprocal(out=zz[:], in_=zz[:]), reads=[T_gw], writes=[T_gw])
                    S_.op("dve", lambda e, t2=t2: e.tensor_tensor(out=IJg2[:, t2, 2, :].rearrange("p (h s) -> p h s", h=8), in0=gw[:], in1=zz[:].unsqueeze(2).to_broadcast([128, 8, 16]), op=ALU.mult),
                          reads=[T_gw], writes=[T_IJg])
                    yield
                    S_.op("dve", lambda e: e.tensor_single_scalar(out=k12[:, 0], in_=pos[:], scalar=4, op=ALU.logical_shift_right), reads=[T_pos], writes=[T_k12])
                    S_.op("dve", lambda e: e.tensor_single_scalar(out=k12[:, 1], in_=pos[:], scalar=15, op=ALU.bitwise_and), reads=[T_pos, T_k12], writes=[T_k12])
                    S_.op("dve", lambda e: e.tensor_copy(out=k12f[:], in_=k12[:]), reads=[T_k12], writes=[T_k12])
                    yield
                    for z in range(2):
                        S_.op("dve", lambda e, z=z: e.tensor_tensor(out=eq[:], in0=k12f[:, z].unsqueeze(3).to_broadcast([128, 8, 16, 16]),
                                                                      in1=iota16.unsqueeze(1).unsqueeze(1).to_broadcast([128, 8, 16, 16]), op=ALU.is_equal),
                              reads=[T_k12, T_const, T_eq], writes=[T_eq])
                        yield
                        S_.op("dve", lambda e, z=z: e.tensor_tensor(out=eq[:], in0=eq[:], in1=i4v[:, :, z, :].unsqueeze(2).to_broadcast([128, 8, 16, 16]), op=ALU.mult),
                              reads=[T_eq, T_idx], writes=[T_eq])
                        yield
                        S_.op("dve", lambda e, z=z, t2=t2: e.reduce_sum(out=IJg2[:, t2, z, :], in_=eq[:].rearrange("p h s k -> p (h s) k"), axis=AX.X), reads=[T_eq], writes=[T_IJg])
                        yield

            def emit_B6(gpe=None):
                for t2 in range(2):
                    S_.group("pe", [lambda e, j=j, t2=t2: e.transpose(out=psum[5][:, j * 128:(j + 1) * 128], in_=IJg2[:, t2, j, :], identity=ident_f[:]) for j in range(3)],
                             reads=[T_IJg, T_const], writes=[PS[5]])
                    S_.op("act", lambda e, t2=t2: e.copy(out=IJgT[:, :, t2 * 128:(t2 + 1) * 128], in_=psum[5][:, 0:384].rearrange("p (j t) -> p j t", j=3)), reads=[PS[5]], writes=[T_IJgT])
                for tq in range(TB // 4):
                    gb = 5 + (tq % 2)
                    for tt in range(4):
                        t = tq * 4 + tt
                        rs = t % NRING
                        S_.op("dve", lambda e, t=t, rs=rs: e.tensor_scalar(out=AB[:, rs, 0, :], in0=iota_f[:], scalar1=IJgT[:, 0, t:t + 1], scalar2=IJgT[:, 2, t:t + 1], op0=ALU.is_equal, op1=ALU.mult),
                              reads=[T_IJgT, T_const], writes=[T_AB[rs]])
                        S_.op("dve", lambda e, t=t, rs=rs: e.tensor_scalar(out=AB[:, rs, 1, :], in0=iota_f[:], scalar1=IJgT[:, 1, t:t + 1], scalar2=None, op0=ALU.is_equal),
                              reads=[T_IJgT, T_const], writes=[T_BB[rs]])
                        S_.op("pe", lambda e, tt=tt, rs=rs, gb=gb: e.matmul(psum[gb][:, tt * 128:(tt + 1) * 128], lhsT=AB[:, rs, 1, :], rhs=AB[:, rs, 0, :], start=True, stop=True),
                              reads=[T_AB[rs], T_BB[rs]], writes=[PS[gb]])
                    S_.op("act", lambda e, tq=tq, gb=gb: e.copy(out=Gsb[:, tq * 4:tq * 4 + 4, :].rearrange("j t i -> j (t i)"), in_=psum[gb][:]),
                          reads=[PS[gb]], writes=[T_G])
                    if gpe is not None:
                        next(gpe, None)
                if gpe is not None:
                    for _ in gpe:
                        pass

            def emit_B7(bi, filler):
                bf = bi % 2

                def f71(i):
                    u = i % NUV
                    a = i % 3
                    S_.dma("sp", UTb[u][:], UT_scr[i], reads=[T_UV], writes=[T_UTb[u]])
                    S_.dma("sp", Vb[u][:], V_scr[i * 128:(i + 1) * 128, :], reads=[T_UV], writes=[T_Vb[u]])
                    pa_ = psum[(4, 6, 7)[a]][:, 0:TB]
                    S_.group("pe", [lambda e, k=k, u=u, pa_=pa_: e.matmul(pa_, lhsT=UTb[u][:, k, :], rhs=hT[bf][:, k, :], start=(k == 0), stop=(k == KD - 1)) for k in range(KD)],
                             reads=[T_UTb[u], T_hT[bf]], writes=[PSA[a]])
                    S_.op("act", lambda e, a=a, pa_=pa_: e.activation(out=asb[a][:], in_=pa_, func=AF.Gelu), reads=[PSA[a]], writes=[T_asb[a]])
                    S_.op("pool", lambda e, a=a, i=i: e.tensor_tensor(out=wsb[a][:], in0=asb[a][:], in1=Gsb[:, :, i], op=ALU.mult), reads=[T_asb[a], T_G], writes=[T_wsb[a]])

                def f72(i):
                    u = i % NUV
                    a = i % 3
                    S_.group("pe", [lambda e, t2=t2, hf=hf, a=a, u=u, i=i: e.matmul(psum[t2 * 2 + hf][:], lhsT=wsb[a][:, t2 * 128:(t2 + 1) * 128], rhs=Vb[u][:, hf * 512:(hf + 1) * 512], start=(i == 0), stop=(i == n_exp_chunks - 1)) for t2 in range(2) for hf in range(2)],
                             reads=[T_wsb[a], T_Vb[u]], writes=[PS[0], PS[1], PS[2], PS[3]])
                    if filler is not None:
                        next(filler, None)
                pipeline(range(n_exp_chunks), f71, f72, depth=2)
                if filler is not None:
                    for _ in filler:
                        pass

            def emit_B8(bi):
                si, t0 = blocks[bi]
                bf = bi % 2
                for t2 in range(2):
                    for hf in range(2):
                        S_.op("dve", lambda e, t2=t2, hf=hf: e.tensor_tensor(out=yt[:], in0=psum[t2 * 2 + hf][:], in1=x1[bf][t2][:, hf * 512:(hf + 1) * 512], op=ALU.add),
                              reads=[PS[t2 * 2 + hf], T_x1[bf][t2]], writes=[T_yt])
                        S_.dma("pool", ys[si][t0 + t2 * 128:t0 + (t2 + 1) * 128, hf * 512:(hf + 1) * 512], yt[:], reads=[T_yt], writes=[T_y])

            for _ in gen_pe(0, (5, 6)):
                pass
            for _ in gen_dve(0):
                pass
            for bi in range(len(blocks)):
                emit_B6(gen_pe(bi + 1, (7, 4)) if bi + 1 < len(blocks) else None)
                filler = gen_dve(bi + 1) if bi + 1 < len(blocks) else None
                emit_B7(bi, filler)
                emit_B8(bi)
        S_.finish()
    print("instructions:", S_.ninst)
    return nc


_CONSTS = None


def _in_maps(inputs, seq_lens_per_core, core_seqs):
    global _CONSTS
    if _CONSTS is None:
        _CONSTS = host_consts()
    c = _CONSTS
    base = {
        "rel_bias": inputs["rel_bias"], "attn_norm_g": inputs["attn_norm_g"][0], "w_in": inputs["w_in"][0],
        "q_norm_a": inputs["q_norm_a"][0], "k_norm_a": inputs["k_norm_a"][0], "q_norm_b": inputs["q_norm_b"][0],
        "k_norm_b": inputs["k_norm_b"][0], "lambda_q1": inputs["lambda_q1"][0], "lambda_k1": inputs["lambda_k1"][0],
        "lambda_q2": inputs["lambda_q2"][0], "lambda_k2": inputs["lambda_k2"][0], "diff_norm_g": inputs["diff_norm_g"][0],
        "w_out": inputs["w_out"][0], "ffn_norm_g": inputs["ffn_norm_g"][0], "peer_w_q": inputs["peer_w_q"][0],
        "peer_sub_keys": inputs["peer_sub_keys"][0], "peer_u": inputs["peer_u"][0], "peer_v": inputs["peer_v"][0],
        "c_ident": c["ident"], "c_antiI": c["antiI"], "c_blockones": c["blockones"], "c_iota": c["iota"],
        "c_ohb": c["ohb"], "c_oha": c["oha"], "c_sel65": c["sel65"],
    }
    base = {k: np.ascontiguousarray(np.asarray(v, dtype=np.float32)) for k, v in base.items()}
    maps = []
    for seqs in core_seqs:
        m = dict(base)
        for i, xarr in enumerate(seqs):
            m[f"x{i}"] = np.ascontiguousarray(np.asarray(xarr, dtype=np.float32))
        maps.append(m)
    return maps


def kernel(**inputs):
    inputs = {k: np.asarray(v) for k, v in inputs.items()}
    xp = inputs["x_prompt"]
    xsmp = inputs["x_sample"]
    n = 8
    seq_lens = [xp.shape[1], xsmp.shape[1], xsmp.shape[1]]
    core_seqs = [[xp[c], xsmp[2 * c], xsmp[2 * c + 1]] for c in range(n)]
    nc = build(seq_lens)
    maps = _in_maps(inputs, seq_lens, core_seqs)
    res = run_bass_kernel_spmd(nc, maps, core_ids=list(range(n)))
    yp = np.stack([res.results[c]["y0"] for c in range(n)], axis=0).astype(np.float32)
    ysm = np.empty(xsmp.shape, np.float32)
    for c in range(n):
        ysm[2 * c] = res.results[c]["y1"]
        ysm[2 * c + 1] = res.results[c]["y2"]
    return (yp, ysm)
```

```python
import math
from contextlib import ExitStack
import numpy as np
import concourse.bass as bass
import concourse.mybir as mybir
from concourse.bass_utils import run_bass_kernel_spmd

F32 = mybir.dt.float32
BF16 = mybir.dt.bfloat16
U32 = mybir.dt.uint32
AF = mybir.ActivationFunctionType
ALU = mybir.AluOpType
AX = mybir.AxisListType

D = 1024
KD = 8
EPS = 1e-6
NEG = -30000.0
NDS = 56


class Dep:
    __slots__ = ("w", "r")

    def __init__(self):
        self.w = None
        self.r = {}


class Eng:
    def __init__(self, name, eng, sem):
        self.name = name
        self.eng = eng
        self.sem = sem
        self.count = 0
        self.seen = {}


class Sched:
    def __init__(self, nc, stack):
        self.nc = nc
        self.E = {}
        for name, eng in [("pe", nc.tensor), ("act", nc.scalar), ("dve", nc.vector),
                          ("pool", nc.gpsimd), ("sp", nc.sync)]:
            sem = stack.enter_context(nc.semaphore(f"s_{name}"))
            self.E[name] = Eng(name, eng, sem)
        self.dsems = []
        self.dpool = {"sp": [], "pool": []}
        for i in range(NDS):
            sem = stack.enter_context(nc.semaphore(f"dq{i}"))
            self.dsems.append([sem, 0])
            self.dpool["pool" if i >= NDS - 16 else "sp"].append(i)
        self.dnext = {"sp": 0, "pool": 0}
        self.ninst = 0

    def _wait(self, e, deps):
        best = {}
        for (key, sem, val) in deps:
            if key == "pe" and e.name == "pe":
                continue
            if val > best.get(key, (None, 0))[1]:
                best[key] = (sem, val)
        for key, (sem, val) in best.items():
            if e.seen.get(key, 0) < val:
                e.eng.wait_ge(sem, val)
                e.seen[key] = val

    @staticmethod
    def _deps(reads, writes):
        d = []
        for t in reads:
            if t.w is not None:
                d.append(t.w)
        for t in writes:
            if t.w is not None:
                d.append(t.w)
            d.extend(t.r.values())
        return d

    @staticmethod
    def _mark(tok, reads, writes):
        for t in writes:
            t.w = tok
            t.r = {}
        for t in reads:
            t.r[tok[0]] = tok

    def op(self, en, fn, reads=(), writes=()):
        e = self.E[en]
        self._wait(e, self._deps(reads, writes))
        ins = fn(e.eng)
        e.count += 1
        ins.then_inc(e.sem, 1)
        self._mark((en, e.sem, e.count), reads, writes)
        self.ninst += 1

    def group(self, en, fns, reads=(), writes=()):
        e = self.E[en]
        self._wait(e, self._deps(reads, writes))
        ins = None
        for fn in fns:
            ins = fn(e.eng)
            self.ninst += 1
        e.count += 1
        ins.then_inc(e.sem, 1)
        self._mark((en, e.sem, e.count), reads, writes)

    def dma(self, qn, out, in_, reads=(), writes=(), **kw):
        e = self.E[qn]
        pl = self.dpool[qn]
        idx = pl[self.dnext[qn]]
        self.dnext[qn] = (self.dnext[qn] + 1) % len(pl)
        slot = self.dsems[idx]
        deps = self._deps(reads, writes)
        key = ("d", idx)
        if slot[1] > 0:
            deps.append((key, slot[0], slot[1]))
        self._wait(e, deps)
        ins = e.eng.dma_start(out=out, in_=in_, **kw)
        slot[1] += 16
        ins.then_inc(slot[0], 16)
        self._mark((key, slot[0], slot[1]), reads, writes)
        self.ninst += 1

    def barrier(self):
        for e in self.E.values():
            for o in self.E.values():
                if o.count == 0:
                    continue
                if e.seen.get(o.name, 0) < o.count:
                    e.eng.wait_ge(o.sem, o.count)
                    e.seen[o.name] = o.count
            for idx, slot in enumerate(self.dsems):
                key = ("d", idx)
                if slot[1] > 0 and e.seen.get(key, 0) < slot[1]:
                    e.eng.wait_ge(slot[0], slot[1])
                    e.seen[key] = slot[1]

    def finish(self):
        e = self.E["sp"]
        for idx, slot in enumerate(self.dsems):
            if slot[1] > 0 and e.seen.get(("d", idx), 0) < slot[1]:
                e.eng.wait_ge(slot[0], slot[1])
        for name in ("pe", "act", "dve", "pool"):
            o = self.E[name]
            if o.count > 0:
                e.eng.wait_ge(o.sem, o.count)


def pipeline(items, first, second, depth=1):
    items = list(items)
    n = len(items)
    for i in range(n + depth):
        if i < n:
            first(items[i])
        if i - depth >= 0:
            second(items[i - depth])


def rel_bucket_np(rel):
    nb = 16
    max_exact = 8
    n = np.abs(rel)
    with np.errstate(divide="ignore"):
        large = max_exact + (np.log(np.maximum(n, 1).astype(np.float32) / np.float32(max_exact))
                             / np.float32(math.log(1024 / max_exact)) * (nb - max_exact)).astype(np.int32)
    large = np.minimum(large, nb - 1)
    return (rel > 0).astype(np.int32) * nb + np.where(n < max_exact, n, large)


RB = 4095
FAR = 559


def host_consts():
    c = {}
    c["ident"] = np.eye(128, dtype=np.float32)
    c["antiI"] = np.ascontiguousarray(np.eye(128, dtype=np.float32)[::-1])
    bo = np.zeros((128, 128), np.float32)
    bo[:64, :64] = 1
    bo[64:, 64:] = 1
    c["blockones"] = bo
    c["iota"] = np.tile(np.arange(128, dtype=np.float32)[None, :], (128, 1))
    i = np.arange(8192)
    ohb = np.zeros((32, 8192), np.float32)
    ohb[rel_bucket_np(RB - i), i] = 1.0
    c["ohb"] = ohb
    oha = np.zeros((3, 33, 384), np.float32)
    for di, d in enumerate((1, 4, 16)):
        for ii in range(384):
            rm = 191 - ii
            if abs(rm) <= 64:
                oha[di, rel_bucket_np(np.array(rm * d)), ii] = 1.0
            else:
                oha[di, 32, ii] = 1.0
    c["oha"] = oha
    sel = np.zeros((128, 64), np.float32)
    sel[64, :] = 1.0
    c["sel65"] = sel
    return c


def build(seq_lens, n_exp_chunks=128, debug=False):
    nc = bass.Bass("TRN2", target_bir_lowering=False)
    NSEQ = len(seq_lens)
    TTOT = sum(seq_lens)
    SMAX = max(seq_lens)

    def din(name, shape, dt=F32):
        return nc.dram_tensor(name, list(shape), dt, kind="ExternalInput").ap()

    xs = [din(f"x{i}", [S, D]) for i, S in enumerate(seq_lens)]
    ys = [nc.dram_tensor(f"y{i}", [S, D], F32, kind="ExternalOutput").ap() for i, S in enumerate(seq_lens)]
    rel_bias = din("rel_bias", [32, 12])
    attn_g = din("attn_norm_g", [D])
    w_in = din("w_in", [D, 3072])
    qn_a = din("q_norm_a", [64])
    kn_a = din("k_norm_a", [64])
    qn_b = din("q_norm_b", [64])
    kn_b = din("k_norm_b", [64])
    lq1 = din("lambda_q1", [64])
    lk1 = din("lambda_k1", [64])
    lq2 = din("lambda_q2", [64])
    lk2 = din("lambda_k2", [64])
    dn_g = din("diff_norm_g", [128])
    w_out = din("w_out", [D, D])
    ffn_g = din("ffn_norm_g", [D])
    w_q = din("peer_w_q", [D, 2048])
    subk = din("peer_sub_keys", [2, 128, 128])
    pu = din("peer_u", [16384, D])
    pv = din("peer_v", [16384, D])
    c_ident = din("c_ident", [128, 128])
    c_anti = din("c_antiI", [128, 128])
    c_bo = din("c_blockones", [128, 128])
    c_iota = din("c_iota", [128, 128])
    c_ohb = din("c_ohb", [32, 8192])
    c_oha = din("c_oha", [3, 33, 384])
    c_sel = din("c_sel65", [128, 64])

    def dscr(name, shape, dt):
        return nc.dram_tensor(name, list(shape), dt, kind="Internal")

    win_bf = dscr("win_bf", [128, KD, 3072], BF16).ap()
    GB_h = dscr("GBseq", [4, 8192], BF16)
    GA_h = dscr("GAseq", [8, 3, 384], BF16)
    oT_scr = [dscr(f"oT{i}", [D, S], BF16).ap() for i, S in enumerate(seq_lens)]
    wq_scr = dscr("wq_bf", [128, KD, 2048], BF16).ap()
    UT_scr = dscr("UTs", [128, 128, KD, 128], BF16).ap()
    V_scr = dscr("Vs", [16384, D], BF16).ap()
    dbg = {}
    if debug:
        for i, S in enumerate(seq_lens):
            dbg[f"dbg_x1_{i}"] = nc.dram_tensor(f"dbg_x1_{i}", [S, D], F32, kind="ExternalOutput").ap()

    with ExitStack() as st:
        S_ = Sched(nc, st)

        def sb(name, shape, dt):
            return st.enter_context(nc.sbuf_tensor(name, list(shape), dt))

        ident_f = sb("ident_f", [128, 128], F32)
        ident_b = sb("ident_b", [128, 128], BF16)
        anti_b = sb("anti_b", [128, 128], BF16)
        bo_b = sb("bo_b", [128, 128], BF16)
        ones_b = sb("ones_b", [128, 128], BF16)
        ones_f = sb("ones_f", [128, 128], F32)
        iota_f = sb("iota_f", [128, 128], F32)
        sel_f = sb("sel_f", [128, 64], F32)
        gq_a = sb("gq_a", [128, 1], F32)
        gk_a = sb("gk_a", [128, 1], F32)
        gq_b = sb("gq_b", [128, 1], F32)
        gk_b = sb("gk_b", [128, 1], F32)
        gdiff = sb("gdiff", [128, 1], F32)
        neglam = sb("neglam", [128, 1], F32)
        biasfar = sb("biasfar", [128, 4, 2], F32)
        gA = sb("gA", [128, KD], F32)
        gF = sb("gF", [128, KD], F32)
        T_const = Dep()
        psum = [st.enter_context(nc.psum_tensor(f"ps{i}", [128, 512], F32)) for i in range(8)]
        PS = [Dep() for _ in range(8)]
        PSA = [PS[4], PS[6], PS[7]]
        T_oTs = [Dep() for _ in seq_lens]
        T_y = Dep()
        stA = ExitStack()
        BMA = stA.enter_context(nc.sbuf_tensor('BMA', [128, 8, 3, 2, 128], BF16))

        with ExitStack() as s0:
            def sb0(name, shape, dt):
                return s0.enter_context(nc.sbuf_tensor(name, list(shape), dt))
            stage = sb0("stage0", [128, 4096], F32)
            T_stage = Dep()
            tmp = sb0("tmp0", [128, 512], F32)
            T_tmp = Dep()
            S_.dma("sp", ident_f[:], c_ident, writes=[T_const])
            S_.dma("sp", iota_f[:], c_iota, writes=[T_const])
            S_.dma("sp", sel_f[:], c_sel, writes=[T_const])
            S_.dma("sp", stage[:, 0:128], c_anti, writes=[T_stage])
            S_.dma("sp", stage[:, 128:256], c_bo, writes=[T_stage])
            S_.op("dve", lambda e: e.tensor_copy(out=ident_b[:], in_=ident_f[:]), reads=[T_const], writes=[T_const])
            S_.op("dve", lambda e: e.tensor_copy(out=anti_b[:], in_=stage[:, 0:128]), reads=[T_stage], writes=[T_const])
            S_.op("dve", lambda e: e.tensor_copy(out=bo_b[:], in_=stage[:, 128:256]), reads=[T_stage], writes=[T_const])
            S_.op("dve", lambda e: e.memset(ones_b[:], 1.0), writes=[T_const])
            S_.op("dve", lambda e: e.memset(ones_f[:], 1.0), writes=[T_const])
            T_g = Dep()
            graw = sb0("graw", [128, 8], F32)
            for j, src in enumerate((qn_a, kn_a, qn_b, kn_b)):
                for hh in range(2):
                    S_.dma("sp", graw[hh * 64:(hh + 1) * 64, j:j + 1], src.rearrange("(c o) -> c o", o=1), writes=[T_g])
            S_.dma("sp", graw[:, 4:5], dn_g.rearrange("(c o) -> c o", o=1), writes=[T_g])
            S_.op("dve", lambda e: e.tensor_scalar_mul(out=gq_a[:], in0=graw[:, 0:1], scalar1=0.125), reads=[T_g], writes=[T_const])
            S_.op("dve", lambda e: e.tensor_copy(out=gk_a[:], in_=graw[:, 1:2]), reads=[T_g], writes=[T_const])
            S_.op("dve", lambda e: e.tensor_scalar_mul(out=gq_b[:], in0=graw[:, 2:3], scalar1=0.125), reads=[T_g], writes=[T_const])
            S_.op("dve", lambda e: e.tensor_copy(out=gk_b[:], in_=graw[:, 3:4]), reads=[T_g], writes=[T_const])
            S_.op("dve", lambda e: e.tensor_scalar_mul(out=gdiff[:], in0=graw[:, 4:5], scalar1=0.8), reads=[T_g], writes=[T_const])
            S_.dma("sp", gA[:], attn_g.rearrange("(k p) -> p k", p=128), writes=[T_const], allow_slow_non_contiguous=True)
            S_.dma("sp", gF[:], ffn_g.rearrange("(k p) -> p k", p=128), writes=[T_const], allow_slow_non_contiguous=True)
            lam4 = sb0("lam4", [128, 4, 64], F32)
            T_l = Dep()
            for j, src in enumerate((lq1, lk1, lq2, lk2)):
                S_.dma("sp", lam4[:, j, :], src.partition_broadcast(128), writes=[T_l])
            lamw = sb0("lamw", [128, 8], F32)
            T_lw = Dep()
            prod = sb0("lprod", [128, 2, 64], F32)
            S_.op("dve", lambda e: e.tensor_tensor(out=prod[:, 0, :], in0=lam4[:, 0, :], in1=lam4[:, 1, :], op=ALU.mult), reads=[T_l], writes=[T_lw])
            S_.op("dve", lambda e: e.tensor_tensor(out=prod[:, 1, :], in0=lam4[:, 2, :], in1=lam4[:, 3, :], op=ALU.mult), reads=[T_l, T_lw], writes=[T_lw])
            S_.op("dve", lambda e: e.reduce_sum(out=lamw[:, 0:2], in_=prod[:], axis=AX.X), reads=[T_lw], writes=[T_lw])
            S_.op("act", lambda e: e.activation(out=lamw[:, 2:4], in_=lamw[:, 0:2], func=AF.Exp), reads=[T_lw], writes=[T_lw])
            S_.op("dve", lambda e: e.tensor_tensor(out=lamw[:, 4:5], in0=lamw[:, 3:4], in1=lamw[:, 2:3], op=ALU.subtract), reads=[T_lw], writes=[T_lw])
            S_.op("dve", lambda e: e.tensor_scalar_add(out=neglam[:], in0=lamw[:, 4:5], scalar1=-0.2), reads=[T_lw], writes=[T_const])
            for hb in range(4):
                for sg, row in enumerate((15, 31)):
                    S_.dma("sp", biasfar[:, hb, sg:sg + 1],
                           rel_bias[row:row + 1, 8 + hb:9 + hb].rearrange("a b -> (a b)").partition_broadcast(128), writes=[T_const])
            tabA = sb0("tabA", [33, 12], F32)
            T_tab = Dep()
            S_.op("dve", lambda e: e.memset(tabA[:], NEG), writes=[T_tab])
            S_.dma("sp", tabA[0:32, :], rel_bias, reads=[], writes=[T_tab])
            ohb_sb = sb0("ohb_sb", [32, 8192], F32)
            oha_sb = sb0("oha_sb", [33, 3, 384], F32)
            T_oh = Dep()
            S_.dma("sp", ohb_sb[:], c_ohb, writes=[T_oh])
            S_.dma("sp", oha_sb[:], c_oha.rearrange("d b i -> b d i"), writes=[T_oh])
            seqb = sb0("seqb", [8, 8192], BF16)
            T_sq = Dep()
            for g in range(16):
                S_.op("pe", lambda e, g=g: e.matmul(psum[g % 2][0:4, :], lhsT=tabA[0:32, 8:12], rhs=ohb_sb[:, g * 512:(g + 1) * 512], start=True, stop=True),
                      reads=[T_tab, T_oh], writes=[PS[g % 2]])
                S_.op("act", lambda e, g=g: e.copy(out=seqb[0:4, g * 512:(g + 1) * 512], in_=psum[g % 2][0:4, :]), reads=[PS[g % 2]], writes=[T_sq])
            T_GB = Dep()
            S_.dma("sp", GB_h.ap(), seqb[0:4, :], reads=[T_sq], writes=[T_GB])
            seqa = sb0("seqa", [8, 3, 384], BF16)
            T_sa = Dep()
            for di in range(3):
                S_.op("pe", lambda e, di=di: e.matmul(psum[2][0:8, 0:384], lhsT=tabA[0:33, 0:8], rhs=oha_sb[:, di, :], start=True, stop=True),
                      reads=[T_tab, T_oh], writes=[PS[2]])
                S_.op("act", lambda e, di=di: e.copy(out=seqa[:, di, :], in_=psum[2][0:8, 0:384]), reads=[PS[2]], writes=[T_sa])
            T_GA = Dep()
            S_.dma("sp", GA_h.ap(), seqa[:], reads=[T_sa], writes=[T_GA])
            for h in range(8):
                for di in range(3):
                    for c in range(2):
                        src = bass.AP(GA_h, (h * 3 + di) * 384 + 128 * (1 - c), [[1, 128], [1, 128]])
                        S_.dma("sp", BMA[:, h, di, c, :], src, reads=[T_GA], writes=[T_const])
            T_win = Dep()
            wst = sb0("wst", [128, KD, 512], BF16)
            T_wst = Dep()
            for cb in range(6):
                S_.dma("sp", stage[:].rearrange("p (k c) -> p k c", k=KD),
                       w_in[:, cb * 512:(cb + 1) * 512].rearrange("(k p) c -> p k c", p=128), writes=[T_stage])
                for k in range(KD):
                    S_.op("dve" if k % 2 else "pool", lambda e, k=k: e.tensor_scalar(out=wst[:, k, :], in0=stage[:, k * 512:(k + 1) * 512], scalar1=gA[:, k:k + 1], scalar2=None, op0=ALU.mult),
                          reads=[T_stage, T_const], writes=[T_wst])
                S_.dma("sp", win_bf[:, :, cb * 512:(cb + 1) * 512], wst[:], reads=[T_wst], writes=[T_win])
            T_UV = Dep()
            gFrow = sb0("gFrow", [128, D], F32)
            T_gfr = Dep()
            S_.dma("sp", gFrow[:], ffn_g.partition_broadcast(128), writes=[T_gfr])
            NB0 = 4
            ub = [sb0(f"ub{i}", [128, D], BF16) for i in range(NB0)]
            T_ub = [Dep() for _ in range(NB0)]
            ut = [sb0(f"ut{i}", [128, KD, 128], BF16) for i in range(NB0)]
            T_ut = [Dep() for _ in range(NB0)]
            vb16 = [sb0(f"vb16{i}", [128, D], BF16) for i in range(NB0)]
            T_vb = [Dep() for _ in range(NB0)]
            ust = [sb0(f"ust{i}", [128, D], F32) for i in range(NB0)]
            T_ust = [Dep() for _ in range(NB0)]
            vst = [sb0(f"vst{i}", [128, D], F32) for i in range(NB0)]
            T_vst = [Dep() for _ in range(NB0)]
            def ld_uv(i):
                b = i % NB0
                S_.dma("sp", ust[b][:], pu[i * 128:(i + 1) * 128, :], writes=[T_ust[b]])
                S_.dma("sp", vst[b][:], pv[i * 128:(i + 1) * 128, :], writes=[T_vst[b]])
            for i in range(min(NB0 - 1, n_exp_chunks)):
                ld_uv(i)
            for i in range(n_exp_chunks):
                b = i % NB0
                pb2 = 4 + (i % 2)
                if i + NB0 - 1 < n_exp_chunks:
                    ld_uv(i + NB0 - 1)
                S_.op("dve", lambda e, b=b: e.tensor_tensor(out=ub[b][:], in0=ust[b][:], in1=gFrow[:], op=ALU.mult),
                      reads=[T_ust[b], T_gfr], writes=[T_ub[b]])
                pst = psum[pb2][:].bitcast(BF16)
                S_.group("pe", [lambda e, k=k, b=b, pst=pst: e.transpose(out=pst[:, k * 128:(k + 1) * 128], in_=ub[b][:, k * 128:(k + 1) * 128], identity=ident_b[:]) for k in range(KD)],
                         reads=[T_ub[b], T_const], writes=[PS[pb2]])
                S_.op("act", lambda e, b=b, pst=pst: e.copy(out=ut[b][:].rearrange("p k e -> p (k e)"), in_=pst), reads=[PS[pb2]], writes=[T_ut[b]])
                S_.dma("sp", UT_scr[i], ut[b][:], reads=[T_ut[b]], writes=[T_UV])
                S_.op("pool", lambda e, b=b: e.tensor_copy(out=vb16[b][:], in_=vst[b][:]), reads=[T_vst[b]], writes=[T_vb[b]])
                S_.dma("sp", V_scr[i * 128:(i + 1) * 128, :], vb16[b][:], reads=[T_vb[b]], writes=[T_UV])

        S_.barrier()
        for si, S in enumerate(seq_lens):
            S_.barrier()
            NT = S // 128
            NG = S // 512
            with ExitStack() as s1:
                def sb1(name, shape, dt):
                    return s1.enter_context(nc.sbuf_tensor(f"{name}_{si}", list(shape), dt))
                xnT = sb1("xnT", [128, KD, S], BF16)
                T_xnT = Dep()
                xt = [sb1(f"xt{i}", [128, D], F32) for i in range(2)]
                T_xt = [Dep(), Dep()]
                junk = sb1("junk", [128, D], BF16)
                T_junk = Dep()
                xnb = [sb1(f"xnb{i}", [128, D], BF16) for i in range(2)]
                T_xnb = [Dep(), Dep()]
                st4 = sb1("st4", [128, 8], F32)
                T_st4 = Dep()
                for i in range(NT):
                    b = i % 2
                    S_.dma("sp", xt[b][:], xs[si][i * 128:(i + 1) * 128, :], writes=[T_xt[b]])
                    S_.op("act", lambda e, b=b: e.activation(out=junk[:], in_=xt[b][:], func=AF.Square, accum_out=st4[:, 0:1]),
                          reads=[T_xt[b]], writes=[T_junk, T_st4])
                    S_.op("act", lambda e: e.activation(out=st4[:, 1:2], in_=st4[:, 0:1], func=AF.Sqrt, scale=1.0 / D, bias=EPS), reads=[T_st4], writes=[T_st4])
                    S_.op("dve", lambda e: e.reciprocal(out=st4[:, 2:3], in_=st4[:, 1:2]), reads=[T_st4], writes=[T_st4])
                    S_.op("dve", lambda e, b=b: e.tensor_scalar(out=xnb[b][:], in0=xt[b][:], scalar1=st4[:, 2:3], scalar2=None, op0=ALU.mult),
                          reads=[T_xt[b], T_st4], writes=[T_xnb[b]])
                    pst = psum[b][:].bitcast(BF16)
                    S_.group("pe", [lambda e, k=k, b=b, pst=pst: e.transpose(out=pst[:, k * 128:(k + 1) * 128], in_=xnb[b][:, k * 128:(k + 1) * 128], identity=ident_b[:]) for k in range(KD)],
                             reads=[T_xnb[b], T_const], writes=[PS[b]])
                    S_.op("act", lambda e, i=i, pst=pst: e.copy(out=xnT[:, :, i * 128:(i + 1) * 128], in_=pst.rearrange("p (k t) -> p k t", k=KD)),
                          reads=[PS[b]], writes=[T_xnT])

                wsl = sb1("wsl", [128, KD, 384], BF16)
                T_wsl = Dep()
                qT = sb1("qT", [128, 2, S], BF16)
                kT = sb1("kT", [128, S], BF16)
                T_qT, T_kT = Dep(), Dep()
                sq = [sb1(f"sq{i}", [128, 512], BF16) for i in range(2)]
                T_sq2 = [Dep(), Dep()]
                sd = [sb1(f"sd{i}", [128, 512], F32) for i in range(2)]
                T_sd = [Dep(), Dep()]

                S_.op("pool", lambda e: e.memset(qT[:], 0.0), writes=[T_qT])

                def qk_proj(dst, T_dst, wcol, gain, split=False):
                    def f1(g):
                        b = g % 2
                        pa = psum[b]
                        S_.group("pe", [lambda e, k=k, g=g, pa=pa: e.matmul(pa[:], lhsT=wsl[:, k, wcol:wcol + 128], rhs=xnT[:, k, g * 512:(g + 1) * 512], start=(k == 0), stop=(k == KD - 1)) for k in range(KD)],
                                 reads=[T_wsl, T_xnT], writes=[PS[b]])
                        S_.op("act", lambda e, b=b, pa=pa: e.activation(out=sq[b][:], in_=pa[:], func=AF.Square), reads=[PS[b]], writes=[T_sq2[b]])

                    def f2(g):
                        b = g % 2
                        pa, pb_ = psum[b], psum[2 + b]
                        S_.op("pe", lambda e, b=b, pb_=pb_: e.matmul(pb_[:], lhsT=bo_b[:], rhs=sq[b][:], start=True, stop=True), reads=[T_sq2[b], T_const], writes=[PS[2 + b]])
                        S_.op("act", lambda e, b=b, pb_=pb_: e.activation(out=sd[b][:], in_=pb_[:], func=AF.Sqrt, scale=1.0 / 64, bias=EPS), reads=[PS[2 + b]], writes=[T_sd[b]])
                        S_.op("dve", lambda e, b=b: e.reciprocal(out=sd[b][:], in_=sd[b][:]), reads=[T_sd[b]], writes=[T_sd[b]])
                        if split:
                            for hh_ in range(2):
                                rs_ = slice(hh_ * 64, hh_ * 64 + 64)
                                S_.op("dve", lambda e, b=b, g=g, pa=pa, rs_=rs_, hh_=hh_: e.scalar_tensor_tensor(out=dst[rs_, hh_, g * 512:(g + 1) * 512], in0=pa[rs_, :], scalar=gain[rs_, 0:1], in1=sd[b][rs_, :], op0=ALU.mult, op1=ALU.mult),
                                      reads=[PS[b], T_sd[b], T_const], writes=[T_dst])
                        else:
                            S_.op("dve", lambda e, b=b, g=g, pa=pa: e.scalar_tensor_tensor(out=dst[:, g * 512:(g + 1) * 512], in0=pa[:], scalar=gain[:, 0:1], in1=sd[b][:], op0=ALU.mult, op1=ALU.mult),
                                  reads=[PS[b], T_sd[b], T_const], writes=[T_dst])
                    pipeline(range(NG), f1, f2)

                dils = (1, 4, 16)
                with ExitStack() as s2:
                    def sb2(name, shape, dt):
                        return s2.enter_context(nc.sbuf_tensor(f"{name}_{si}", list(shape), dt))
                    Vp = [sb2(f"Vp{di}", [128, NT, 2, 65], BF16) for di in range(3)]
                    T_Vp = [Dep() for _ in range(3)]
                    acc = sb2("accA", [128, S], F32)
                    T_acc = Dep()
                    pT = [sb2(f"pTa{i}", [128, 2, 128], BF16) for i in range(3)]
                    T_pT = [Dep(), Dep(), Dep()]
                    rden = sb2("rdenA", [64, 512], F32)
                    T_rden = Dep()
                    oTt = [sb2(f"oTtA{i}", [64, 512], BF16) for i in range(2)]
                    T_oTt = [Dep(), Dep()]
                    for di in range(3):
                        S_.op("pool", lambda e, di=di: e.memset(Vp[di][:, :, :, 64:65], 1.0), writes=[T_Vp[di]])
                    for hp in range(4):
                        for j, c0 in enumerate((hp * 128, 512 + hp * 128, 1024 + hp * 128)):
                            S_.dma("sp", wsl[:, :, j * 128:(j + 1) * 128], win_bf[:, :, c0:c0 + 128], reads=[T_win], writes=[T_wsl])
                        qk_proj(qT, T_qT, 0, gq_a, split=True)
                        qk_proj(kT, T_kT, 128, gk_a)
                        cnt = 0
                        for di, d in enumerate(dils):
                            L = S // d
                            ntc = L // 128
                            for r in range(d):
                                for j0 in range(0, ntc, 4):
                                    nj = min(4, ntc - j0)
                                    b = cnt % 2
                                    cnt += 1
                                    pv_ = psum[4 + b]
                                    fns = []
                                    for jj in range(nj):
                                        j = j0 + jj
                                        t0 = r + d * 128 * j
                                        for k in range(KD):
                                            fns.append(lambda e, k=k, jj=jj, t0=t0, d=d, pv_=pv_: e.matmul(
                                                pv_[:, jj * 128:(jj + 1) * 128], lhsT=xnT[:, k, t0:t0 + d * 127 + 1:d], rhs=wsl[:, k, 256:384],
                                                start=(k == 0), stop=(k == KD - 1)))
                                    S_.group("pe", fns, reads=[T_xnT, T_wsl], writes=[PS[4 + b]])
                                    ti = r * ntc + j0
                                    S_.op("act", lambda e, di=di, ti=ti, nj=nj, pv_=pv_: e.copy(
                                        out=Vp[di][:, ti:ti + nj, :, 0:64],
                                        in_=pv_[:, 0:nj * 128].rearrange("p (j h c) -> p j h c", j=nj, h=2)),
                                        reads=[PS[4 + b]], writes=[T_Vp[di]])
                        for hh in range(2):
                            h = hp * 2 + hh
                            ro = slice(hh * 64, hh * 64 + 64)
                            allb = []
                            for di, d in enumerate(dils):
                                L = S // d
                                ntc = L // 128
                                for r in range(d):
                                    blocks = [(0, 64, [(0, 1, 64)])]
                                    for bb in range(ntc - 1):
                                        blocks.append((128 * bb + 64, 128, [(bb, 0, 0), (bb + 1, 1, 0)]))
                                    blocks.append((L - 64, 64, [(ntc - 1, 0, 0)]))
                                    for (qm0, nq, kts) in blocks:
                                        allb.append((len(allb) % 3, di, d, r, ntc, qm0, nq, kts))

                            def fA1(it):
                                b, di, d, r, ntc, qm0, nq, kts = it
                                ps_s = psum[b]
                                q0 = r + d * qm0
                                qsl = slice(q0, q0 + d * (nq - 1) + 1, d)
                                fns = []
                                for ci, (kt, ch, c0) in enumerate(kts):
                                    k0 = r + d * 128 * kt
                                    fns.append(lambda e, ci=ci, k0=k0, d=d, qsl=qsl, nq=nq, ps_s=ps_s: e.matmul(
                                        ps_s[:, ci * 128:ci * 128 + nq], lhsT=kT[:, k0:k0 + d * 127 + 1:d], rhs=qT[:, hh, qsl], start=True, stop=False))
                                    fns.append(lambda e, ci=ci, ch=ch, c0=c0, nq=nq, di=di, ps_s=ps_s: e.matmul(
                                        ps_s[:, ci * 128:ci * 128 + nq], lhsT=anti_b[:], rhs=BMA[:, h, di, ch, c0:c0 + nq], start=False, stop=True))
                                S_.group("pe", fns, reads=[T_qT, T_kT, T_const], writes=[PS[b]])
                                if nq == 128:
                                    S_.op("act", lambda e, b=b, ps_s=ps_s: e.activation(out=pT[b][:].rearrange("p c q -> p (c q)"), in_=ps_s[:, 0:256], func=AF.Exp),
                                          reads=[PS[b]], writes=[T_pT[b]])
                                else:
                                    S_.op("act", lambda e, b=b, ps_s=ps_s: e.activation(out=pT[b][:, 0, 0:64], in_=ps_s[:, 0:64], func=AF.Exp),
                                          reads=[PS[b]], writes=[T_pT[b]])

                            def fA2(it):
                                b, di, d, r, ntc, qm0, nq, kts = it
                                ps_o = psum[3 + b]
                                q0 = r + d * qm0
                                qsl = slice(q0, q0 + d * (nq - 1) + 1, d)
                                nk = len(kts)
                                fns = []
                                for ci, (kt, ch, c0) in enumerate(kts):
                                    ti = r * ntc + kt
                                    fns.append(lambda e, ci=ci, ti=ti, di=di, nq=nq, b=b, nk=nk, ps_o=ps_o: e.matmul(
                                        ps_o[0:65, 0:nq], lhsT=Vp[di][:, ti, hh, :], rhs=pT[b][:, ci, 0:nq], start=(ci == 0), stop=(ci == nk - 1)))
                                S_.group("pe", fns, reads=[T_Vp[di], T_pT[b]], writes=[PS[3 + b]])
                                if di == 0:
                                    S_.op("dve", lambda e, qsl=qsl, nq=nq, ps_o=ps_o: e.tensor_copy(out=acc[0:65, qsl], in_=ps_o[0:65, 0:nq]),
                                          reads=[PS[3 + b]], writes=[T_acc])
                                else:
                                    S_.op("dve", lambda e, qsl=qsl, nq=nq, ps_o=ps_o: e.tensor_tensor(out=acc[0:65, qsl], in0=ps_o[0:65, 0:nq], in1=acc[0:65, qsl], op=ALU.add),
                                          reads=[PS[3 + b], T_acc], writes=[T_acc])
                            pipeline(allb, fA1, fA2, depth=2)
                            for g in range(NG):
                                b = g % 2
                                pd = psum[4 + b]
                                S_.op("pe", lambda e, g=g, pd=pd: e.matmul(pd[0:64, :], lhsT=sel_f[0:65, :], rhs=acc[0:65, g * 512:(g + 1) * 512], start=True, stop=True),
                                      reads=[T_acc, T_const], writes=[PS[4 + b]])
                                S_.op("dve", lambda e, pd=pd: e.reciprocal(out=rden[:], in_=pd[0:64, :]), reads=[PS[4 + b]], writes=[T_rden])
                                S_.op("dve", lambda e, g=g, b=b: e.tensor_tensor(out=oTt[b][:], in0=acc[0:64, g * 512:(g + 1) * 512], in1=rden[:], op=ALU.mult),
                                      reads=[T_acc, T_rden], writes=[T_oTt[b]])
                                S_.dma("pool", oT_scr[si][h * 64:(h + 1) * 64, g * 512:(g + 1) * 512], oTt[b][:], reads=[T_oTt[b]], writes=[T_oTs[si]])

                S_.barrier()
                with ExitStack() as s3:
                    def sb3(name, shape, dt):
                        return s3.enter_context(nc.sbuf_tensor(f"{name}_{si}", list(shape), dt))
                    offs = list(range(-640, 1025, 128))
                    BMB = sb3("BMB", [128, len(offs), 512], BF16)
                    T_BMB = Dep()
                    vB = sb3("vB", [128, NT, 128], BF16)
                    T_vB = Dep()
                    pTb = [sb3(f"pTb{i}", [128, 512], BF16) for i in range(4)]
                    T_pTb = [Dep() for _ in range(4)]
                    fa = [sb3(f"fa{i}", [128, 512], F32) for i in range(4)]
                    T_fa = [Dep() for _ in range(4)]
                    oTb = [sb3(f"oTb{i}", [128, 512], BF16) for i in range(2)]
                    T_oTb = [Dep(), Dep()]
                    for hb in range(4):
                        for j, c0 in enumerate((1536 + hb * 128, 2048 + hb * 128, 2560 + hb * 128)):
                            S_.dma("sp", wsl[:, :, j * 128:(j + 1) * 128], win_bf[:, :, c0:c0 + 128], reads=[T_win], writes=[T_wsl])
                        for oi, off in enumerate(offs):
                            base = RB - off - 127
                            if base < 0 or base + 127 + 511 >= 8192:
                                continue
                            src = bass.AP(GB_h, hb * 8192 + base, [[1, 128], [1, 512]])
                            S_.dma("sp", BMB[:, oi, :], src, reads=[T_GB], writes=[T_BMB])
                        qk_proj(qT, T_qT, 0, gq_b, split=True)
                        qk_proj(kT, T_kT, 128, gk_b)
                        for j0 in range(0, NT, 4):
                            b = (j0 // 4) % 2
                            pv_ = psum[4 + b]
                            fns = []
                            for jj in range(4):
                                j = j0 + jj
                                for k in range(KD):
                                    fns.append(lambda e, k=k, jj=jj, j=j, pv_=pv_: e.matmul(
                                        pv_[:, jj * 128:(jj + 1) * 128], lhsT=xnT[:, k, j * 128:(j + 1) * 128], rhs=wsl[:, k, 256:384],
                                        start=(k == 0), stop=(k == KD - 1)))
                            S_.group("pe", fns, reads=[T_xnT, T_wsl], writes=[PS[4 + b]])
                            S_.op("act", lambda e, j0=j0, pv_=pv_: e.copy(out=vB[:, j0:j0 + 4, :], in_=pv_[:].rearrange("p (j c) -> p j c", j=4)),
                                  reads=[PS[4 + b]], writes=[T_vB])
                        scnt = 0
                        for qg in range(NG):
                            itemsB = []
                            for kc in range(NT):
                                for mp in range(2):
                                    itemsB.append((kc, mp, scnt % 4))
                                    scnt += 1

                            def fB1(it, qg=qg):
                                kc, mp, sl = it
                                off = kc * 128 - qg * 512
                                near = not (off + 127 <= -FAR or off - 511 >= FAR)
                                ps_s = psum[sl]
                                rr = slice(mp * 64, mp * 64 + 64)
                                fns = [lambda e, rr=rr, kc=kc, qg=qg, ps_s=ps_s, near=near: e.matmul(
                                    ps_s[:], lhsT=kT[:, kc * 128:(kc + 1) * 128], rhs=qT[:, mp, qg * 512:(qg + 1) * 512], start=True, stop=not near)]
                                rds = [T_qT, T_kT]
                                if near:
                                    oi = offs.index(off)
                                    fns.append(lambda e, oi=oi, ps_s=ps_s: e.matmul(ps_s[:], lhsT=anti_b[:], rhs=BMB[:, oi, :], start=False, stop=True))
                                    rds += [T_BMB, T_const]
                                S_.group("pe", fns, reads=rds, writes=[PS[sl]])
                                if near:
                                    S_.op("act", lambda e, sl=sl, ps_s=ps_s: e.activation(out=pTb[sl][:], in_=ps_s[:], func=AF.Exp), reads=[PS[sl]], writes=[T_pTb[sl]])
                                else:
                                    sg = 0 if off < 0 else 1
                                    S_.op("act", lambda e, sl=sl, ps_s=ps_s, sg=sg: e.activation(out=pTb[sl][:], in_=ps_s[:], func=AF.Exp, bias=biasfar[:, hb, sg:sg + 1]),
                                          reads=[PS[sl], T_const], writes=[T_pTb[sl]])

                            def fB2(it):
                                kc, mp, sl = it
                                S_.group("pe", [
                                    lambda e, sl=sl, kc=kc, mp=mp: e.matmul(psum[4 + 2 * mp][:], lhsT=vB[:, kc, :], rhs=pTb[sl][:], start=(kc == 0), stop=(kc == NT - 1)),
                                    lambda e, sl=sl, kc=kc, mp=mp: e.matmul(psum[5 + 2 * mp][:], lhsT=ones_b[:], rhs=pTb[sl][:], start=(kc == 0), stop=(kc == NT - 1)),
                                ], reads=[T_vB, T_pTb[sl], T_const], writes=[PS[4 + 2 * mp], PS[5 + 2 * mp]])
                            pipeline(itemsB, fB1, fB2, depth=2)
                            S_.op("dve", lambda e: e.reciprocal(out=fa[0][:], in_=psum[5][:]), reads=[PS[5]], writes=[T_fa[0]])
                            S_.op("dve", lambda e: e.tensor_tensor(out=fa[1][:], in0=psum[4][:], in1=fa[0][:], op=ALU.mult), reads=[PS[4], T_fa[0]], writes=[T_fa[1]])
                            S_.op("dve", lambda e: e.reciprocal(out=fa[0][:], in_=psum[7][:]), reads=[PS[7], T_fa[0]], writes=[T_fa[0]])
                            S_.op("dve", lambda e: e.tensor_tensor(out=fa[2][:], in0=psum[6][:], in1=fa[0][:], op=ALU.mult), reads=[PS[6], T_fa[0]], writes=[T_fa[2]])
                            S_.op("dve", lambda e: e.scalar_tensor_tensor(out=fa[3][:], in0=fa[2][:], scalar=neglam[:, 0:1], in1=fa[1][:], op0=ALU.mult, op1=ALU.add),
                                  reads=[T_fa[1], T_fa[2], T_const], writes=[T_fa[3]])
                            S_.op("act", lambda e: e.activation(out=fa[1][:], in_=fa[3][:], func=AF.Square), reads=[T_fa[3], T_fa[1]], writes=[T_fa[1]])
                            S_.op("pe", lambda e: e.matmul(psum[0][:], lhsT=ones_f[:], rhs=fa[1][:], start=True, stop=True), reads=[T_fa[1], T_const], writes=[PS[0]])
                            S_.op("act", lambda e: e.activation(out=fa[2][:], in_=psum[0][:], func=AF.Sqrt, scale=1.0 / 128, bias=EPS), reads=[PS[0], T_fa[2]], writes=[T_fa[2]])
                            S_.op("dve", lambda e: e.reciprocal(out=fa[2][:], in_=fa[2][:]), reads=[T_fa[2]], writes=[T_fa[2]])
                            ob = qg % 2
                            S_.op("dve", lambda e, ob=ob: e.scalar_tensor_tensor(out=oTb[ob][:], in0=fa[3][:], scalar=gdiff[:, 0:1], in1=fa[2][:], op0=ALU.mult, op1=ALU.mult),
                                  reads=[T_fa[3], T_fa[2], T_const], writes=[T_oTb[ob]])
                            S_.dma("pool", oT_scr[si][512 + hb * 128:512 + (hb + 1) * 128, qg * 512:(qg + 1) * 512], oTb[ob][:], reads=[T_oTb[ob]], writes=[T_oTs[si]])

        S_.barrier()
        stA.close()
        with ExitStack() as s4:
            def sb4(name, shape, dt):
                return s4.enter_context(nc.sbuf_tensor(name, list(shape), dt))
            wout_b = sb4("wout_b", [128, KD, D], BF16)
            wqs = [sb4(f"wqs{i}", [128, KD, 128], BF16) for i in range(2)]
            T_wqs = [Dep(), Dep()]
            T_wqscr = Dep()
            KzT = sb4("KzT", [128, 2, 128], BF16)
            T_w4 = Dep()
            s4t = ExitStack()
            stg = s4t.enter_context(nc.sbuf_tensor("stg4", [128, KD, 512], F32))
            T_stg = Dep()
            wqst = s4t.enter_context(nc.sbuf_tensor("wqst", [128, KD, 512], BF16))
            T_wqst = Dep()
            for cb in range(2):
                S_.dma("sp", stg[:], w_out[:, cb * 512:(cb + 1) * 512].rearrange("(k p) c -> p k c", p=128), writes=[T_stg])
                S_.op("dve", lambda e, cb=cb: e.tensor_copy(out=wout_b[:, :, cb * 512:(cb + 1) * 512], in_=stg[:]), reads=[T_stg], writes=[T_w4])
            for cb in range(4):
                S_.dma("sp", stg[:], w_q[:, cb * 512:(cb + 1) * 512].rearrange("(k p) c -> p k c", p=128), writes=[T_stg])
                for k in range(KD):
                    S_.op("dve", lambda e, k=k, cb=cb: e.tensor_scalar(out=wqst[:, k, :], in0=stg[:, k, :], scalar1=gF[:, k:k + 1], scalar2=None, op0=ALU.mult),
                          reads=[T_stg, T_const], writes=[T_wqst])
                S_.dma("sp", wq_scr[:, :, cb * 512:(cb + 1) * 512], wqst[:], reads=[T_wqst], writes=[T_wqscr])
            for z in range(2):
                S_.dma("sp", stg[:, 0, 0:128], subk[z], writes=[T_stg])
                S_.op("pe", lambda e: e.transpose(out=psum[0][:, 0:128], in_=stg[:, 0, 0:128], identity=ident_f[:]), reads=[T_stg, T_const], writes=[PS[0]])
                S_.op("act", lambda e, z=z: e.copy(out=KzT[:, z, :], in_=psum[0][:, 0:128]), reads=[PS[0]], writes=[T_w4])

            S_.barrier()
            s4t.close()
            TB = 256
            oTl = sb4("oTl", [128, KD, TB], BF16)
            T_oTl = Dep()
            xt4 = sb4("xt4", [128, D], F32)
            T_xt4 = Dep()
            x1 = [[sb4(f"x1_{bf}_{i}", [128, D], F32) for i in range(2)] for bf in range(2)]
            T_x1 = [[Dep(), Dep()] for _ in range(2)]
            st5 = sb4("st5", [128, 8], F32)
            T_st5 = Dep()
            S_.op("pool", lambda e: e.memset(st5[:, 4:5], -0.5), writes=[T_st5])
            xtmp = sb4("xtmp", [128, 512], F32)
            T_xtmp = Dep()
            hnb = sb4("hnb", [128, D], BF16)
            T_hnb = Dep()
            hT = [sb4(f"hT{bf}", [128, KD, TB], BF16) for bf in range(2)]
            T_hT = [Dep(), Dep()]
            qpT = sb4("qpT", [128, 16, TB], BF16)
            T_qpT = Dep()
            sc = [sb4(f"sc{i}", [128, 2048], F32) for i in range(2)]
            T_sc = [[Dep() for _ in range(4)] for _ in range(2)]
            sc2 = sb4("sc2", [128, 128], F32)
            T_sc2 = Dep()
            top = sb4("top", [128, 16, 16], F32)
            idx = sb4("idx", [128, 16, 16], U32)
            idxf = sb4("idxf", [128, 16, 16], F32)
            T_top, T_idx = Dep(), Dep()
            T_cand = Dep()
            cand2 = sb4("cand2", [128, 256], F32)
            T_cand2 = Dep()
            best = sb4("best", [128, 8, 16], F32)
            pos = sb4("pos", [128, 8, 16], U32)
            k12 = sb4("k12", [128, 2, 8, 16], U32)
            k12f = sb4("k12f", [128, 2, 8, 16], F32)
            T_best, T_pos, T_k12 = Dep(), Dep(), Dep()
            T_eq = T_cand
            IJg2 = sb4("IJg", [128, 2, 3, 128], F32)
            T_IJg = Dep()
            gw = sb4("gw", [128, 8, 16], F32)
            zz = sb4("zz", [128, 8], F32)
            T_gw = Dep()
            IJgT = sb4("IJgT", [128, 3, TB], F32)
            T_IJgT = Dep()
            NRING = 12
            AB = sb4("AB", [128, NRING, 2, 128], BF16)
            T_AB = [Dep() for _ in range(NRING)]
            T_BB = [Dep() for _ in range(NRING)]
            Gsb = sb4("Gsb", [128, TB, 128], BF16)
            T_G = Dep()
            NUV = 8
            UTb = [sb4(f"UTb{i}", [128, KD, 128], BF16) for i in range(NUV)]
            Vb = [sb4(f"Vb{i}", [128, D], BF16) for i in range(NUV)]
            T_UTb = [Dep() for _ in range(NUV)]
            T_Vb = [Dep() for _ in range(NUV)]
            asb = [sb4(f"asb{i}", [128, TB], BF16) for i in range(3)]
            wsb = [sb4(f"wsb{i}", [128, TB], BF16) for i in range(3)]
            T_asb = [Dep(), Dep(), Dep()]
            T_wsb = [Dep(), Dep(), Dep()]
            yt = sb4("yt", [128, 512], F32)
            T_yt = Dep()
            iota16 = iota_f[:, 0:16]

            blocks = [(si, blk * TB) for si, S in enumerate(seq_lens) for blk in range(S // TB)]

            def gen_pe(bi, pbks):
                si, t0 = blocks[bi]
                bf = bi % 2
                cnt_ = [0]

                def nb_():
                    cnt_[0] += 1
                    return pbks[cnt_[0] % len(pbks)]
                S_.dma("sp", oTl[:], oT_scr[si][:, t0:t0 + TB].rearrange("(k p) t -> p k t", p=128), reads=[T_oTs[si]], writes=[T_oTl])
                for t2 in range(2):
                    S_.dma("sp", xt4[:], xs[si][t0 + t2 * 128:t0 + (t2 + 1) * 128, :], writes=[T_xt4])
                    for hf in range(2):
                        pbk = nb_()
                        S_.group("pe", [lambda e, k=k, t2=t2, hf=hf, pbk=pbk: e.matmul(psum[pbk][:], lhsT=oTl[:, k, t2 * 128:(t2 + 1) * 128], rhs=wout_b[:, k, hf * 512:(hf + 1) * 512], start=(k == 0), stop=(k == KD - 1)) for k in range(KD)],
                                 reads=[T_oTl, T_w4], writes=[PS[pbk]])
                        S_.op("act", lambda e, pbk=pbk: e.copy(out=xtmp[:], in_=psum[pbk][:]), reads=[PS[pbk]], writes=[T_xtmp])
                        S_.op("pool", lambda e, t2=t2, hf=hf: e.tensor_tensor(out=x1[bf][t2][:, hf * 512:(hf + 1) * 512], in0=xtmp[:], in1=xt4[:, hf * 512:(hf + 1) * 512], op=ALU.add),
                              reads=[T_xtmp, T_xt4], writes=[T_x1[bf][t2]])
                        yield
                    if debug:
                        S_.dma("pool", dbg[f"dbg_x1_{si}"][t0 + t2 * 128:t0 + (t2 + 1) * 128, :], x1[bf][t2][:], reads=[T_x1[bf][t2]], writes=[Dep()])
                    S_.op("act", lambda e, t2=t2: e.activation(out=hnb[:], in_=x1[bf][t2][:], func=AF.Square, accum_out=st5[:, 0:1]), reads=[T_x1[bf][t2]], writes=[T_hnb, T_st5])
                    S_.op("pool", lambda e: e.tensor_scalar(out=st5[:, 1:2], in0=st5[:, 0:1], scalar1=1.0 / D, scalar2=EPS, op0=ALU.mult, op1=ALU.add), reads=[T_st5], writes=[T_st5])
                    S_.op("pool", lambda e: e.tensor_tensor(out=st5[:, 2:3], in0=st5[:, 1:2], in1=st5[:, 4:5], op=ALU.pow), reads=[T_st5], writes=[T_st5])
                    S_.op("act", lambda e, t2=t2: e.activation(out=hnb[:], in_=x1[bf][t2][:], func=AF.Copy, scale=st5[:, 2:3]), reads=[T_x1[bf][t2], T_st5], writes=[T_hnb])
                    yield
                    pbk = nb_()
                    pst = psum[pbk][:].bitcast(BF16)
                    S_.group("pe", [lambda e, k=k, pst=pst: e.transpose(out=pst[:, k * 128:(k + 1) * 128], in_=hnb[:, k * 128:(k + 1) * 128], identity=ident_b[:]) for k in range(KD)],
                             reads=[T_hnb, T_const], writes=[PS[pbk]])
                    S_.op("act", lambda e, t2=t2, pst=pst: e.copy(out=hT[bf][:, :, t2 * 128:(t2 + 1) * 128], in_=pst.rearrange("p (k t) -> p k t", k=KD)), reads=[PS[pbk]], writes=[T_hT[bf]])
                    yield
                for pz in range(16):
                    b = pz % 2
                    pbk = nb_()
                    S_.dma("sp", wqs[b][:], wq_scr[:, :, pz * 128:(pz + 1) * 128], reads=[T_wqscr], writes=[T_wqs[b]])
                    S_.group("pe", [lambda e, k=k, pz=pz, b=b, pbk=pbk: e.matmul(psum[pbk][:, 0:TB], lhsT=wqs[b][:, k, :], rhs=hT[bf][:, k, :], start=(k == 0), stop=(k == KD - 1)) for k in range(KD)],
                             reads=[T_hT[bf], T_wqs[b]], writes=[PS[pbk]])
                    S_.op("act", lambda e, pz=pz, b=b, pbk=pbk: e.copy(out=qpT[:, pz, :], in_=psum[pbk][:, 0:TB]), reads=[PS[pbk]], writes=[T_qpT])
                    yield
                for t2 in range(2):
                    for bk in range(4):
                        pbk = nb_()
                        S_.group("pe", [lambda e, pz=pz, t2=t2, pbk=pbk: e.matmul(psum[pbk][:, (pz % 4) * 128:(pz % 4 + 1) * 128], lhsT=qpT[:, pz, t2 * 128:(t2 + 1) * 128], rhs=KzT[:, pz % 2, :], start=True, stop=True) for pz in range(bk * 4, bk * 4 + 4)],
                                 reads=[T_qpT, T_w4], writes=[PS[pbk]])
                        S_.op("act", lambda e, t2=t2, bk=bk, pbk=pbk: e.copy(out=sc[t2][:, bk * 512:(bk + 1) * 512], in_=psum[pbk][:]), reads=[PS[pbk]], writes=[T_sc[t2][bk], T_cand])
                        yield

            def gen_dve(bi):
                si, t0 = blocks[bi]
                bf = bi % 2
                for t2 in range(2):
                    for bk in range(4):
                        for g in range(bk * 4, bk * 4 + 4):
                            gs = slice(g * 128, (g + 1) * 128)
                            S_.op("dve", lambda e, g=g, gs=gs, t2=t2: e.max(out=top[:, g, 0:8], in_=sc[t2][:, gs]), reads=[T_sc[t2][bk]], writes=[T_top])
                            S_.op("dve", lambda e, g=g, gs=gs, t2=t2: e.max_index(out=idx[:, g, 0:8], in_max=top[:, g, 0:8], in_values=sc[t2][:, gs]), reads=[T_sc[t2][bk], T_top], writes=[T_idx])
                            S_.op("dve", lambda e, g=g, gs=gs, t2=t2: e.match_replace(out=sc2[:], in_to_replace=top[:, g, 0:8], in_values=sc[t2][:, gs], imm_value=-1e30), reads=[T_sc[t2][bk], T_top], writes=[T_sc2])
                            yield
                            S_.op("dve", lambda e, g=g: e.max(out=top[:, g, 8:16], in_=sc2[:]), reads=[T_sc2], writes=[T_top])
                            S_.op("dve", lambda e, g=g: e.max_index(out=idx[:, g, 8:16], in_max=top[:, g, 8:16], in_values=sc2[:]), reads=[T_sc2, T_top], writes=[T_idx])
                            yield
                    S_.op("dve", lambda e: e.tensor_copy(out=idxf[:], in_=idx[:]), reads=[T_idx], writes=[T_idx])
                    t4v = top[:].rearrange("p (h z) k -> p h z k", z=2)
                    i4v = idxf[:].rearrange("p (h z) k -> p h z k", z=2)
                    cand = sc[t2][:].rearrange("p (h c) -> p h c", h=8)
                    eq = sc[t2][:].rearrange("p (h s k) -> p h s k", h=8, s=16)
                    c4 = sc[t2][:].rearrange("p (h a b) -> p h a b", h=8, a=16)
                    S_.op("dve", lambda e: e.tensor_tensor(out=c4, in0=t4v[:, :, 0, :].unsqueeze(3).to_broadcast([128, 8, 16, 16]), in1=t4v[:, :, 1, :].unsqueeze(2).to_broadcast([128, 8, 16, 16]), op=ALU.add),
                          reads=[T_top], writes=[T_cand] + T_sc[t2])
                    yield
                    for p in range(8):
                        S_.op("dve", lambda e, p=p: e.max(out=best[:, p, 0:8], in_=cand[:, p, :]), reads=[T_cand], writes=[T_best])
                        S_.op("dve", lambda e, p=p: e.max_index(out=pos[:, p, 0:8], in_max=best[:, p, 0:8], in_values=cand[:, p, :]), reads=[T_cand, T_best], writes=[T_pos])
                        S_.op("dve", lambda e, p=p: e.match_replace(out=cand2[:], in_to_replace=best[:, p, 0:8], in_values=cand[:, p, :], imm_value=-1e30), reads=[T_cand, T_best], writes=[T_cand2])
                        yield
                        S_.op("dve", lambda e, p=p: e.max(out=best[:, p, 8:16], in_=cand2[:]), reads=[T_cand2], writes=[T_best])
                        S_.op("dve", lambda e, p=p: e.max_index(out=pos[:, p, 8:16], in_max=best[:, p, 8:16], in_values=cand2[:]), reads=[T_cand2, T_best], writes=[T_pos])
                        yield
                    S_.op("dve", lambda e: e.tensor_tensor(out=gw[:], in0=best[:], in1=best[:, :, 0:1].to_broadcast([128, 8, 16]), op=ALU.subtract), reads=[T_best], writes=[T_gw])
                    S_.op("act", lambda e: e.activation(out=gw[:], in_=gw[:], func=AF.Exp), reads=[T_gw], writes=[T_gw])
                    S_.op("dve", lambda e: e.reduce_sum(out=zz[:], in_=gw[:], axis=AX.X), reads=[T_gw], writes=[T_gw])
                    S_.op("dve", lambda e: e.reciprocal(out=zz[:], in_=zz[:]), reads=[T_gw], writes=[T_gw])
                    S_.op("dve", lambda e, t2=t2: e.tensor_tensor(out=IJg2[:, t2, 2, :].rearrange("p (h s) -> p h s", h=8), in0=gw[:], in1=zz[:].unsqueeze(2).to_broadcast([128, 8, 16]), op=ALU.mult),
                          reads=[T_gw], writes=[T_IJg])
                    yield
                    S_.op("dve", lambda e: e.tensor_single_scalar(out=k12[:, 0], in_=pos[:], scalar=4, op=ALU.logical_shift_right), reads=[T_pos], writes=[T_k12])
                    S_.op("dve", lambda e: e.tensor_single_scalar(out=k12[:, 1], in_=pos[:], scalar=15, op=ALU.bitwise_and), reads=[T_pos, T_k12], writes=[T_k12])
                    S_.op("dve", lambda e: e.tensor_copy(out=k12f[:], in_=k12[:]), reads=[T_k12], writes=[T_k12])
                    yield
                    for z in range(2):
                        S_.op("dve", lambda e, z=z: e.tensor_tensor(out=eq[:], in0=k12f[:, z].unsqueeze(3).to_broadcast([128, 8, 16, 16]),
                                                                      in1=iota16.unsqueeze(1).unsqueeze(1).to_broadcast([128, 8, 16, 16]), op=ALU.is_equal),
                              reads=[T_k12, T_const, T_eq], writes=[T_eq])
                        yield
                        S_.op("dve", lambda e, z=z: e.tensor_tensor(out=eq[:], in0=eq[:], in1=i4v[:, :, z, :].unsqueeze(2).to_broadcast([128, 8, 16, 16]), op=ALU.mult),
                              reads=[T_eq, T_idx], writes=[T_eq])
                        yield
                        S_.op("dve", lambda e, z=z, t2=t2: e.reduce_sum(out=IJg2[:, t2, z, :], in_=eq[:].rearrange("p h s k -> p (h s) k"), axis=AX.X), reads=[T_eq], writes=[T_IJg])
                        yield

            def emit_B6(gpe=None):
                for t2 in range(2):
                    S_.group("pe", [lambda e, j=j, t2=t2: e.transpose(out=psum[5][:, j * 128:(j + 1) * 128], in_=IJg2[:, t2, j, :], identity=ident_f[:]) for j in range(3)],
                             reads=[T_IJg, T_const], writes=[PS[5]])
                    S_.op("act", lambda e, t2=t2: e.copy(out=IJgT[:, :, t2 * 128:(t2 + 1) * 128], in_=psum[5][:, 0:384].rearrange("p (j t) -> p j t", j=3)), reads=[PS[5]], writes=[T_IJgT])
                for tq in range(TB // 4):
                    gb = 5 + (tq % 2)
                    fns = []
                    rds = []
                    for tt in range(4):
                        t = tq * 4 + tt
                        rs = t % NRING
                        S_.op("dve", lambda e, t=t, rs=rs: e.tensor_scalar(out=AB[:, rs, 0, :], in0=iota_f[:], scalar1=IJgT[:, 0, t:t + 1], scalar2=IJgT[:, 2, t:t + 1], op0=ALU.is_equal, op1=ALU.mult),
                              reads=[T_IJgT, T_const], writes=[T_AB[rs]])
                        S_.op("dve", lambda e, t=t, rs=rs: e.tensor_scalar(out=AB[:, rs, 1, :], in0=iota_f[:], scalar1=IJgT[:, 1, t:t + 1], scalar2=None, op0=ALU.is_equal),
                              reads=[T_IJgT, T_const], writes=[T_BB[rs]])
                        fns.append(lambda e, tt=tt, rs=rs, gb=gb: e.matmul(psum[gb][:, tt * 128:(tt + 1) * 128], lhsT=AB[:, rs, 1, :], rhs=AB[:, rs, 0, :], start=True, stop=True))
                        rds += [T_AB[rs], T_BB[rs]]
                    S_.group("pe", fns, reads=rds, writes=[PS[gb]])
                    S_.op("act", lambda e, tq=tq, gb=gb: e.copy(out=Gsb[:, tq * 4:tq * 4 + 4, :].rearrange("j t i -> j (t i)"), in_=psum[gb][:]),
                          reads=[PS[gb]], writes=[T_G])
                    if gpe is not None:
                        next(gpe, None)
                if gpe is not None:
                    for _ in gpe:
                        pass

            def emit_B7(bi, filler):
                bf = bi % 2

                def f71(i):
                    u = i % NUV
                    a = i % 3
                    S_.dma("sp", UTb[u][:], UT_scr[i], reads=[T_UV], writes=[T_UTb[u]])
                    S_.dma("sp", Vb[u][:], V_scr[i * 128:(i + 1) * 128, :], reads=[T_UV], writes=[T_Vb[u]])
                    pa_ = psum[(4, 6, 7)[a]][:, 0:TB]
                    S_.group("pe", [lambda e, k=k, u=u, pa_=pa_: e.matmul(pa_, lhsT=UTb[u][:, k, :], rhs=hT[bf][:, k, :], start=(k == 0), stop=(k == KD - 1)) for k in range(KD)],
                             reads=[T_UTb[u], T_hT[bf]], writes=[PSA[a]])
                    S_.op("act", lambda e, a=a, pa_=pa_: e.activation(out=asb[a][:], in_=pa_, func=AF.Gelu), reads=[PSA[a]], writes=[T_asb[a]])
                    S_.op("pool", lambda e, a=a, i=i: e.tensor_tensor(out=wsb[a][:], in0=asb[a][:], in1=Gsb[:, :, i], op=ALU.mult), reads=[T_asb[a], T_G], writes=[T_wsb[a]])

                def f72(i):
                    u = i % NUV
                    a = i % 3
                    S_.group("pe", [lambda e, t2=t2, hf=hf, a=a, u=u, i=i: e.matmul(psum[t2 * 2 + hf][:], lhsT=wsb[a][:, t2 * 128:(t2 + 1) * 128], rhs=Vb[u][:, hf * 512:(hf + 1) * 512], start=(i == 0), stop=(i == n_exp_chunks - 1)) for t2 in range(2) for hf in range(2)],
                             reads=[T_wsb[a], T_Vb[u]], writes=[PS[0], PS[1], PS[2], PS[3]])
                    if filler is not None:
                        next(filler, None)
                pipeline(range(n_exp_chunks), f71, f72, depth=2)
                if filler is not None:
                    for _ in filler:
                        pass

            def emit_B8(bi):
                si, t0 = blocks[bi]
                bf = bi % 2
                for t2 in range(2):
                    for hf in range(2):
                        S_.op("dve", lambda e, t2=t2, hf=hf: e.tensor_tensor(out=yt[:], in0=psum[t2 * 2 + hf][:], in1=x1[bf][t2][:, hf * 512:(hf + 1) * 512], op=ALU.add),
                              reads=[PS[t2 * 2 + hf], T_x1[bf][t2]], writes=[T_yt])
                        S_.dma("pool", ys[si][t0 + t2 * 128:t0 + (t2 + 1) * 128, hf * 512:(hf + 1) * 512], yt[:], reads=[T_yt], writes=[T_y])

            for _ in gen_pe(0, (5, 6)):
                pass
            for _ in gen_dve(0):
                pass
            for bi in range(len(blocks)):
                emit_B6(gen_pe(bi + 1, (7, 4)) if bi + 1 < len(blocks) else None)
                filler = gen_dve(bi + 1) if bi + 1 < len(blocks) else None
                emit_B7(bi, filler)
                emit_B8(bi)
        S_.finish()
    print("instructions:", S_.ninst)
    return nc


_CONSTS = None


def _in_maps(inputs, seq_lens_per_core, core_seqs):
    global _CONSTS
    if _CONSTS is None:
        _CONSTS = host_consts()
    c = _CONSTS
    base = {
        "rel_bias": inputs["rel_bias"], "attn_norm_g": inputs["attn_norm_g"][0], "w_in": inputs["w_in"][0],
        "q_norm_a": inputs["q_norm_a"][0], "k_norm_a": inputs["k_norm_a"][0], "q_norm_b": inputs["q_norm_b"][0],
        "k_norm_b": inputs["k_norm_b"][0], "lambda_q1": inputs["lambda_q1"][0], "lambda_k1": inputs["lambda_k1"][0],
        "lambda_q2": inputs["lambda_q2"][0], "lambda_k2": inputs["lambda_k2"][0], "diff_norm_g": inputs["diff_norm_g"][0],
        "w_out": inputs["w_out"][0], "ffn_norm_g": inputs["ffn_norm_g"][0], "peer_w_q": inputs["peer_w_q"][0],
        "peer_sub_keys": inputs["peer_sub_keys"][0], "peer_u": inputs["peer_u"][0], "peer_v": inputs["peer_v"][0],
        "c_ident": c["ident"], "c_antiI": c["antiI"], "c_blockones": c["blockones"], "c_iota": c["iota"],
        "c_ohb": c["ohb"], "c_oha": c["oha"], "c_sel65": c["sel65"],
    }
    base = {k: np.ascontiguousarray(np.asarray(v, dtype=np.float32)) for k, v in base.items()}
    maps = []
    for seqs in core_seqs:
        m = dict(base)
        for i, xarr in enumerate(seqs):
            m[f"x{i}"] = np.ascontiguousarray(np.asarray(xarr, dtype=np.float32))
        maps.append(m)
    return maps


def kernel(**inputs):
    inputs = {k: np.asarray(v) for k, v in inputs.items()}
    xp = inputs["x_prompt"]
    xsmp = inputs["x_sample"]
    n = 8
    seq_lens = [xp.shape[1], xsmp.shape[1], xsmp.shape[1]]
    core_seqs = [[xp[c], xsmp[2 * c], xsmp[2 * c + 1]] for c in range(n)]
    nc = build(seq_lens)
    maps = _in_maps(inputs, seq_lens, core_seqs)
    res = run_bass_kernel_spmd(nc, maps, core_ids=list(range(n)))
    yp = np.stack([res.results[c]["y0"] for c in range(n)], axis=0).astype(np.float32)
    ysm = np.empty(xsmp.shape, np.float32)
    for c in range(n):
        ysm[2 * c] = res.results[c]["y1"]
        ysm[2 * c + 1] = res.results[c]["y2"]
    return (yp, ysm)
```

```python
import math
from contextlib import ExitStack
import numpy as np
import concourse.bass as bass
import concourse.mybir as mybir
from concourse.bass_utils import run_bass_kernel_spmd

F32 = mybir.dt.float32
BF16 = mybir.dt.bfloat16
U32 = mybir.dt.uint32
AF = mybir.ActivationFunctionType
ALU = mybir.AluOpType
AX = mybir.AxisListType

D = 1024
KD = 8
EPS = 1e-6
NEG = -30000.0
NDS = 56


class Dep:
    __slots__ = ("w", "r")

    def __init__(self):
        self.w = None
        self.r = {}


class Eng:
    def __init__(self, name, eng, sem):
        self.name = name
        self.eng = eng
        self.sem = sem
        self.count = 0
        self.seen = {}


class Sched:
    def __init__(self, nc, stack):
        self.nc = nc
        self.E = {}
        for name, eng in [("pe", nc.tensor), ("act", nc.scalar), ("dve", nc.vector),
                          ("pool", nc.gpsimd), ("sp", nc.sync)]:
            sem = stack.enter_context(nc.semaphore(f"s_{name}"))
            self.E[name] = Eng(name, eng, sem)
        self.dsems = []
        self.dpool = {"sp": [], "pool": []}
        for i in range(NDS):
            sem = stack.enter_context(nc.semaphore(f"dq{i}"))
            self.dsems.append([sem, 0])
            self.dpool["pool" if i >= NDS - 16 else "sp"].append(i)
        self.dnext = {"sp": 0, "pool": 0}
        self.ninst = 0

    def _wait(self, e, deps):
        best = {}
        for (key, sem, val) in deps:
            if key == "pe" and e.name == "pe":
                continue
            if val > best.get(key, (None, 0))[1]:
                best[key] = (sem, val)
        for key, (sem, val) in best.items():
            if e.seen.get(key, 0) < val:
                e.eng.wait_ge(sem, val)
                e.seen[key] = val

    @staticmethod
    def _deps(reads, writes):
        d = []
        for t in reads:
            if t.w is not None:
                d.append(t.w)
        for t in writes:
            if t.w is not None:
                d.append(t.w)
            d.extend(t.r.values())
        return d

    @staticmethod
    def _mark(tok, reads, writes):
        for t in writes:
            t.w = tok
            t.r = {}
        for t in reads:
            t.r[tok[0]] = tok

    def op(self, en, fn, reads=(), writes=()):
        e = self.E[en]
        self._wait(e, self._deps(reads, writes))
        ins = fn(e.eng)
        e.count += 1
        ins.then_inc(e.sem, 1)
        self._mark((en, e.sem, e.count), reads, writes)
        self.ninst += 1

    def group(self, en, fns, reads=(), writes=()):
        e = self.E[en]
        self._wait(e, self._deps(reads, writes))
        ins = None
        for fn in fns:
            ins = fn(e.eng)
            self.ninst += 1
        e.count += 1
        ins.then_inc(e.sem, 1)
        self._mark((en, e.sem, e.count), reads, writes)

    def dma(self, qn, out, in_, reads=(), writes=(), **kw):
        e = self.E[qn]
        pl = self.dpool[qn]
        idx = pl[self.dnext[qn]]
        self.dnext[qn] = (self.dnext[qn] + 1) % len(pl)
        slot = self.dsems[idx]
        deps = self._deps(reads, writes)
        key = ("d", idx)
        if slot[1] > 0:
            deps.append((key, slot[0], slot[1]))
        self._wait(e, deps)
        ins = e.eng.dma_start(out=out, in_=in_, **kw)
        slot[1] += 16
        ins.then_inc(slot[0], 16)
        self._mark((key, slot[0], slot[1]), reads, writes)
        self.ninst += 1

    def barrier(self):
        for e in self.E.values():
            for o in self.E.values():
                if o.count == 0:
                    continue
                if e.seen.get(o.name, 0) < o.count:
                    e.eng.wait_ge(o.sem, o.count)
                    e.seen[o.name] = o.count
            for idx, slot in enumerate(self.dsems):
                key = ("d", idx)
                if slot[1] > 0 and e.seen.get(key, 0) < slot[1]:
                    e.eng.wait_ge(slot[0], slot[1])
                    e.seen[key] = slot[1]

    def finish(self):
        e = self.E["sp"]
        for idx, slot in enumerate(self.dsems):
            if slot[1] > 0 and e.seen.get(("d", idx), 0) < slot[1]:
                e.eng.wait_ge(slot[0], slot[1])
        for name in ("pe", "act", "dve", "pool"):
            o = self.E[name]
            if o.count > 0:
                e.eng.wait_ge(o.sem, o.count)


def pipeline(items, first, second, depth=1):
    items = list(items)
    n = len(items)
    for i in range(n + depth):
        if i < n:
            first(items[i])
        if i - depth >= 0:
            second(items[i - depth])


def rel_bucket_np(rel):
    nb = 16
    max_exact = 8
    n = np.abs(rel)
    with np.errstate(divide="ignore"):
        large = max_exact + (np.log(np.maximum(n, 1).astype(np.float32) / np.float32(max_exact))
                             / np.float32(math.log(1024 / max_exact)) * (nb - max_exact)).astype(np.int32)
    large = np.minimum(large, nb - 1)
    return (rel > 0).astype(np.int32) * nb + np.where(n < max_exact, n, large)


RB = 4095
FAR = 559


def host_consts():
    c = {}
    c["ident"] = np.eye(128, dtype=np.float32)
    c["antiI"] = np.ascontiguousarray(np.eye(128, dtype=np.float32)[::-1])
    bo = np.zeros((128, 128), np.float32)
    bo[:64, :64] = 1
    bo[64:, 64:] = 1
    c["blockones"] = bo
    c["iota"] = np.tile(np.arange(128, dtype=np.float32)[None, :], (128, 1))
    i = np.arange(8192)
    ohb = np.zeros((32, 8192), np.float32)
    ohb[rel_bucket_np(RB - i), i] = 1.0
    c["ohb"] = ohb
    oha = np.zeros((3, 33, 384), np.float32)
    for di, d in enumerate((1, 4, 16)):
        for ii in range(384):
            rm = 191 - ii
            if abs(rm) <= 64:
                oha[di, rel_bucket_np(np.array(rm * d)), ii] = 1.0
            else:
                oha[di, 32, ii] = 1.0
    c["oha"] = oha
    sel = np.zeros((128, 64), np.float32)
    sel[64, :] = 1.0
    c["sel65"] = sel
    return c


def build(seq_lens, n_exp_chunks=128, debug=False):
    nc = bass.Bass("TRN2", target_bir_lowering=False)
    NSEQ = len(seq_lens)
    TTOT = sum(seq_lens)
    SMAX = max(seq_lens)

    def din(name, shape, dt=F32):
        return nc.dram_tensor(name, list(shape), dt, kind="ExternalInput").ap()

    xs = [din(f"x{i}", [S, D]) for i, S in enumerate(seq_lens)]
    ys = [nc.dram_tensor(f"y{i}", [S, D], F32, kind="ExternalOutput").ap() for i, S in enumerate(seq_lens)]
    rel_bias = din("rel_bias", [32, 12])
    attn_g = din("attn_norm_g", [D])
    w_in = din("w_in", [D, 3072])
    qn_a = din("q_norm_a", [64])
    kn_a = din("k_norm_a", [64])
    qn_b = din("q_norm_b", [64])
    kn_b = din("k_norm_b", [64])
    lq1 = din("lambda_q1", [64])
    lk1 = din("lambda_k1", [64])
    lq2 = din("lambda_q2", [64])
    lk2 = din("lambda_k2", [64])
    dn_g = din("diff_norm_g", [128])
    w_out = din("w_out", [D, D])
    ffn_g = din("ffn_norm_g", [D])
    w_q = din("peer_w_q", [D, 2048])
    subk = din("peer_sub_keys", [2, 128, 128])
    pu = din("peer_u", [16384, D])
    pv = din("peer_v", [16384, D])
    c_ident = din("c_ident", [128, 128])
    c_anti = din("c_antiI", [128, 128])
    c_bo = din("c_blockones", [128, 128])
    c_iota = din("c_iota", [128, 128])
    c_ohb = din("c_ohb", [32, 8192])
    c_oha = din("c_oha", [3, 33, 384])
    c_sel = din("c_sel65", [128, 64])

    def dscr(name, shape, dt):
        return nc.dram_tensor(name, list(shape), dt, kind="Internal")

    win_bf = dscr("win_bf", [128, KD, 3072], BF16).ap()
    GB_h = dscr("GBseq", [4, 8192], BF16)
    GA_h = dscr("GAseq", [8, 3, 384], BF16)
    oT_scr = [dscr(f"oT{i}", [D, S], BF16).ap() for i, S in enumerate(seq_lens)]
    wq_scr = dscr("wq_bf", [128, KD, 2048], BF16).ap()
    UT_scr = dscr("UTs", [128, 128, KD, 128], BF16).ap()
    V_scr = dscr("Vs", [16384, D], BF16).ap()
    dbg = {}
    if debug:
        for i, S in enumerate(seq_lens):
            dbg[f"dbg_x1_{i}"] = nc.dram_tensor(f"dbg_x1_{i}", [S, D], F32, kind="ExternalOutput").ap()

    with ExitStack() as st:
        S_ = Sched(nc, st)

        def sb(name, shape, dt):
            return st.enter_context(nc.sbuf_tensor(name, list(shape), dt))

        ident_f = sb("ident_f", [128, 128], F32)
        ident_b = sb("ident_b", [128, 128], BF16)
        anti_b = sb("anti_b", [128, 128], BF16)
        bo_b = sb("bo_b", [128, 128], BF16)
        ones_b = sb("ones_b", [128, 128], BF16)
        ones_f = sb("ones_f", [128, 128], F32)
        iota_f = sb("iota_f", [128, 128], F32)
        sel_f = sb("sel_f", [128, 64], F32)
        gq_a = sb("gq_a", [128, 1], F32)
        gk_a = sb("gk_a", [128, 1], F32)
        gq_b = sb("gq_b", [128, 1], F32)
        gk_b = sb("gk_b", [128, 1], F32)
        gdiff = sb("gdiff", [128, 1], F32)
        neglam = sb("neglam", [128, 1], F32)
        biasfar = sb("biasfar", [128, 4, 2], F32)
        gA = sb("gA", [128, KD], F32)
        gF = sb("gF", [128, KD], F32)
        T_const = Dep()
        psum = [st.enter_context(nc.psum_tensor(f"ps{i}", [128, 512], F32)) for i in range(8)]
        PS = [Dep() for _ in range(8)]
        PSA = [PS[4], PS[6], PS[7]]
        T_oTs = [Dep() for _ in seq_lens]
        T_y = Dep()
        stA = ExitStack()
        BMA = stA.enter_context(nc.sbuf_tensor('BMA', [128, 8, 3, 2, 128], BF16))

        with ExitStack() as s0:
            def sb0(name, shape, dt):
                return s0.enter_context(nc.sbuf_tensor(name, list(shape), dt))
            stage = sb0("stage0", [128, 4096], F32)
            T_stage = Dep()
            tmp = sb0("tmp0", [128, 512], F32)
            T_tmp = Dep()
            S_.dma("sp", ident_f[:], c_ident, writes=[T_const])
            S_.dma("sp", iota_f[:], c_iota, writes=[T_const])
            S_.dma("sp", sel_f[:], c_sel, writes=[T_const])
            S_.dma("sp", stage[:, 0:128], c_anti, writes=[T_stage])
            S_.dma("sp", stage[:, 128:256], c_bo, writes=[T_stage])
            S_.op("dve", lambda e: e.tensor_copy(out=ident_b[:], in_=ident_f[:]), reads=[T_const], writes=[T_const])
            S_.op("dve", lambda e: e.tensor_copy(out=anti_b[:], in_=stage[:, 0:128]), reads=[T_stage], writes=[T_const])
            S_.op("dve", lambda e: e.tensor_copy(out=bo_b[:], in_=stage[:, 128:256]), reads=[T_stage], writes=[T_const])
            S_.op("dve", lambda e: e.memset(ones_b[:], 1.0), writes=[T_const])
            S_.op("dve", lambda e: e.memset(ones_f[:], 1.0), writes=[T_const])
            T_g = Dep()
            graw = sb0("graw", [128, 8], F32)
            for j, src in enumerate((qn_a, kn_a, qn_b, kn_b)):
                for hh in range(2):
                    S_.dma("sp", graw[hh * 64:(hh + 1) * 64, j:j + 1], src.rearrange("(c o) -> c o", o=1), writes=[T_g])
            S_.dma("sp", graw[:, 4:5], dn_g.rearrange("(c o) -> c o", o=1), writes=[T_g])
            S_.op("dve", lambda e: e.tensor_scalar_mul(out=gq_a[:], in0=graw[:, 0:1], scalar1=0.125), reads=[T_g], writes=[T_const])
            S_.op("dve", lambda e: e.tensor_copy(out=gk_a[:], in_=graw[:, 1:2]), reads=[T_g], writes=[T_const])
            S_.op("dve", lambda e: e.tensor_scalar_mul(out=gq_b[:], in0=graw[:, 2:3], scalar1=0.125), reads=[T_g], writes=[T_const])
            S_.op("dve", lambda e: e.tensor_copy(out=gk_b[:], in_=graw[:, 3:4]), reads=[T_g], writes=[T_const])
            S_.op("dve", lambda e: e.tensor_scalar_mul(out=gdiff[:], in0=graw[:, 4:5], scalar1=0.8), reads=[T_g], writes=[T_const])
            S_.dma("sp", gA[:], attn_g.rearrange("(k p) -> p k", p=128), writes=[T_const], allow_slow_non_contiguous=True)
            S_.dma("sp", gF[:], ffn_g.rearrange("(k p) -> p k", p=128), writes=[T_const], allow_slow_non_contiguous=True)
            lam4 = sb0("lam4", [128, 4, 64], F32)
            T_l = Dep()
            for j, src in enumerate((lq1, lk1, lq2, lk2)):
                S_.dma("sp", lam4[:, j, :], src.partition_broadcast(128), writes=[T_l])
            lamw = sb0("lamw", [128, 8], F32)
            T_lw = Dep()
            prod = sb0("lprod", [128, 2, 64], F32)
            S_.op("dve", lambda e: e.tensor_tensor(out=prod[:, 0, :], in0=lam4[:, 0, :], in1=lam4[:, 1, :], op=ALU.mult), reads=[T_l], writes=[T_lw])
            S_.op("dve", lambda e: e.tensor_tensor(out=prod[:, 1, :], in0=lam4[:, 2, :], in1=lam4[:, 3, :], op=ALU.mult), reads=[T_l, T_lw], writes=[T_lw])
            S_.op("dve", lambda e: e.reduce_sum(out=lamw[:, 0:2], in_=prod[:], axis=AX.X), reads=[T_lw], writes=[T_lw])
            S_.op("act", lambda e: e.activation(out=lamw[:, 2:4], in_=lamw[:, 0:2], func=AF.Exp), reads=[T_lw], writes=[T_lw])
            S_.op("dve", lambda e: e.tensor_tensor(out=lamw[:, 4:5], in0=lamw[:, 3:4], in1=lamw[:, 2:3], op=ALU.subtract), reads=[T_lw], writes=[T_lw])
            S_.op("dve", lambda e: e.tensor_scalar_add(out=neglam[:], in0=lamw[:, 4:5], scalar1=-0.2), reads=[T_lw], writes=[T_const])
            for hb in range(4):
                for sg, row in enumerate((15, 31)):
                    S_.dma("sp", biasfar[:, hb, sg:sg + 1],
                           rel_bias[row:row + 1, 8 + hb:9 + hb].rearrange("a b -> (a b)").partition_broadcast(128), writes=[T_const])
            tabA = sb0("tabA", [33, 12], F32)
            T_tab = Dep()
            S_.op("dve", lambda e: e.memset(tabA[:], NEG), writes=[T_tab])
            S_.dma("sp", tabA[0:32, :], rel_bias, reads=[], writes=[T_tab])
            ohb_sb = sb0("ohb_sb", [32, 8192], F32)
            oha_sb = sb0("oha_sb", [33, 3, 384], F32)
            T_oh = Dep()
            S_.dma("sp", ohb_sb[:], c_ohb, writes=[T_oh])
            S_.dma("sp", oha_sb[:], c_oha.rearrange("d b i -> b d i"), writes=[T_oh])
            seqb = sb0("seqb", [8, 8192], BF16)
            T_sq = Dep()
            for g in range(16):
                S_.op("pe", lambda e, g=g: e.matmul(psum[g % 2][0:4, :], lhsT=tabA[0:32, 8:12], rhs=ohb_sb[:, g * 512:(g + 1) * 512], start=True, stop=True),
                      reads=[T_tab, T_oh], writes=[PS[g % 2]])
                S_.op("act", lambda e, g=g: e.copy(out=seqb[0:4, g * 512:(g + 1) * 512], in_=psum[g % 2][0:4, :]), reads=[PS[g % 2]], writes=[T_sq])
            T_GB = Dep()
            S_.dma("sp", GB_h.ap(), seqb[0:4, :], reads=[T_sq], writes=[T_GB])
            seqa = sb0("seqa", [8, 3, 384], BF16)
            T_sa = Dep()
            for di in range(3):
                S_.op("pe", lambda e, di=di: e.matmul(psum[2][0:8, 0:384], lhsT=tabA[0:33, 0:8], rhs=oha_sb[:, di, :], start=True, stop=True),
                      reads=[T_tab, T_oh], writes=[PS[2]])
                S_.op("act", lambda e, di=di: e.copy(out=seqa[:, di, :], in_=psum[2][0:8, 0:384]), reads=[PS[2]], writes=[T_sa])
            T_GA = Dep()
            S_.dma("sp", GA_h.ap(), seqa[:], reads=[T_sa], writes=[T_GA])
            for h in range(8):
                for di in range(3):
                    for c in range(2):
                        src = bass.AP(GA_h, (h * 3 + di) * 384 + 128 * (1 - c), [[1, 128], [1, 128]])
                        S_.dma("sp", BMA[:, h, di, c, :], src, reads=[T_GA], writes=[T_const])
            T_win = Dep()
            wst = sb0("wst", [128, KD, 512], BF16)
            T_wst = Dep()
            for cb in range(6):
                S_.dma("sp", stage[:].rearrange("p (k c) -> p k c", k=KD),
                       w_in[:, cb * 512:(cb + 1) * 512].rearrange("(k p) c -> p k c", p=128), writes=[T_stage])
                for k in range(KD):
                    S_.op("dve" if k % 2 else "pool", lambda e, k=k: e.tensor_scalar(out=wst[:, k, :], in0=stage[:, k * 512:(k + 1) * 512], scalar1=gA[:, k:k + 1], scalar2=None, op0=ALU.mult),
                          reads=[T_stage, T_const], writes=[T_wst])
                S_.dma("sp", win_bf[:, :, cb * 512:(cb + 1) * 512], wst[:], reads=[T_wst], writes=[T_win])
            T_UV = Dep()
            gFrow = sb0("gFrow", [128, D], F32)
            T_gfr = Dep()
            S_.dma("sp", gFrow[:], ffn_g.partition_broadcast(128), writes=[T_gfr])
            NB0 = 4
            ub = [sb0(f"ub{i}", [128, D], BF16) for i in range(NB0)]
            T_ub = [Dep() for _ in range(NB0)]
            ut = [sb0(f"ut{i}", [128, KD, 128], BF16) for i in range(NB0)]
            T_ut = [Dep() for _ in range(NB0)]
            vb16 = [sb0(f"vb16{i}", [128, D], BF16) for i in range(NB0)]
            T_vb = [Dep() for _ in range(NB0)]
            ust = [sb0(f"ust{i}", [128, D], F32) for i in range(NB0)]
            T_ust = [Dep() for _ in range(NB0)]
            vst = [sb0(f"vst{i}", [128, D], F32) for i in range(NB0)]
            T_vst = [Dep() for _ in range(NB0)]
            def ld_uv(i):
                b = i % NB0
                S_.dma("sp", ust[b][:], pu[i * 128:(i + 1) * 128, :], writes=[T_ust[b]])
                S_.dma("sp", vst[b][:], pv[i * 128:(i + 1) * 128, :], writes=[T_vst[b]])
            for i in range(min(NB0 - 1, n_exp_chunks)):
                ld_uv(i)
            for i in range(n_exp_chunks):
                b = i % NB0
                pb2 = 4 + (i % 2)
                if i + NB0 - 1 < n_exp_chunks:
                    ld_uv(i + NB0 - 1)
                S_.op("dve", lambda e, b=b: e.tensor_tensor(out=ub[b][:], in0=ust[b][:], in1=gFrow[:], op=ALU.mult),
                      reads=[T_ust[b], T_gfr], writes=[T_ub[b]])
                pst = psum[pb2][:].bitcast(BF16)
                S_.group("pe", [lambda e, k=k, b=b, pst=pst: e.transpose(out=pst[:, k * 128:(k + 1) * 128], in_=ub[b][:, k * 128:(k + 1) * 128], identity=ident_b[:]) for k in range(KD)],
                         reads=[T_ub[b], T_const], writes=[PS[pb2]])
                S_.op("act", lambda e, b=b, pst=pst: e.copy(out=ut[b][:].rearrange("p k e -> p (k e)"), in_=pst), reads=[PS[pb2]], writes=[T_ut[b]])
                S_.dma("sp", UT_scr[i], ut[b][:], reads=[T_ut[b]], writes=[T_UV])
                S_.op("pool", lambda e, b=b: e.tensor_copy(out=vb16[b][:], in_=vst[b][:]), reads=[T_vst[b]], writes=[T_vb[b]])
                S_.dma("sp", V_scr[i * 128:(i + 1) * 128, :], vb16[b][:], reads=[T_vb[b]], writes=[T_UV])

        S_.barrier()
        for si, S in enumerate(seq_lens):
            S_.barrier()
            NT = S // 128
            NG = S // 512
            with ExitStack() as s1:
                def sb1(name, shape, dt):
                    return s1.enter_context(nc.sbuf_tensor(f"{name}_{si}", list(shape), dt))
                xnT = sb1("xnT", [128, KD, S], BF16)
                T_xnT = Dep()
                xt = [sb1(f"xt{i}", [128, D], F32) for i in range(2)]
                T_xt = [Dep(), Dep()]
                junk = sb1("junk", [128, D], BF16)
                T_junk = Dep()
                xnb = [sb1(f"xnb{i}", [128, D], BF16) for i in range(2)]
                T_xnb = [Dep(), Dep()]
                st4 = sb1("st4", [128, 8], F32)
                T_st4 = Dep()
                for i in range(NT):
                    b = i % 2
                    S_.dma("sp", xt[b][:], xs[si][i * 128:(i + 1) * 128, :], writes=[T_xt[b]])
                    S_.op("act", lambda e, b=b: e.activation(out=junk[:], in_=xt[b][:], func=AF.Square, accum_out=st4[:, 0:1]),
                          reads=[T_xt[b]], writes=[T_junk, T_st4])
                    S_.op("act", lambda e: e.activation(out=st4[:, 1:2], in_=st4[:, 0:1], func=AF.Sqrt, scale=1.0 / D, bias=EPS), reads=[T_st4], writes=[T_st4])
                    S_.op("dve", lambda e: e.reciprocal(out=st4[:, 2:3], in_=st4[:, 1:2]), reads=[T_st4], writes=[T_st4])
                    S_.op("dve", lambda e, b=b: e.tensor_scalar(out=xnb[b][:], in0=xt[b][:], scalar1=st4[:, 2:3], scalar2=None, op0=ALU.mult),
                          reads=[T_xt[b], T_st4], writes=[T_xnb[b]])
                    pst = psum[b][:].bitcast(BF16)
                    S_.group("pe", [lambda e, k=k, b=b, pst=pst: e.transpose(out=pst[:, k * 128:(k + 1) * 128], in_=xnb[b][:, k * 128:(k + 1) * 128], identity=ident_b[:]) for k in range(KD)],
                             reads=[T_xnb[b], T_const], writes=[PS[b]])
                    S_.op("act", lambda e, i=i, pst=pst: e.copy(out=xnT[:, :, i * 128:(i + 1) * 128], in_=pst.rearrange("p (k t) -> p k t", k=KD)),
                          reads=[PS[b]], writes=[T_xnT])

                wsl = sb1("wsl", [128, KD, 384], BF16)
                T_wsl = Dep()
                qT = sb1("qT", [128, 2, S], BF16)
                kT = sb1("kT", [128, S], BF16)
                T_qT, T_kT = Dep(), Dep()
                sq = [sb1(f"sq{i}", [128, 512], BF16) for i in range(2)]
                T_sq2 = [Dep(), Dep()]
                sd = [sb1(f"sd{i}", [128, 512], F32) for i in range(2)]
                T_sd = [Dep(), Dep()]

                S_.op("pool", lambda e: e.memset(qT[:], 0.0), writes=[T_qT])

                def qk_proj(dst, T_dst, wcol, gain, split=False):
                    def f1(g):
                        b = g % 2
                        pa = psum[b]
                        S_.group("pe", [lambda e, k=k, g=g, pa=pa: e.matmul(pa[:], lhsT=wsl[:, k, wcol:wcol + 128], rhs=xnT[:, k, g * 512:(g + 1) * 512], start=(k == 0), stop=(k == KD - 1)) for k in range(KD)],
                                 reads=[T_wsl, T_xnT], writes=[PS[b]])
                        S_.op("act", lambda e, b=b, pa=pa: e.activation(out=sq[b][:], in_=pa[:], func=AF.Square), reads=[PS[b]], writes=[T_sq2[b]])

                    def f2(g):
                        b = g % 2
                        pa, pb_ = psum[b], psum[2 + b]
                        S_.op("pe", lambda e, b=b, pb_=pb_: e.matmul(pb_[:], lhsT=bo_b[:], rhs=sq[b][:], start=True, stop=True), reads=[T_sq2[b], T_const], writes=[PS[2 + b]])
                        S_.op("act", lambda e, b=b, pb_=pb_: e.activation(out=sd[b][:], in_=pb_[:], func=AF.Sqrt, scale=1.0 / 64, bias=EPS), reads=[PS[2 + b]], writes=[T_sd[b]])
                        S_.op("dve", lambda e, b=b: e.reciprocal(out=sd[b][:], in_=sd[b][:]), reads=[T_sd[b]], writes=[T_sd[b]])
                        if split:
                            for hh_ in range(2):
                                rs_ = slice(hh_ * 64, hh_ * 64 + 64)
                                S_.op("dve", lambda e, b=b, g=g, pa=pa, rs_=rs_, hh_=hh_: e.scalar_tensor_tensor(out=dst[rs_, hh_, g * 512:(g + 1) * 512], in0=pa[rs_, :], scalar=gain[rs_, 0:1], in1=sd[b][rs_, :], op0=ALU.mult, op1=ALU.mult),
                                      reads=[PS[b], T_sd[b], T_const], writes=[T_dst])
                        else:
                            S_.op("dve", lambda e, b=b, g=g, pa=pa: e.scalar_tensor_tensor(out=dst[:, g * 512:(g + 1) * 512], in0=pa[:], scalar=gain[:, 0:1], in1=sd[b][:], op0=ALU.mult, op1=ALU.mult),
                                  reads=[PS[b], T_sd[b], T_const], writes=[T_dst])
                    pipeline(range(NG), f1, f2)

                dils = (1, 4, 16)
                with ExitStack() as s2:
                    def sb2(name, shape, dt):
                        return s2.enter_context(nc.sbuf_tensor(f"{name}_{si}", list(shape), dt))
                    Vp = [sb2(f"Vp{di}", [128, NT, 2, 65], BF16) for di in range(3)]
                    T_Vp = [Dep() for _ in range(3)]
                    acc = sb2("accA", [128, S], F32)
                    T_acc = Dep()
                    pT = [sb2(f"pTa{i}", [128, 2, 128], BF16) for i in range(4)]
                    T_pT = [Dep(), Dep(), Dep(), Dep()]
                    rden = sb2("rdenA", [64, 512], F32)
                    T_rden = Dep()
                    oTt = [sb2(f"oTtA{i}", [64, 512], BF16) for i in range(2)]
                    T_oTt = [Dep(), Dep()]
                    for di in range(3):
                        S_.op("pool", lambda e, di=di: e.memset(Vp[di][:, :, :, 64:65], 1.0), writes=[T_Vp[di]])
                    for hp in range(4):
                        for j, c0 in enumerate((hp * 128, 512 + hp * 128, 1024 + hp * 128)):
                            S_.dma("sp", wsl[:, :, j * 128:(j + 1) * 128], win_bf[:, :, c0:c0 + 128], reads=[T_win], writes=[T_wsl])
                        qk_proj(qT, T_qT, 0, gq_a, split=True)
                        qk_proj(kT, T_kT, 128, gk_a)
                        cnt = 0
                        for di, d in enumerate(dils):
                            L = S // d
                            ntc = L // 128
                            for r in range(d):
                                for j0 in range(0, ntc, 4):
                                    nj = min(4, ntc - j0)
                                    b = cnt % 2
                                    cnt += 1
                                    pv_ = psum[4 + b]
                                    fns = []
                                    for jj in range(nj):
                                        j = j0 + jj
                                        t0 = r + d * 128 * j
                                        for k in range(KD):
                                            fns.append(lambda e, k=k, jj=jj, t0=t0, d=d, pv_=pv_: e.matmul(
                                                pv_[:, jj * 128:(jj + 1) * 128], lhsT=xnT[:, k, t0:t0 + d * 127 + 1:d], rhs=wsl[:, k, 256:384],
                                                start=(k == 0), stop=(k == KD - 1)))
                                    S_.group("pe", fns, reads=[T_xnT, T_wsl], writes=[PS[4 + b]])
                                    ti = r * ntc + j0
                                    S_.op("act", lambda e, di=di, ti=ti, nj=nj, pv_=pv_: e.copy(
                                        out=Vp[di][:, ti:ti + nj, :, 0:64],
                                        in_=pv_[:, 0:nj * 128].rearrange("p (j h c) -> p j h c", j=nj, h=2)),
                                        reads=[PS[4 + b]], writes=[T_Vp[di]])
                        for hh in range(2):
                            h = hp * 2 + hh
                            ro = slice(hh * 64, hh * 64 + 64)
                            allb = []
                            for di, d in enumerate(dils):
                                L = S // d
                                ntc = L // 128
                                for r in range(d):
                                    blocks = [(0, 64, [(0, 1, 64)])]
                                    for bb in range(ntc - 1):
                                        blocks.append((128 * bb + 64, 128, [(bb, 0, 0), (bb + 1, 1, 0)]))
                                    blocks.append((L - 64, 64, [(ntc - 1, 0, 0)]))
                                    for (qm0, nq, kts) in blocks:
                                        allb.append((len(allb) % 4, di, d, r, ntc, qm0, nq, kts))

                            def fA1(it):
                                b, di, d, r, ntc, qm0, nq, kts = it
                                ps_s = psum[b]
                                q0 = r + d * qm0
                                qsl = slice(q0, q0 + d * (nq - 1) + 1, d)
                                fns = []
                                for ci, (kt, ch, c0) in enumerate(kts):
                                    k0 = r + d * 128 * kt
                                    fns.append(lambda e, ci=ci, k0=k0, d=d, qsl=qsl, nq=nq, ps_s=ps_s: e.matmul(
                                        ps_s[:, ci * 128:ci * 128 + nq], lhsT=kT[:, k0:k0 + d * 127 + 1:d], rhs=qT[:, hh, qsl], start=True, stop=False))
                                    fns.append(lambda e, ci=ci, ch=ch, c0=c0, nq=nq, di=di, ps_s=ps_s: e.matmul(
                                        ps_s[:, ci * 128:ci * 128 + nq], lhsT=anti_b[:], rhs=BMA[:, h, di, ch, c0:c0 + nq], start=False, stop=True))
                                S_.group("pe", fns, reads=[T_qT, T_kT, T_const], writes=[PS[b]])
                                if nq == 128:
                                    S_.op("act", lambda e, b=b, ps_s=ps_s: e.activation(out=pT[b][:].rearrange("p c q -> p (c q)"), in_=ps_s[:, 0:256], func=AF.Exp),
                                          reads=[PS[b]], writes=[T_pT[b]])
                                else:
                                    S_.op("act", lambda e, b=b, ps_s=ps_s: e.activation(out=pT[b][:, 0, 0:64], in_=ps_s[:, 0:64], func=AF.Exp),
                                          reads=[PS[b]], writes=[T_pT[b]])

                            def fA2(it):
                                b, di, d, r, ntc, qm0, nq, kts = it
                                ps_o = psum[4 + b]
                                q0 = r + d * qm0
                                qsl = slice(q0, q0 + d * (nq - 1) + 1, d)
                                nk = len(kts)
                                fns = []
                                for ci, (kt, ch, c0) in enumerate(kts):
                                    ti = r * ntc + kt
                                    fns.append(lambda e, ci=ci, ti=ti, di=di, nq=nq, b=b, nk=nk, ps_o=ps_o: e.matmul(
                                        ps_o[0:65, 0:nq], lhsT=Vp[di][:, ti, hh, :], rhs=pT[b][:, ci, 0:nq], start=(ci == 0), stop=(ci == nk - 1)))
                                S_.group("pe", fns, reads=[T_Vp[di], T_pT[b]], writes=[PS[4 + b]])
                                if di == 0:
                                    S_.op("dve", lambda e, qsl=qsl, nq=nq, ps_o=ps_o: e.tensor_copy(out=acc[0:65, qsl], in_=ps_o[0:65, 0:nq]),
                                          reads=[PS[4 + b]], writes=[T_acc])
                                else:
                                    S_.op("dve", lambda e, qsl=qsl, nq=nq, ps_o=ps_o: e.tensor_tensor(out=acc[0:65, qsl], in0=ps_o[0:65, 0:nq], in1=acc[0:65, qsl], op=ALU.add),
                                          reads=[PS[4 + b], T_acc], writes=[T_acc])
                            pipeline(allb, fA1, fA2, depth=3)
                            for g in range(NG):
                                b = g % 2
                                pd = psum[4 + b]
                                S_.op("pe", lambda e, g=g, pd=pd: e.matmul(pd[0:64, :], lhsT=sel_f[0:65, :], rhs=acc[0:65, g * 512:(g + 1) * 512], start=True, stop=True),
                                      reads=[T_acc, T_const], writes=[PS[4 + b]])
                                S_.op("dve", lambda e, pd=pd: e.reciprocal(out=rden[:], in_=pd[0:64, :]), reads=[PS[4 + b]], writes=[T_rden])
                                S_.op("dve", lambda e, g=g, b=b: e.tensor_tensor(out=oTt[b][:], in0=acc[0:64, g * 512:(g + 1) * 512], in1=rden[:], op=ALU.mult),
                                      reads=[T_acc, T_rden], writes=[T_oTt[b]])
                                S_.dma("pool", oT_scr[si][h * 64:(h + 1) * 64, g * 512:(g + 1) * 512], oTt[b][:], reads=[T_oTt[b]], writes=[T_oTs[si]])

                S_.barrier()
                with ExitStack() as s3:
                    def sb3(name, shape, dt):
                        return s3.enter_context(nc.sbuf_tensor(f"{name}_{si}", list(shape), dt))
                    offs = list(range(-640, 1025, 128))
                    BMB = sb3("BMB", [128, len(offs), 512], BF16)
                    T_BMB = Dep()
                    vB = sb3("vB", [128, NT, 128], BF16)
                    T_vB = Dep()
                    pTb = [sb3(f"pTb{i}", [128, 512], BF16) for i in range(4)]
                    T_pTb = [Dep() for _ in range(4)]
                    fa = [sb3(f"fa{i}", [128, 512], F32) for i in range(4)]
                    T_fa = [Dep() for _ in range(4)]
                    oTb = [sb3(f"oTb{i}", [128, 512], BF16) for i in range(2)]
                    T_oTb = [Dep(), Dep()]
                    for hb in range(4):
                        for j, c0 in enumerate((1536 + hb * 128, 2048 + hb * 128, 2560 + hb * 128)):
                            S_.dma("sp", wsl[:, :, j * 128:(j + 1) * 128], win_bf[:, :, c0:c0 + 128], reads=[T_win], writes=[T_wsl])
                        for oi, off in enumerate(offs):
                            base = RB - off - 127
                            if base < 0 or base + 127 + 511 >= 8192:
                                continue
                            src = bass.AP(GB_h, hb * 8192 + base, [[1, 128], [1, 512]])
                            S_.dma("sp", BMB[:, oi, :], src, reads=[T_GB], writes=[T_BMB])
                        qk_proj(qT, T_qT, 0, gq_b, split=True)
                        qk_proj(kT, T_kT, 128, gk_b)
                        for j0 in range(0, NT, 4):
                            b = (j0 // 4) % 2
                            pv_ = psum[4 + b]
                            fns = []
                            for jj in range(4):
                                j = j0 + jj
                                for k in range(KD):
                                    fns.append(lambda e, k=k, jj=jj, j=j, pv_=pv_: e.matmul(
                                        pv_[:, jj * 128:(jj + 1) * 128], lhsT=xnT[:, k, j * 128:(j + 1) * 128], rhs=wsl[:, k, 256:384],
                                        start=(k == 0), stop=(k == KD - 1)))
                            S_.group("pe", fns, reads=[T_xnT, T_wsl], writes=[PS[4 + b]])
                            S_.op("act", lambda e, j0=j0, pv_=pv_: e.copy(out=vB[:, j0:j0 + 4, :], in_=pv_[:].rearrange("p (j c) -> p j c", j=4)),
                                  reads=[PS[4 + b]], writes=[T_vB])
                        scnt = 0
                        for qg in range(NG):
                            itemsB = []
                            for kc in range(NT):
                                for mp in range(2):
                                    itemsB.append((kc, mp, scnt % 4))
                                    scnt += 1

                            def fB1(it, qg=qg):
                                kc, mp, sl = it
                                off = kc * 128 - qg * 512
                                near = not (off + 127 <= -FAR or off - 511 >= FAR)
                                ps_s = psum[sl]
                                rr = slice(mp * 64, mp * 64 + 64)
                                fns = [lambda e, rr=rr, kc=kc, qg=qg, ps_s=ps_s, near=near: e.matmul(
                                    ps_s[:], lhsT=kT[:, kc * 128:(kc + 1) * 128], rhs=qT[:, mp, qg * 512:(qg + 1) * 512], start=True, stop=not near)]
                                rds = [T_qT, T_kT]
                                if near:
                                    oi = offs.index(off)
                                    fns.append(lambda e, oi=oi, ps_s=ps_s: e.matmul(ps_s[:], lhsT=anti_b[:], rhs=BMB[:, oi, :], start=False, stop=True))
                                    rds += [T_BMB, T_const]
                                S_.group("pe", fns, reads=rds, writes=[PS[sl]])
                                if near:
                                    S_.op("act", lambda e, sl=sl, ps_s=ps_s: e.activation(out=pTb[sl][:], in_=ps_s[:], func=AF.Exp), reads=[PS[sl]], writes=[T_pTb[sl]])
                                else:
                                    sg = 0 if off < 0 else 1
                                    S_.op("act", lambda e, sl=sl, ps_s=ps_s, sg=sg: e.activation(out=pTb[sl][:], in_=ps_s[:], func=AF.Exp, bias=biasfar[:, hb, sg:sg + 1]),
                                          reads=[PS[sl], T_const], writes=[T_pTb[sl]])

                            def fB2(it):
                                kc, mp, sl = it
                                S_.group("pe", [
                                    lambda e, sl=sl, kc=kc, mp=mp: e.matmul(psum[4 + 2 * mp][:], lhsT=vB[:, kc, :], rhs=pTb[sl][:], start=(kc == 0), stop=(kc == NT - 1)),
                                    lambda e, sl=sl, kc=kc, mp=mp: e.matmul(psum[5 + 2 * mp][:], lhsT=ones_b[:], rhs=pTb[sl][:], start=(kc == 0), stop=(kc == NT - 1)),
                                ], reads=[T_vB, T_pTb[sl], T_const], writes=[PS[4 + 2 * mp], PS[5 + 2 * mp]])
                            pipeline(itemsB, fB1, fB2, depth=2)
                            S_.op("dve", lambda e: e.reciprocal(out=fa[0][:], in_=psum[5][:]), reads=[PS[5]], writes=[T_fa[0]])
                            S_.op("dve", lambda e: e.tensor_tensor(out=fa[1][:], in0=psum[4][:], in1=fa[0][:], op=ALU.mult), reads=[PS[4], T_fa[0]], writes=[T_fa[1]])
                            S_.op("dve", lambda e: e.reciprocal(out=fa[0][:], in_=psum[7][:]), reads=[PS[7], T_fa[0]], writes=[T_fa[0]])
                            S_.op("dve", lambda e: e.tensor_tensor(out=fa[2][:], in0=psum[6][:], in1=fa[0][:], op=ALU.mult), reads=[PS[6], T_fa[0]], writes=[T_fa[2]])
                            S_.op("dve", lambda e: e.scalar_tensor_tensor(out=fa[3][:], in0=fa[2][:], scalar=neglam[:, 0:1], in1=fa[1][:], op0=ALU.mult, op1=ALU.add),
                                  reads=[T_fa[1], T_fa[2], T_const], writes=[T_fa[3]])
                            S_.op("act", lambda e: e.activation(out=fa[1][:], in_=fa[3][:], func=AF.Square), reads=[T_fa[3], T_fa[1]], writes=[T_fa[1]])
                            S_.op("pe", lambda e: e.matmul(psum[0][:], lhsT=ones_f[:], rhs=fa[1][:], start=True, stop=True), reads=[T_fa[1], T_const], writes=[PS[0]])
                            S_.op("act", lambda e: e.activation(out=fa[2][:], in_=psum[0][:], func=AF.Sqrt, scale=1.0 / 128, bias=EPS), reads=[PS[0], T_fa[2]], writes=[T_fa[2]])
                            S_.op("dve", lambda e: e.reciprocal(out=fa[2][:], in_=fa[2][:]), reads=[T_fa[2]], writes=[T_fa[2]])
                            ob = qg % 2
                            S_.op("dve", lambda e, ob=ob: e.scalar_tensor_tensor(out=oTb[ob][:], in0=fa[3][:], scalar=gdiff[:, 0:1], in1=fa[2][:], op0=ALU.mult, op1=ALU.mult),
                                  reads=[T_fa[3], T_fa[2], T_const], writes=[T_oTb[ob]])
                            S_.dma("pool", oT_scr[si][512 + hb * 128:512 + (hb + 1) * 128, qg * 512:(qg + 1) * 512], oTb[ob][:], reads=[T_oTb[ob]], writes=[T_oTs[si]])

        S_.barrier()
        stA.close()
        with ExitStack() as s4:
            def sb4(name, shape, dt):
                return s4.enter_context(nc.sbuf_tensor(name, list(shape), dt))
            wout_b = sb4("wout_b", [128, KD, D], BF16)
            wqs = [sb4(f"wqs{i}", [128, KD, 128], BF16) for i in range(2)]
            T_wqs = [Dep(), Dep()]
            T_wqscr = Dep()
            KzT = sb4("KzT", [128, 2, 128], BF16)
            T_w4 = Dep()
            s4t = ExitStack()
            stg = s4t.enter_context(nc.sbuf_tensor("stg4", [128, KD, 512], F32))
            T_stg = Dep()
            wqst = s4t.enter_context(nc.sbuf_tensor("wqst", [128, KD, 512], BF16))
            T_wqst = Dep()
            for cb in range(2):
                S_.dma("sp", stg[:], w_out[:, cb * 512:(cb + 1) * 512].rearrange("(k p) c -> p k c", p=128), writes=[T_stg])
                S_.op("dve", lambda e, cb=cb: e.tensor_copy(out=wout_b[:, :, cb * 512:(cb + 1) * 512], in_=stg[:]), reads=[T_stg], writes=[T_w4])
            for cb in range(4):
                S_.dma("sp", stg[:], w_q[:, cb * 512:(cb + 1) * 512].rearrange("(k p) c -> p k c", p=128), writes=[T_stg])
                for k in range(KD):
                    S_.op("dve", lambda e, k=k, cb=cb: e.tensor_scalar(out=wqst[:, k, :], in0=stg[:, k, :], scalar1=gF[:, k:k + 1], scalar2=None, op0=ALU.mult),
                          reads=[T_stg, T_const], writes=[T_wqst])
                S_.dma("sp", wq_scr[:, :, cb * 512:(cb + 1) * 512], wqst[:], reads=[T_wqst], writes=[T_wqscr])
            for z in range(2):
                S_.dma("sp", stg[:, 0, 0:128], subk[z], writes=[T_stg])
                S_.op("pe", lambda e: e.transpose(out=psum[0][:, 0:128], in_=stg[:, 0, 0:128], identity=ident_f[:]), reads=[T_stg, T_const], writes=[PS[0]])
                S_.op("act", lambda e, z=z: e.copy(out=KzT[:, z, :], in_=psum[0][:, 0:128]), reads=[PS[0]], writes=[T_w4])

            S_.barrier()
            s4t.close()
            TB = 256
            oTl = sb4("oTl", [128, KD, TB], BF16)
            T_oTl = Dep()
            xt4 = sb4("xt4", [128, D], F32)
            T_xt4 = Dep()
            x1 = [[sb4(f"x1_{bf}_{i}", [128, D], F32) for i in range(2)] for bf in range(2)]
            T_x1 = [[Dep(), Dep()] for _ in range(2)]
            st5 = sb4("st5", [128, 8], F32)
            T_st5 = Dep()
            S_.op("pool", lambda e: e.memset(st5[:, 4:5], -0.5), writes=[T_st5])
            xtmp = sb4("xtmp", [128, 512], F32)
            T_xtmp = Dep()
            hnb = sb4("hnb", [128, D], BF16)
            T_hnb = Dep()
            hT = [sb4(f"hT{bf}", [128, KD, TB], BF16) for bf in range(2)]
            T_hT = [Dep(), Dep()]
            qpT = sb4("qpT", [128, 16, TB], BF16)
            T_qpT = Dep()
            sc = [sb4(f"sc{i}", [128, 2048], F32) for i in range(2)]
            T_sc = [[Dep() for _ in range(4)] for _ in range(2)]
            sc2 = sb4("sc2", [128, 128], F32)
            T_sc2 = Dep()
            top = sb4("top", [128, 16, 16], F32)
            idx = sb4("idx", [128, 16, 16], U32)
            idxf = sb4("idxf", [128, 16, 16], F32)
            T_top, T_idx = Dep(), Dep()
            T_cand = Dep()
            cand2 = sb4("cand2", [128, 256], F32)
            T_cand2 = Dep()
            best = sb4("best", [128, 8, 16], F32)
            pos = sb4("pos", [128, 8, 16], U32)
            k12 = sb4("k12", [128, 2, 8, 16], U32)
            k12f = sb4("k12f", [128, 2, 8, 16], F32)
            T_best, T_pos, T_k12 = Dep(), Dep(), Dep()
            T_eq = T_cand
            IJg2 = sb4("IJg", [128, 2, 3, 128], F32)
            T_IJg = Dep()
            gw = sb4("gw", [128, 8, 16], F32)
            zz = sb4("zz", [128, 8], F32)
            T_gw = Dep()
            IJgT = sb4("IJgT", [128, 3, TB], F32)
            T_IJgT = Dep()
            NRING = 12
            AB = sb4("AB", [128, NRING, 2, 128], BF16)
            T_AB = [Dep() for _ in range(NRING)]
            T_BB = [Dep() for _ in range(NRING)]
            Gsb = sb4("Gsb", [128, TB, 128], BF16)
            T_G = Dep()
            NUV = 8
            UTb = [sb4(f"UTb{i}", [128, KD, 128], BF16) for i in range(NUV)]
            Vb = [sb4(f"Vb{i}", [128, D], BF16) for i in range(NUV)]
            T_UTb = [Dep() for _ in range(NUV)]
            T_Vb = [Dep() for _ in range(NUV)]
            asb = [sb4(f"asb{i}", [128, TB], BF16) for i in range(3)]
            wsb = [sb4(f"wsb{i}", [128, TB], BF16) for i in range(3)]
            T_asb = [Dep(), Dep(), Dep()]
            T_wsb = [Dep(), Dep(), Dep()]
            yt = sb4("yt", [128, 512], F32)
            T_yt = Dep()
            iota16 = iota_f[:, 0:16]

            blocks = [(si, blk * TB) for si, S in enumerate(seq_lens) for blk in range(S // TB)]

            def gen_pe(bi, pbks):
                si, t0 = blocks[bi]
                bf = bi % 2
                cnt_ = [0]

                def nb_():
                    cnt_[0] += 1
                    return pbks[cnt_[0] % len(pbks)]
                S_.dma("sp", oTl[:], oT_scr[si][:, t0:t0 + TB].rearrange("(k p) t -> p k t", p=128), reads=[T_oTs[si]], writes=[T_oTl])
                for t2 in range(2):
                    S_.dma("sp", xt4[:], xs[si][t0 + t2 * 128:t0 + (t2 + 1) * 128, :], writes=[T_xt4])
                    for hf in range(2):
                        pbk = nb_()
                        S_.group("pe", [lambda e, k=k, t2=t2, hf=hf, pbk=pbk: e.matmul(psum[pbk][:], lhsT=oTl[:, k, t2 * 128:(t2 + 1) * 128], rhs=wout_b[:, k, hf * 512:(hf + 1) * 512], start=(k == 0), stop=(k == KD - 1)) for k in range(KD)],
                                 reads=[T_oTl, T_w4], writes=[PS[pbk]])
                        S_.op("act", lambda e, pbk=pbk: e.copy(out=xtmp[:], in_=psum[pbk][:]), reads=[PS[pbk]], writes=[T_xtmp])
                        S_.op("pool", lambda e, t2=t2, hf=hf: e.tensor_tensor(out=x1[bf][t2][:, hf * 512:(hf + 1) * 512], in0=xtmp[:], in1=xt4[:, hf * 512:(hf + 1) * 512], op=ALU.add),
                              reads=[T_xtmp, T_xt4], writes=[T_x1[bf][t2]])
                        yield
                    if debug:
                        S_.dma("pool", dbg[f"dbg_x1_{si}"][t0 + t2 * 128:t0 + (t2 + 1) * 128, :], x1[bf][t2][:], reads=[T_x1[bf][t2]], writes=[Dep()])
                    S_.op("act", lambda e, t2=t2: e.activation(out=hnb[:], in_=x1[bf][t2][:], func=AF.Square, accum_out=st5[:, 0:1]), reads=[T_x1[bf][t2]], writes=[T_hnb, T_st5])
                    S_.op("pool", lambda e: e.tensor_scalar(out=st5[:, 1:2], in0=st5[:, 0:1], scalar1=1.0 / D, scalar2=EPS, op0=ALU.mult, op1=ALU.add), reads=[T_st5], writes=[T_st5])
                    S_.op("pool", lambda e: e.tensor_tensor(out=st5[:, 2:3], in0=st5[:, 1:2], in1=st5[:, 4:5], op=ALU.pow), reads=[T_st5], writes=[T_st5])
                    S_.op("act", lambda e, t2=t2: e.activation(out=hnb[:], in_=x1[bf][t2][:], func=AF.Copy, scale=st5[:, 2:3]), reads=[T_x1[bf][t2], T_st5], writes=[T_hnb])
                    yield
                    pbk = nb_()
                    pst = psum[pbk][:].bitcast(BF16)
                    S_.group("pe", [lambda e, k=k, pst=pst: e.transpose(out=pst[:, k * 128:(k + 1) * 128], in_=hnb[:, k * 128:(k + 1) * 128], identity=ident_b[:]) for k in range(KD)],
                             reads=[T_hnb, T_const], writes=[PS[pbk]])
                    S_.op("act", lambda e, t2=t2, pst=pst: e.copy(out=hT[bf][:, :, t2 * 128:(t2 + 1) * 128], in_=pst.rearrange("p (k t) -> p k t", k=KD)), reads=[PS[pbk]], writes=[T_hT[bf]])
                    yield
                for pz in range(16):
                    b = pz % 2
                    pbk = nb_()
                    S_.dma("sp", wqs[b][:], wq_scr[:, :, pz * 128:(pz + 1) * 128], reads=[T_wqscr], writes=[T_wqs[b]])
                    S_.group("pe", [lambda e, k=k, pz=pz, b=b, pbk=pbk: e.matmul(psum[pbk][:, 0:TB], lhsT=wqs[b][:, k, :], rhs=hT[bf][:, k, :], start=(k == 0), stop=(k == KD - 1)) for k in range(KD)],
                             reads=[T_hT[bf], T_wqs[b]], writes=[PS[pbk]])
                    S_.op("act", lambda e, pz=pz, b=b, pbk=pbk: e.copy(out=qpT[:, pz, :], in_=psum[pbk][:, 0:TB]), reads=[PS[pbk]], writes=[T_qpT])
                    yield
                for t2 in range(2):
                    for bk in range(4):
                        pbk = nb_()
                        S_.group("pe", [lambda e, pz=pz, t2=t2, pbk=pbk: e.matmul(psum[pbk][:, (pz % 4) * 128:(pz % 4 + 1) * 128], lhsT=qpT[:, pz, t2 * 128:(t2 + 1) * 128], rhs=KzT[:, pz % 2, :], start=True, stop=True) for pz in range(bk * 4, bk * 4 + 4)],
                                 reads=[T_qpT, T_w4], writes=[PS[pbk]])
                        S_.op("act", lambda e, t2=t2, bk=bk, pbk=pbk: e.copy(out=sc[t2][:, bk * 512:(bk + 1) * 512], in_=psum[pbk][:]), reads=[PS[pbk]], writes=[T_sc[t2][bk], T_cand])
                        yield

            def gen_dve(bi):
                si, t0 = blocks[bi]
                bf = bi % 2
                for t2 in range(2):
                    for bk in range(4):
                        for g in range(bk * 4, bk * 4 + 4):
                            gs = slice(g * 128, (g + 1) * 128)
                            S_.op("dve", lambda e, g=g, gs=gs, t2=t2: e.max(out=top[:, g, 0:8], in_=sc[t2][:, gs]), reads=[T_sc[t2][bk]], writes=[T_top])
                            S_.op("dve", lambda e, g=g, gs=gs, t2=t2: e.max_index(out=idx[:, g, 0:8], in_max=top[:, g, 0:8], in_values=sc[t2][:, gs]), reads=[T_sc[t2][bk], T_top], writes=[T_idx])
                            S_.op("dve", lambda e, g=g, gs=gs, t2=t2: e.match_replace(out=sc2[:], in_to_replace=top[:, g, 0:8], in_values=sc[t2][:, gs], imm_value=-1e30), reads=[T_sc[t2][bk], T_top], writes=[T_sc2])
                            yield
                            S_.op("dve", lambda e, g=g: e.max(out=top[:, g, 8:16], in_=sc2[:]), reads=[T_sc2], writes=[T_top])
                            S_.op("dve", lambda e, g=g: e.max_index(out=idx[:, g, 8:16], in_max=top[:, g, 8:16], in_values=sc2[:]), reads=[T_sc2, T_top], writes=[T_idx])
                            yield
                    S_.op("dve", lambda e: e.tensor_copy(out=idxf[:], in_=idx[:]), reads=[T_idx], writes=[T_idx])
                    t4v = top[:].rearrange("p (h z) k -> p h z k", z=2)
                    i4v = idxf[:].rearrange("p (h z) k -> p h z k", z=2)
                    cand = sc[t2][:].rearrange("p (h c) -> p h c", h=8)
                    eq = sc[t2][:].rearrange("p (h s k) -> p h s k", h=8, s=16)
                    c4 = sc[t2][:].rearrange("p (h a b) -> p h a b", h=8, a=16)
                    S_.op("dve", lambda e: e.tensor_tensor(out=c4, in0=t4v[:, :, 0, :].unsqueeze(3).to_broadcast([128, 8, 16, 16]), in1=t4v[:, :, 1, :].unsqueeze(2).to_broadcast([128, 8, 16, 16]), op=ALU.add),
                          reads=[T_top], writes=[T_cand] + T_sc[t2])
                    yield
                    for p in range(8):
                        S_.op("dve", lambda e, p=p: e.max(out=best[:, p, 0:8], in_=cand[:, p, :]), reads=[T_cand], writes=[T_best])
                        S_.op("dve", lambda e, p=p: e.max_index(out=pos[:, p, 0:8], in_max=best[:, p, 0:8], in_values=cand[:, p, :]), reads=[T_cand, T_best], writes=[T_pos])
                        S_.op("dve", lambda e, p=p: e.match_replace(out=cand2[:], in_to_replace=best[:, p, 0:8], in_values=cand[:, p, :], imm_value=-1e30), reads=[T_cand, T_best], writes=[T_cand2])
                        yield
                        S_.op("dve", lambda e, p=p: e.max(out=best[:, p, 8:16], in_=cand2[:]), reads=[T_cand2], writes=[T_best])
                        S_.op("dve", lambda e, p=p: e.max_index(out=pos[:, p, 8:16], in_max=best[:, p, 8:16], in_values=cand2[:]), reads=[T_cand2, T_best], writes=[T_pos])
                        yield
                    S_.op("dve", lambda e: e.tensor_tensor(out=gw[:], in0=best[:], in1=best[:, :, 0:1].to_broadcast([128, 8, 16]), op=ALU.subtract), reads=[T_best], writes=[T_gw])
                    S_.op("act", lambda e: e.activation(out=gw[:], in_=gw[:], func=AF.Exp), reads=[T_gw], writes=[T_gw])
                    S_.op("dve", lambda e: e.reduce_sum(out=zz[:], in_=gw[:], axis=AX.X), reads=[T_gw], writes=[T_gw])
                    S_.op("dve", lambda e: e.reciprocal(out=zz[:], in_=zz[:]), reads=[T_gw], writes=[T_gw])
                    S_.op("dve", lambda e, t2=t2: e.tensor_tensor(out=IJg2[:, t2, 2, :].rearrange("p (h s) -> p h s", h=8), in0=gw[:], in1=zz[:].unsqueeze(2).to_broadcast([128, 8, 16]), op=ALU.mult),
                          reads=[T_gw], writes=[T_IJg])
                    yield
                    S_.op("dve", lambda e: e.tensor_single_scalar(out=k12[:, 0], in_=pos[:], scalar=4, op=ALU.logical_shift_right), reads=[T_pos], writes=[T_k12])
                    S_.op("dve", lambda e: e.tensor_single_scalar(out=k12[:, 1], in_=pos[:], scalar=15, op=ALU.bitwise_and), reads=[T_pos, T_k12], writes=[T_k12])
                    S_.op("dve", lambda e: e.tensor_copy(out=k12f[:], in_=k12[:]), reads=[T_k12], writes=[T_k12])
                    yield
                    for z in range(2):
                        S_.op("dve", lambda e, z=z: e.tensor_tensor(out=eq[:], in0=k12f[:, z].unsqueeze(3).to_broadcast([128, 8, 16, 16]),
                                                                      in1=iota16.unsqueeze(1).unsqueeze(1).to_broadcast([128, 8, 16, 16]), op=ALU.is_equal),
                              reads=[T_k12, T_const, T_eq], writes=[T_eq])
                        yield
                        S_.op("dve", lambda e, z=z: e.tensor_tensor(out=eq[:], in0=eq[:], in1=i4v[:, :, z, :].unsqueeze(2).to_broadcast([128, 8, 16, 16]), op=ALU.mult),
                              reads=[T_eq, T_idx], writes=[T_eq])
                        yield
                        S_.op("dve", lambda e, z=z, t2=t2: e.reduce_sum(out=IJg2[:, t2, z, :], in_=eq[:].rearrange("p h s k -> p (h s) k"), axis=AX.X), reads=[T_eq], writes=[T_IJg])
                        yield

            def emit_B6(gpe=None):
                for t2 in range(2):
                    S_.group("pe", [lambda e, j=j, t2=t2: e.transpose(out=psum[5][:, j * 128:(j + 1) * 128], in_=IJg2[:, t2, j, :], identity=ident_f[:]) for j in range(3)],
                             reads=[T_IJg, T_const], writes=[PS[5]])
                    S_.op("act", lambda e, t2=t2: e.copy(out=IJgT[:, :, t2 * 128:(t2 + 1) * 128], in_=psum[5][:, 0:384].rearrange("p (j t) -> p j t", j=3)), reads=[PS[5]], writes=[T_IJgT])
                for tq in range(TB // 4):
                    gb = 5 + (tq % 2)
                    for tt in range(4):
                        t = tq * 4 + tt
                        rs = t % NRING
                        S_.op("dve", lambda e, t=t, rs=rs: e.tensor_scalar(out=AB[:, rs, 0, :], in0=iota_f[:], scalar1=IJgT[:, 0, t:t + 1], scalar2=IJgT[:, 2, t:t + 1], op0=ALU.is_equal, op1=ALU.mult),
                              reads=[T_IJgT, T_const], writes=[T_AB[rs]])
                        S_.op("dve", lambda e, t=t, rs=rs: e.tensor_scalar(out=AB[:, rs, 1, :], in0=iota_f[:], scalar1=IJgT[:, 1, t:t + 1], scalar2=None, op0=ALU.is_equal),
                              reads=[T_IJgT, T_const], writes=[T_BB[rs]])
                        S_.op("pe", lambda e, tt=tt, rs=rs, gb=gb: e.matmul(psum[gb][:, tt * 128:(tt + 1) * 128], lhsT=AB[:, rs, 1, :], rhs=AB[:, rs, 0, :], start=True, stop=True),
                              reads=[T_AB[rs], T_BB[rs]], writes=[PS[gb]])
                    S_.op("act", lambda e, tq=tq, gb=gb: e.copy(out=Gsb[:, tq * 4:tq * 4 + 4, :].rearrange("j t i -> j (t i)"), in_=psum[gb][:]),
                          reads=[PS[gb]], writes=[T_G])
                    if gpe is not None:
                        next(gpe, None)
                if gpe is not None:
                    for _ in gpe:
                        pass

            def emit_B7(bi, filler):
                bf = bi % 2

                def f71(i):
                    u = i % NUV
                    a = i % 3
                    S_.dma("sp", UTb[u][:], UT_scr[i], reads=[T_UV], writes=[T_UTb[u]])
                    S_.dma("sp", Vb[u][:], V_scr[i * 128:(i + 1) * 128, :], reads=[T_UV], writes=[T_Vb[u]])
                    pa_ = psum[(4, 6, 7)[a]][:, 0:TB]
                    S_.group("pe", [lambda e, k=k, u=u, pa_=pa_: e.matmul(pa_, lhsT=UTb[u][:, k, :], rhs=hT[bf][:, k, :], start=(k == 0), stop=(k == KD - 1)) for k in range(KD)],
                             reads=[T_UTb[u], T_hT[bf]], writes=[PSA[a]])
                    S_.op("act", lambda e, a=a, pa_=pa_: e.activation(out=asb[a][:], in_=pa_, func=AF.Gelu), reads=[PSA[a]], writes=[T_asb[a]])
                    S_.op("pool", lambda e, a=a, i=i: e.tensor_tensor(out=wsb[a][:], in0=asb[a][:], in1=Gsb[:, :, i], op=ALU.mult), reads=[T_asb[a], T_G], writes=[T_wsb[a]])

                def f72(i):
                    u = i % NUV
                    a = i % 3
                    S_.group("pe", [lambda e, t2=t2, hf=hf, a=a, u=u, i=i: e.matmul(psum[t2 * 2 + hf][:], lhsT=wsb[a][:, t2 * 128:(t2 + 1) * 128], rhs=Vb[u][:, hf * 512:(hf + 1) * 512], start=(i == 0), stop=(i == n_exp_chunks - 1)) for t2 in range(2) for hf in range(2)],
                             reads=[T_wsb[a], T_Vb[u]], writes=[PS[0], PS[1], PS[2], PS[3]])
                    if filler is not None:
                        next(filler, None)
                pipeline(range(n_exp_chunks), f71, f72, depth=2)
                if filler is not None:
                    for _ in filler:
                        pass

            def emit_B8(bi):
                si, t0 = blocks[bi]
                bf = bi % 2
                for t2 in range(2):
                    for hf in range(2):
                        S_.op("dve", lambda e, t2=t2, hf=hf: e.tensor_tensor(out=yt[:], in0=psum[t2 * 2 + hf][:], in1=x1[bf][t2][:, hf * 512:(hf + 1) * 512], op=ALU.add),
                              reads=[PS[t2 * 2 + hf], T_x1[bf][t2]], writes=[T_yt])
                        S_.dma("pool", ys[si][t0 + t2 * 128:t0 + (t2 + 1) * 128, hf * 512:(hf + 1) * 512], yt[:], reads=[T_yt], writes=[T_y])

            for _ in gen_pe(0, (5, 6)):
                pass
            for _ in gen_dve(0):
                pass
            for bi in range(len(blocks)):
                emit_B6(gen_pe(bi + 1, (7, 4)) if bi + 1 < len(blocks) else None)
                filler = gen_dve(bi + 1) if bi + 1 < len(blocks) else None
                emit_B7(bi, filler)
                emit_B8(bi)
        S_.finish()
    print("instructions:", S_.ninst)
    return nc


_CONSTS = None


def _in_maps(inputs, seq_lens_per_core, core_seqs):
    global _CONSTS
    if _CONSTS is None:
        _CONSTS = host_consts()
    c = _CONSTS
    base = {
        "rel_bias": inputs["rel_bias"], "attn_norm_g": inputs["attn_norm_g"][0], "w_in": inputs["w_in"][0],
        "q_norm_a": inputs["q_norm_a"][0], "k_norm_a": inputs["k_norm_a"][0], "q_norm_b": inputs["q_norm_b"][0],
        "k_norm_b": inputs["k_norm_b"][0], "lambda_q1": inputs["lambda_q1"][0], "lambda_k1": inputs["lambda_k1"][0],
        "lambda_q2": inputs["lambda_q2"][0], "lambda_k2": inputs["lambda_k2"][0], "diff_norm_g": inputs["diff_norm_g"][0],
        "w_out": inputs["w_out"][0], "ffn_norm_g": inputs["ffn_norm_g"][0], "peer_w_q": inputs["peer_w_q"][0],
        "peer_sub_keys": inputs["peer_sub_keys"][0], "peer_u": inputs["peer_u"][0], "peer_v": inputs["peer_v"][0],
        "c_ident": c["ident"], "c_antiI": c["antiI"], "c_blockones": c["blockones"], "c_iota": c["iota"],
        "c_ohb": c["ohb"], "c_oha": c["oha"], "c_sel65": c["sel65"],
    }
    base = {k: np.ascontiguousarray(np.asarray(v, dtype=np.float32)) for k, v in base.items()}
    maps = []
    for seqs in core_seqs:
        m = dict(base)
        for i, xarr in enumerate(seqs):
            m[f"x{i}"] = np.ascontiguousarray(np.asarray(xarr, dtype=np.float32))
        maps.append(m)
    return maps


def kernel(**inputs):
    inputs = {k: np.asarray(v) for k, v in inputs.items()}
    xp = inputs["x_prompt"]
    xsmp = inputs["x_sample"]
    n = 8
    seq_lens = [xp.shape[1], xsmp.shape[1], xsmp.shape[1]]
    core_seqs = [[xp[c], xsmp[2 * c], xsmp[2 * c + 1]] for c in range(n)]
    nc = build(seq_lens)
    maps = _in_maps(inputs, seq_lens, core_seqs)
    res = run_bass_kernel_spmd(nc, maps, core_ids=list(range(n)))
    yp = np.stack([res.results[c]["y0"] for c in range(n)], axis=0).astype(np.float32)
    ysm = np.empty(xsmp.shape, np.float32)
    for c in range(n):
        ysm[2 * c] = res.results[c]["y1"]
        ysm[2 * c + 1] = res.results[c]["y2"]
    return (yp, ysm)
```

```python
import math
from contextlib import ExitStack
import numpy as np
import concourse.bass as bass
import concourse.mybir as mybir
from concourse.bass_utils import run_bass_kernel_spmd

F32 = mybir.dt.float32
BF16 = mybir.dt.bfloat16
U32 = mybir.dt.uint32
AF = mybir.ActivationFunctionType
ALU = mybir.AluOpType
AX = mybir.AxisListType

D = 1024
KD = 8
EPS = 1e-6
NEG = -30000.0
NDS = 56


class Dep:
    __slots__ = ("w", "r")

    def __init__(self):
        self.w = None
        self.r = {}


class Eng:
    def __init__(self, name, eng, sem):
        self.name = name
        self.eng = eng
        self.sem = sem
        self.count = 0
        self.seen = {}


class Sched:
    def __init__(self, nc, stack):
        self.nc = nc
        self.E = {}
        for name, eng in [("pe", nc.tensor), ("act", nc.scalar), ("dve", nc.vector),
                          ("pool", nc.gpsimd), ("sp", nc.sync)]:
            sem = stack.enter_context(nc.semaphore(f"s_{name}"))
            self.E[name] = Eng(name, eng, sem)
        self.dsems = []
        self.dpool = {"sp": [], "pool": []}
        for i in range(NDS):
            sem = stack.enter_context(nc.semaphore(f"dq{i}"))
            self.dsems.append([sem, 0])
            self.dpool["pool" if i >= NDS - 16 else "sp"].append(i)
        self.dnext = {"sp": 0, "pool": 0}
        self.ninst = 0

    def _wait(self, e, deps):
        best = {}
        for (key, sem, val) in deps:
            if key == "pe" and e.name == "pe":
                continue
            if val > best.get(key, (None, 0))[1]:
                best[key] = (sem, val)
        for key, (sem, val) in best.items():
            if e.seen.get(key, 0) < val:
                e.eng.wait_ge(sem, val)
                e.seen[key] = val

    @staticmethod
    def _deps(reads, writes):
        d = []
        for t in reads:
            if t.w is not None:
                d.append(t.w)
        for t in writes:
            if t.w is not None:
                d.append(t.w)
            d.extend(t.r.values())
        return d

    @staticmethod
    def _mark(tok, reads, writes):
        for t in writes:
            t.w = tok
            t.r = {}
        for t in reads:
            t.r[tok[0]] = tok

    def op(self, en, fn, reads=(), writes=()):
        e = self.E[en]
        self._wait(e, self._deps(reads, writes))
        ins = fn(e.eng)
        e.count += 1
        ins.then_inc(e.sem, 1)
        self._mark((en, e.sem, e.count), reads, writes)
        self.ninst += 1

    def group(self, en, fns, reads=(), writes=()):
        e = self.E[en]
        self._wait(e, self._deps(reads, writes))
        ins = None
        for fn in fns:
            ins = fn(e.eng)
            self.ninst += 1
        e.count += 1
        ins.then_inc(e.sem, 1)
        self._mark((en, e.sem, e.count), reads, writes)

    def dma(self, qn, out, in_, reads=(), writes=(), **kw):
        e = self.E[qn]
        pl = self.dpool[qn]
        idx = pl[self.dnext[qn]]
        self.dnext[qn] = (self.dnext[qn] + 1) % len(pl)
        slot = self.dsems[idx]
        deps = self._deps(reads, writes)
        key = ("d", idx)
        if slot[1] > 0:
            deps.append((key, slot[0], slot[1]))
        self._wait(e, deps)
        ins = e.eng.dma_start(out=out, in_=in_, **kw)
        slot[1] += 16
        ins.then_inc(slot[0], 16)
        self._mark((key, slot[0], slot[1]), reads, writes)
        self.ninst += 1

    def barrier(self):
        for e in self.E.values():
            for o in self.E.values():
                if o.count == 0:
                    continue
                if e.seen.get(o.name, 0) < o.count:
                    e.eng.wait_ge(o.sem, o.count)
                    e.seen[o.name] = o.count
            for idx, slot in enumerate(self.dsems):
                key = ("d", idx)
                if slot[1] > 0 and e.seen.get(key, 0) < slot[1]:
                    e.eng.wait_ge(slot[0], slot[1])
                    e.seen[key] = slot[1]

    def finish(self):
        e = self.E["sp"]
        for idx, slot in enumerate(self.dsems):
            if slot[1] > 0 and e.seen.get(("d", idx), 0) < slot[1]:
                e.eng.wait_ge(slot[0], slot[1])
        for name in ("pe", "act", "dve", "pool"):
            o = self.E[name]
            if o.count > 0:
                e.eng.wait_ge(o.sem, o.count)


def pipeline(items, first, second, depth=1):
    items = list(items)
    n = len(items)
    for i in range(n + depth):
        if i < n:
            first(items[i])
        if i - depth >= 0:
            second(items[i - depth])


def rel_bucket_np(rel):
    nb = 16
    max_exact = 8
    n = np.abs(rel)
    with np.errstate(divide="ignore"):
        large = max_exact + (np.log(np.maximum(n, 1).astype(np.float32) / np.float32(max_exact))
                             / np.float32(math.log(1024 / max_exact)) * (nb - max_exact)).astype(np.int32)
    large = np.minimum(large, nb - 1)
    return (rel > 0).astype(np.int32) * nb + np.where(n < max_exact, n, large)


RB = 4095
FAR = 559


def host_consts():
    c = {}
    c["ident"] = np.eye(128, dtype=np.float32)
    c["antiI"] = np.ascontiguousarray(np.eye(128, dtype=np.float32)[::-1])
    bo = np.zeros((128, 128), np.float32)
    bo[:64, :64] = 1
    bo[64:, 64:] = 1
    c["blockones"] = bo
    c["iota"] = np.tile(np.arange(128, dtype=np.float32)[None, :], (128, 1))
    i = np.arange(8192)
    ohb = np.zeros((32, 8192), np.float32)
    ohb[rel_bucket_np(RB - i), i] = 1.0
    c["ohb"] = ohb
    oha = np.zeros((3, 33, 384), np.float32)
    for di, d in enumerate((1, 4, 16)):
        for ii in range(384):
            rm = 191 - ii
            if abs(rm) <= 64:
                oha[di, rel_bucket_np(np.array(rm * d)), ii] = 1.0
            else:
                oha[di, 32, ii] = 1.0
    c["oha"] = oha
    sel = np.zeros((128, 64), np.float32)
    sel[64, :] = 1.0
    c["sel65"] = sel
    return c


def build(seq_lens, n_exp_chunks=128, debug=False):
    nc = bass.Bass("TRN2", target_bir_lowering=False)
    NSEQ = len(seq_lens)
    TTOT = sum(seq_lens)
    SMAX = max(seq_lens)

    def din(name, shape, dt=F32):
        return nc.dram_tensor(name, list(shape), dt, kind="ExternalInput").ap()

    xs = [din(f"x{i}", [S, D]) for i, S in enumerate(seq_lens)]
    ys = [nc.dram_tensor(f"y{i}", [S, D], F32, kind="ExternalOutput").ap() for i, S in enumerate(seq_lens)]
    rel_bias = din("rel_bias", [32, 12])
    attn_g = din("attn_norm_g", [D])
    w_in = din("w_in", [D, 3072])
    qn_a = din("q_norm_a", [64])
    kn_a = din("k_norm_a", [64])
    qn_b = din("q_norm_b", [64])
    kn_b = din("k_norm_b", [64])
    lq1 = din("lambda_q1", [64])
    lk1 = din("lambda_k1", [64])
    lq2 = din("lambda_q2", [64])
    lk2 = din("lambda_k2", [64])
    dn_g = din("diff_norm_g", [128])
    w_out = din("w_out", [D, D])
    ffn_g = din("ffn_norm_g", [D])
    w_q = din("peer_w_q", [D, 2048])
    subk = din("peer_sub_keys", [2, 128, 128])
    pu = din("peer_u", [16384, D])
    pv = din("peer_v", [16384, D])
    c_ident = din("c_ident", [128, 128])
    c_anti = din("c_antiI", [128, 128])
    c_bo = din("c_blockones", [128, 128])
    c_iota = din("c_iota", [128, 128])
    c_ohb = din("c_ohb", [32, 8192])
    c_oha = din("c_oha", [3, 33, 384])
    c_sel = din("c_sel65", [128, 64])

    def dscr(name, shape, dt):
        return nc.dram_tensor(name, list(shape), dt, kind="Internal")

    win_bf = dscr("win_bf", [128, KD, 3072], BF16).ap()
    GB_h = dscr("GBseq", [4, 8192], BF16)
    GA_h = dscr("GAseq", [8, 3, 384], BF16)
    oT_scr = [dscr(f"oT{i}", [D, S], BF16).ap() for i, S in enumerate(seq_lens)]
    wq_scr = dscr("wq_bf", [128, KD, 2048], BF16).ap()
    UT_scr = dscr("UTs", [128, 128, KD, 128], BF16).ap()
    V_scr = dscr("Vs", [16384, D], BF16).ap()
    dbg = {}
    if debug:
        for i, S in enumerate(seq_lens):
            dbg[f"dbg_x1_{i}"] = nc.dram_tensor(f"dbg_x1_{i}", [S, D], F32, kind="ExternalOutput").ap()

    with ExitStack() as st:
        S_ = Sched(nc, st)

        def sb(name, shape, dt):
            return st.enter_context(nc.sbuf_tensor(name, list(shape), dt))

        ident_f = sb("ident_f", [128, 128], F32)
        ident_b = sb("ident_b", [128, 128], BF16)
        anti_b = sb("anti_b", [128, 128], BF16)
        bo_b = sb("bo_b", [128, 128], BF16)
        ones_b = sb("ones_b", [128, 128], BF16)
        ones_f = sb("ones_f", [128, 128], F32)
        iota_f = sb("iota_f", [128, 128], F32)
        sel_f = sb("sel_f", [128, 64], F32)
        gq_a = sb("gq_a", [128, 1], F32)
        gk_a = sb("gk_a", [128, 1], F32)
        gq_b = sb("gq_b", [128, 1], F32)
        gk_b = sb("gk_b", [128, 1], F32)
        gdiff = sb("gdiff", [128, 1], F32)
        neglam = sb("neglam", [128, 1], F32)
        biasfar = sb("biasfar", [128, 4, 2], F32)
        gA = sb("gA", [128, KD], F32)
        gF = sb("gF", [128, KD], F32)
        T_const = Dep()
        psum = [st.enter_context(nc.psum_tensor(f"ps{i}", [128, 512], F32)) for i in range(8)]
        PS = [Dep() for _ in range(8)]
        PSA = [PS[4], PS[6], PS[7]]
        T_oTs = [Dep() for _ in seq_lens]
        T_y = Dep()
        stA = ExitStack()
        BMA = stA.enter_context(nc.sbuf_tensor('BMA', [128, 8, 3, 2, 128], BF16))

        with ExitStack() as s0:
            def sb0(name, shape, dt):
                return s0.enter_context(nc.sbuf_tensor(name, list(shape), dt))
            stage = sb0("stage0", [128, 4096], F32)
            T_stage = Dep()
            tmp = sb0("tmp0", [128, 512], F32)
            T_tmp = Dep()
            S_.dma("sp", ident_f[:], c_ident, writes=[T_const])
            S_.dma("sp", iota_f[:], c_iota, writes=[T_const])
            S_.dma("sp", sel_f[:], c_sel, writes=[T_const])
            S_.dma("sp", stage[:, 0:128], c_anti, writes=[T_stage])
            S_.dma("sp", stage[:, 128:256], c_bo, writes=[T_stage])
            S_.op("dve", lambda e: e.tensor_copy(out=ident_b[:], in_=ident_f[:]), reads=[T_const], writes=[T_const])
            S_.op("dve", lambda e: e.tensor_copy(out=anti_b[:], in_=stage[:, 0:128]), reads=[T_stage], writes=[T_const])
            S_.op("dve", lambda e: e.tensor_copy(out=bo_b[:], in_=stage[:, 128:256]), reads=[T_stage], writes=[T_const])
            S_.op("dve", lambda e: e.memset(ones_b[:], 1.0), writes=[T_const])
            S_.op("dve", lambda e: e.memset(ones_f[:], 1.0), writes=[T_const])
            T_g = Dep()
            graw = sb0("graw", [128, 8], F32)
            for j, src in enumerate((qn_a, kn_a, qn_b, kn_b)):
                for hh in range(2):
                    S_.dma("sp", graw[hh * 64:(hh + 1) * 64, j:j + 1], src.rearrange("(c o) -> c o", o=1), writes=[T_g])
            S_.dma("sp", graw[:, 4:5], dn_g.rearrange("(c o) -> c o", o=1), writes=[T_g])
            S_.op("dve", lambda e: e.tensor_scalar_mul(out=gq_a[:], in0=graw[:, 0:1], scalar1=0.125), reads=[T_g], writes=[T_const])
            S_.op("dve", lambda e: e.tensor_copy(out=gk_a[:], in_=graw[:, 1:2]), reads=[T_g], writes=[T_const])
            S_.op("dve", lambda e: e.tensor_scalar_mul(out=gq_b[:], in0=graw[:, 2:3], scalar1=0.125), reads=[T_g], writes=[T_const])
            S_.op("dve", lambda e: e.tensor_copy(out=gk_b[:], in_=graw[:, 3:4]), reads=[T_g], writes=[T_const])
            S_.op("dve", lambda e: e.tensor_scalar_mul(out=gdiff[:], in0=graw[:, 4:5], scalar1=0.8), reads=[T_g], writes=[T_const])
            S_.dma("sp", gA[:], attn_g.rearrange("(k p) -> p k", p=128), writes=[T_const], allow_slow_non_contiguous=True)
            S_.dma("sp", gF[:], ffn_g.rearrange("(k p) -> p k", p=128), writes=[T_const], allow_slow_non_contiguous=True)
            lam4 = sb0("lam4", [128, 4, 64], F32)
            T_l = Dep()
            for j, src in enumerate((lq1, lk1, lq2, lk2)):
                S_.dma("sp", lam4[:, j, :], src.partition_broadcast(128), writes=[T_l])
            lamw = sb0("lamw", [128, 8], F32)
            T_lw = Dep()
            prod = sb0("lprod", [128, 2, 64], F32)
            S_.op("dve", lambda e: e.tensor_tensor(out=prod[:, 0, :], in0=lam4[:, 0, :], in1=lam4[:, 1, :], op=ALU.mult), reads=[T_l], writes=[T_lw])
            S_.op("dve", lambda e: e.tensor_tensor(out=prod[:, 1, :], in0=lam4[:, 2, :], in1=lam4[:, 3, :], op=ALU.mult), reads=[T_l, T_lw], writes=[T_lw])
            S_.op("dve", lambda e: e.reduce_sum(out=lamw[:, 0:2], in_=prod[:], axis=AX.X), reads=[T_lw], writes=[T_lw])
            S_.op("act", lambda e: e.activation(out=lamw[:, 2:4], in_=lamw[:, 0:2], func=AF.Exp), reads=[T_lw], writes=[T_lw])
            S_.op("dve", lambda e: e.tensor_tensor(out=lamw[:, 4:5], in0=lamw[:, 3:4], in1=lamw[:, 2:3], op=ALU.subtract), reads=[T_lw], writes=[T_lw])
            S_.op("dve", lambda e: e.tensor_scalar_add(out=neglam[:], in0=lamw[:, 4:5], scalar1=-0.2), reads=[T_lw], writes=[T_const])
            for hb in range(4):
                for sg, row in enumerate((15, 31)):
                    S_.dma("sp", biasfar[:, hb, sg:sg + 1],
                           rel_bias[row:row + 1, 8 + hb:9 + hb].rearrange("a b -> (a b)").partition_broadcast(128), writes=[T_const])
            tabA = sb0("tabA", [33, 12], F32)
            T_tab = Dep()
            S_.op("dve", lambda e: e.memset(tabA[:], NEG), writes=[T_tab])
            S_.dma("sp", tabA[0:32, :], rel_bias, reads=[], writes=[T_tab])
            ohb_sb = sb0("ohb_sb", [32, 8192], F32)
            oha_sb = sb0("oha_sb", [33, 3, 384], F32)
            T_oh = Dep()
            S_.dma("sp", ohb_sb[:], c_ohb, writes=[T_oh])
            S_.dma("sp", oha_sb[:], c_oha.rearrange("d b i -> b d i"), writes=[T_oh])
            seqb = sb0("seqb", [8, 8192], BF16)
            T_sq = Dep()
            for g in range(16):
                S_.op("pe", lambda e, g=g: e.matmul(psum[g % 2][0:4, :], lhsT=tabA[0:32, 8:12], rhs=ohb_sb[:, g * 512:(g + 1) * 512], start=True, stop=True),
                      reads=[T_tab, T_oh], writes=[PS[g % 2]])
                S_.op("act", lambda e, g=g: e.copy(out=seqb[0:4, g * 512:(g + 1) * 512], in_=psum[g % 2][0:4, :]), reads=[PS[g % 2]], writes=[T_sq])
            T_GB = Dep()
            S_.dma("sp", GB_h.ap(), seqb[0:4, :], reads=[T_sq], writes=[T_GB])
            seqa = sb0("seqa", [8, 3, 384], BF16)
            T_sa = Dep()
            for di in range(3):
                S_.op("pe", lambda e, di=di: e.matmul(psum[2][0:8, 0:384], lhsT=tabA[0:33, 0:8], rhs=oha_sb[:, di, :], start=True, stop=True),
                      reads=[T_tab, T_oh], writes=[PS[2]])
                S_.op("act", lambda e, di=di: e.copy(out=seqa[:, di, :], in_=psum[2][0:8, 0:384]), reads=[PS[2]], writes=[T_sa])
            T_GA = Dep()
            S_.dma("sp", GA_h.ap(), seqa[:], reads=[T_sa], writes=[T_GA])
            for h in range(8):
                for di in range(3):
                    for c in range(2):
                        src = bass.AP(GA_h, (h * 3 + di) * 384 + 128 * (1 - c), [[1, 128], [1, 128]])
                        S_.dma("sp", BMA[:, h, di, c, :], src, reads=[T_GA], writes=[T_const])
            T_win = Dep()
            wst = sb0("wst", [128, KD, 512], BF16)
            T_wst = Dep()
            for cb in range(6):
                S_.dma("sp", stage[:].rearrange("p (k c) -> p k c", k=KD),
                       w_in[:, cb * 512:(cb + 1) * 512].rearrange("(k p) c -> p k c", p=128), writes=[T_stage])
                for k in range(KD):
                    S_.op("dve" if k % 2 else "pool", lambda e, k=k: e.tensor_scalar(out=wst[:, k, :], in0=stage[:, k * 512:(k + 1) * 512], scalar1=gA[:, k:k + 1], scalar2=None, op0=ALU.mult),
                          reads=[T_stage, T_const], writes=[T_wst])
                S_.dma("sp", win_bf[:, :, cb * 512:(cb + 1) * 512], wst[:], reads=[T_wst], writes=[T_win])
            T_UV = Dep()
            gFrow = sb0("gFrow", [128, D], F32)
            T_gfr = Dep()
            S_.dma("sp", gFrow[:], ffn_g.partition_broadcast(128), writes=[T_gfr])
            NB0 = 4
            ub = [sb0(f"ub{i}", [128, D], BF16) for i in range(NB0)]
            T_ub = [Dep() for _ in range(NB0)]
            ut = [sb0(f"ut{i}", [128, KD, 128], BF16) for i in range(NB0)]
            T_ut = [Dep() for _ in range(NB0)]
            vb16 = [sb0(f"vb16{i}", [128, D], BF16) for i in range(NB0)]
            T_vb = [Dep() for _ in range(NB0)]
            ust = [sb0(f"ust{i}", [128, D], F32) for i in range(NB0)]
            T_ust = [Dep() for _ in range(NB0)]
            vst = [sb0(f"vst{i}", [128, D], F32) for i in range(NB0)]
            T_vst = [Dep() for _ in range(NB0)]
            def ld_uv(i):
                b = i % NB0
                S_.dma("sp", ust[b][:], pu[i * 128:(i + 1) * 128, :], writes=[T_ust[b]])
                S_.dma("sp", vst[b][:], pv[i * 128:(i + 1) * 128, :], writes=[T_vst[b]])
            for i in range(min(NB0 - 1, n_exp_chunks)):
                ld_uv(i)
            for i in range(n_exp_chunks):
                b = i % NB0
                pb2 = 4 + (i % 2)
                if i + NB0 - 1 < n_exp_chunks:
                    ld_uv(i + NB0 - 1)
                S_.op("dve", lambda e, b=b: e.tensor_tensor(out=ub[b][:], in0=ust[b][:], in1=gFrow[:], op=ALU.mult),
                      reads=[T_ust[b], T_gfr], writes=[T_ub[b]])
                pst = psum[pb2][:].bitcast(BF16)
                S_.group("pe", [lambda e, k=k, b=b, pst=pst: e.transpose(out=pst[:, k * 128:(k + 1) * 128], in_=ub[b][:, k * 128:(k + 1) * 128], identity=ident_b[:]) for k in range(KD)],
                         reads=[T_ub[b], T_const], writes=[PS[pb2]])
                S_.op("act", lambda e, b=b, pst=pst: e.copy(out=ut[b][:].rearrange("p k e -> p (k e)"), in_=pst), reads=[PS[pb2]], writes=[T_ut[b]])
                S_.dma("sp", UT_scr[i], ut[b][:], reads=[T_ut[b]], writes=[T_UV])
                S_.op("pool", lambda e, b=b: e.tensor_copy(out=vb16[b][:], in_=vst[b][:]), reads=[T_vst[b]], writes=[T_vb[b]])
                S_.dma("sp", V_scr[i * 128:(i + 1) * 128, :], vb16[b][:], reads=[T_vb[b]], writes=[T_UV])

        S_.barrier()
        for si, S in enumerate(seq_lens):
            S_.barrier()
            NT = S // 128
            NG = S // 512
            with ExitStack() as s1:
                def sb1(name, shape, dt):
                    return s1.enter_context(nc.sbuf_tensor(f"{name}_{si}", list(shape), dt))
                xnT = sb1("xnT", [128, KD, S], BF16)
                T_xnT = Dep()
                xt = [sb1(f"xt{i}", [128, D], F32) for i in range(2)]
                T_xt = [Dep(), Dep()]
                junk = sb1("junk", [128, D], BF16)
                T_junk = Dep()
                xnb = [sb1(f"xnb{i}", [128, D], BF16) for i in range(2)]
                T_xnb = [Dep(), Dep()]
                st4 = sb1("st4", [128, 8], F32)
                T_st4 = Dep()
                for i in range(NT):
                    b = i % 2
                    S_.dma("sp", xt[b][:], xs[si][i * 128:(i + 1) * 128, :], writes=[T_xt[b]])
                    S_.op("act", lambda e, b=b: e.activation(out=junk[:], in_=xt[b][:], func=AF.Square, accum_out=st4[:, 0:1]),
                          reads=[T_xt[b]], writes=[T_junk, T_st4])
                    S_.op("act", lambda e: e.activation(out=st4[:, 1:2], in_=st4[:, 0:1], func=AF.Sqrt, scale=1.0 / D, bias=EPS), reads=[T_st4], writes=[T_st4])
                    S_.op("dve", lambda e: e.reciprocal(out=st4[:, 2:3], in_=st4[:, 1:2]), reads=[T_st4], writes=[T_st4])
                    S_.op("dve", lambda e, b=b: e.tensor_scalar(out=xnb[b][:], in0=xt[b][:], scalar1=st4[:, 2:3], scalar2=None, op0=ALU.mult),
                          reads=[T_xt[b], T_st4], writes=[T_xnb[b]])
                    pst = psum[b][:].bitcast(BF16)
                    S_.group("pe", [lambda e, k=k, b=b, pst=pst: e.transpose(out=pst[:, k * 128:(k + 1) * 128], in_=xnb[b][:, k * 128:(k + 1) * 128], identity=ident_b[:]) for k in range(KD)],
                             reads=[T_xnb[b], T_const], writes=[PS[b]])
                    S_.op("act", lambda e, i=i, pst=pst: e.copy(out=xnT[:, :, i * 128:(i + 1) * 128], in_=pst.rearrange("p (k t) -> p k t", k=KD)),
                          reads=[PS[b]], writes=[T_xnT])

                wsl = sb1("wsl", [128, KD, 384], BF16)
                T_wsl = Dep()
                qT = sb1("qT", [128, 2, S], BF16)
                kT = sb1("kT", [128, S], BF16)
                T_qT, T_kT = Dep(), Dep()
                sq = [sb1(f"sq{i}", [128, 512], BF16) for i in range(3)]
                T_sq2 = [Dep(), Dep(), Dep()]
                sd = [sb1(f"sd{i}", [128, 512], F32) for i in range(3)]
                T_sd = [Dep(), Dep(), Dep()]

                S_.op("pool", lambda e: e.memset(qT[:], 0.0), writes=[T_qT])

                def qk_proj(dst, T_dst, wcol, gain, split=False):
                    def f1(g):
                        b = g % 3
                        pa = psum[b]
                        S_.group("pe", [lambda e, k=k, g=g, pa=pa: e.matmul(pa[:], lhsT=wsl[:, k, wcol:wcol + 128], rhs=xnT[:, k, g * 512:(g + 1) * 512], start=(k == 0), stop=(k == KD - 1)) for k in range(KD)],
                                 reads=[T_wsl, T_xnT], writes=[PS[b]])
                        S_.op("act", lambda e, b=b, pa=pa: e.activation(out=sq[b][:], in_=pa[:], func=AF.Square), reads=[PS[b]], writes=[T_sq2[b]])

                    def f2(g):
                        b = g % 3
                        pa, pb_ = psum[b], psum[3 + b]
                        S_.op("pe", lambda e, b=b, pb_=pb_: e.matmul(pb_[:], lhsT=bo_b[:], rhs=sq[b][:], start=True, stop=True), reads=[T_sq2[b], T_const], writes=[PS[3 + b]])
                        S_.op("act", lambda e, b=b, pb_=pb_: e.activation(out=sd[b][:], in_=pb_[:], func=AF.Sqrt, scale=1.0 / 64, bias=EPS), reads=[PS[3 + b]], writes=[T_sd[b]])
                        S_.op("dve", lambda e, b=b: e.reciprocal(out=sd[b][:], in_=sd[b][:]), reads=[T_sd[b]], writes=[T_sd[b]])
                        if split:
                            for hh_ in range(2):
                                rs_ = slice(hh_ * 64, hh_ * 64 + 64)
                                S_.op("dve", lambda e, b=b, g=g, pa=pa, rs_=rs_, hh_=hh_: e.scalar_tensor_tensor(out=dst[rs_, hh_, g * 512:(g + 1) * 512], in0=pa[rs_, :], scalar=gain[rs_, 0:1], in1=sd[b][rs_, :], op0=ALU.mult, op1=ALU.mult),
                                      reads=[PS[b], T_sd[b], T_const], writes=[T_dst])
                        else:
                            S_.op("dve", lambda e, b=b, g=g, pa=pa: e.scalar_tensor_tensor(out=dst[:, g * 512:(g + 1) * 512], in0=pa[:], scalar=gain[:, 0:1], in1=sd[b][:], op0=ALU.mult, op1=ALU.mult),
                                  reads=[PS[b], T_sd[b], T_const], writes=[T_dst])
                    pipeline(range(NG), f1, f2, depth=2)

                dils = (1, 4, 16)
                with ExitStack() as s2:
                    def sb2(name, shape, dt):
                        return s2.enter_context(nc.sbuf_tensor(f"{name}_{si}", list(shape), dt))
                    Vp = [sb2(f"Vp{di}", [128, NT, 2, 65], BF16) for di in range(3)]
                    T_Vp = [Dep() for _ in range(3)]
                    acc = sb2("accA", [128, S], F32)
                    T_acc = Dep()
                    pT = [sb2(f"pTa{i}", [128, 2, 128], BF16) for i in range(4)]
                    T_pT = [Dep(), Dep(), Dep(), Dep()]
                    rden = sb2("rdenA", [64, 512], F32)
                    T_rden = Dep()
                    oTt = [sb2(f"oTtA{i}", [64, 512], BF16) for i in range(2)]
                    T_oTt = [Dep(), Dep()]
                    for di in range(3):
                        S_.op("pool", lambda e, di=di: e.memset(Vp[di][:, :, :, 64:65], 1.0), writes=[T_Vp[di]])
                    for hp in range(4):
                        for j, c0 in enumerate((hp * 128, 512 + hp * 128, 1024 + hp * 128)):
                            S_.dma("sp", wsl[:, :, j * 128:(j + 1) * 128], win_bf[:, :, c0:c0 + 128], reads=[T_win], writes=[T_wsl])
                        qk_proj(qT, T_qT, 0, gq_a, split=True)
                        qk_proj(kT, T_kT, 128, gk_a)
                        cnt = 0
                        for di, d in enumerate(dils):
                            L = S // d
                            ntc = L // 128
                            for r in range(d):
                                for j0 in range(0, ntc, 4):
                                    nj = min(4, ntc - j0)
                                    b = cnt % 2
                                    cnt += 1
                                    pv_ = psum[4 + b]
                                    fns = []
                                    for jj in range(nj):
                                        j = j0 + jj
                                        t0 = r + d * 128 * j
                                        for k in range(KD):
                                            fns.append(lambda e, k=k, jj=jj, t0=t0, d=d, pv_=pv_: e.matmul(
                                                pv_[:, jj * 128:(jj + 1) * 128], lhsT=xnT[:, k, t0:t0 + d * 127 + 1:d], rhs=wsl[:, k, 256:384],
                                                start=(k == 0), stop=(k == KD - 1)))
                                    S_.group("pe", fns, reads=[T_xnT, T_wsl], writes=[PS[4 + b]])
                                    ti = r * ntc + j0
                                    S_.op("act", lambda e, di=di, ti=ti, nj=nj, pv_=pv_: e.copy(
                                        out=Vp[di][:, ti:ti + nj, :, 0:64],
                                        in_=pv_[:, 0:nj * 128].rearrange("p (j h c) -> p j h c", j=nj, h=2)),
                                        reads=[PS[4 + b]], writes=[T_Vp[di]])
                        for hh in range(2):
                            h = hp * 2 + hh
                            ro = slice(hh * 64, hh * 64 + 64)
                            allb = []
                            for di, d in enumerate(dils):
                                L = S // d
                                ntc = L // 128
                                for r in range(d):
                                    blocks = [(0, 64, [(0, 1, 64)])]
                                    for bb in range(ntc - 1):
                                        blocks.append((128 * bb + 64, 128, [(bb, 0, 0), (bb + 1, 1, 0)]))
                                    blocks.append((L - 64, 64, [(ntc - 1, 0, 0)]))
                                    for (qm0, nq, kts) in blocks:
                                        allb.append((len(allb) % 4, di, d, r, ntc, qm0, nq, kts))

                            def fA1(it):
                                b, di, d, r, ntc, qm0, nq, kts = it
                                ps_s = psum[b]
                                q0 = r + d * qm0
                                qsl = slice(q0, q0 + d * (nq - 1) + 1, d)
                                fns = []
                                for ci, (kt, ch, c0) in enumerate(kts):
                                    k0 = r + d * 128 * kt
                                    fns.append(lambda e, ci=ci, k0=k0, d=d, qsl=qsl, nq=nq, ps_s=ps_s: e.matmul(
                                        ps_s[:, ci * 128:ci * 128 + nq], lhsT=kT[:, k0:k0 + d * 127 + 1:d], rhs=qT[:, hh, qsl], start=True, stop=False))
                                    fns.append(lambda e, ci=ci, ch=ch, c0=c0, nq=nq, di=di, ps_s=ps_s: e.matmul(
                                        ps_s[:, ci * 128:ci * 128 + nq], lhsT=anti_b[:], rhs=BMA[:, h, di, ch, c0:c0 + nq], start=False, stop=True))
                                S_.group("pe", fns, reads=[T_qT, T_kT, T_const], writes=[PS[b]])
                                if nq == 128:
                                    S_.op("act", lambda e, b=b, ps_s=ps_s: e.activation(out=pT[b][:].rearrange("p c q -> p (c q)"), in_=ps_s[:, 0:256], func=AF.Exp),
                                          reads=[PS[b]], writes=[T_pT[b]])
                                else:
                                    S_.op("act", lambda e, b=b, ps_s=ps_s: e.activation(out=pT[b][:, 0, 0:64], in_=ps_s[:, 0:64], func=AF.Exp),
                                          reads=[PS[b]], writes=[T_pT[b]])

                            def fA2(it):
                                b, di, d, r, ntc, qm0, nq, kts = it
                                ps_o = psum[4 + b]
                                q0 = r + d * qm0
                                qsl = slice(q0, q0 + d * (nq - 1) + 1, d)
                                nk = len(kts)
                                fns = []
                                for ci, (kt, ch, c0) in enumerate(kts):
                                    ti = r * ntc + kt
                                    fns.append(lambda e, ci=ci, ti=ti, di=di, nq=nq, b=b, nk=nk, ps_o=ps_o: e.matmul(
                                        ps_o[0:65, 0:nq], lhsT=Vp[di][:, ti, hh, :], rhs=pT[b][:, ci, 0:nq], start=(ci == 0), stop=(ci == nk - 1)))
                                S_.group("pe", fns, reads=[T_Vp[di], T_pT[b]], writes=[PS[4 + b]])
                                if di == 0:
                                    S_.op("dve", lambda e, qsl=qsl, nq=nq, ps_o=ps_o: e.tensor_copy(out=acc[0:65, qsl], in_=ps_o[0:65, 0:nq]),
                                          reads=[PS[4 + b]], writes=[T_acc])
                                else:
                                    S_.op("dve", lambda e, qsl=qsl, nq=nq, ps_o=ps_o: e.tensor_tensor(out=acc[0:65, qsl], in0=ps_o[0:65, 0:nq], in1=acc[0:65, qsl], op=ALU.add),
                                          reads=[PS[4 + b], T_acc], writes=[T_acc])
                            pipeline(allb, fA1, fA2, depth=3)
                            for g in range(NG):
                                b = g % 2
                                pd = psum[4 + b]
                                S_.op("pe", lambda e, g=g, pd=pd: e.matmul(pd[0:64, :], lhsT=sel_f[0:65, :], rhs=acc[0:65, g * 512:(g + 1) * 512], start=True, stop=True),
                                      reads=[T_acc, T_const], writes=[PS[4 + b]])
                                S_.op("dve", lambda e, pd=pd: e.reciprocal(out=rden[:], in_=pd[0:64, :]), reads=[PS[4 + b]], writes=[T_rden])
                                S_.op("dve", lambda e, g=g, b=b: e.tensor_tensor(out=oTt[b][:], in0=acc[0:64, g * 512:(g + 1) * 512], in1=rden[:], op=ALU.mult),
                                      reads=[T_acc, T_rden], writes=[T_oTt[b]])
                                S_.dma("pool", oT_scr[si][h * 64:(h + 1) * 64, g * 512:(g + 1) * 512], oTt[b][:], reads=[T_oTt[b]], writes=[T_oTs[si]])

                S_.barrier()
                with ExitStack() as s3:
                    def sb3(name, shape, dt):
                        return s3.enter_context(nc.sbuf_tensor(f"{name}_{si}", list(shape), dt))
                    offs = list(range(-640, 1025, 128))
                    BMB = sb3("BMB", [128, len(offs), 512], BF16)
                    T_BMB = Dep()
                    vB = sb3("vB", [128, NT, 128], BF16)
                    T_vB = Dep()
                    pTb = [sb3(f"pTb{i}", [128, 512], BF16) for i in range(4)]
                    T_pTb = [Dep() for _ in range(4)]
                    fa = [sb3(f"fa{i}", [128, 512], F32) for i in range(4)]
                    T_fa = [Dep() for _ in range(4)]
                    oTb = [sb3(f"oTb{i}", [128, 512], BF16) for i in range(2)]
                    T_oTb = [Dep(), Dep()]
                    for hb in range(4):
                        for j, c0 in enumerate((1536 + hb * 128, 2048 + hb * 128, 2560 + hb * 128)):
                            S_.dma("sp", wsl[:, :, j * 128:(j + 1) * 128], win_bf[:, :, c0:c0 + 128], reads=[T_win], writes=[T_wsl])
                        for oi, off in enumerate(offs):
                            base = RB - off - 127
                            if base < 0 or base + 127 + 511 >= 8192:
                                continue
                            src = bass.AP(GB_h, hb * 8192 + base, [[1, 128], [1, 512]])
                            S_.dma("sp", BMB[:, oi, :], src, reads=[T_GB], writes=[T_BMB])
                        qk_proj(qT, T_qT, 0, gq_b, split=True)
                        qk_proj(kT, T_kT, 128, gk_b)
                        for j0 in range(0, NT, 4):
                            b = (j0 // 4) % 2
                            pv_ = psum[4 + b]
                            fns = []
                            for jj in range(4):
                                j = j0 + jj
                                for k in range(KD):
                                    fns.append(lambda e, k=k, jj=jj, j=j, pv_=pv_: e.matmul(
                                        pv_[:, jj * 128:(jj + 1) * 128], lhsT=xnT[:, k, j * 128:(j + 1) * 128], rhs=wsl[:, k, 256:384],
                                        start=(k == 0), stop=(k == KD - 1)))
                            S_.group("pe", fns, reads=[T_xnT, T_wsl], writes=[PS[4 + b]])
                            S_.op("act", lambda e, j0=j0, pv_=pv_: e.copy(out=vB[:, j0:j0 + 4, :], in_=pv_[:].rearrange("p (j c) -> p j c", j=4)),
                                  reads=[PS[4 + b]], writes=[T_vB])
                        scnt = 0
                        for qg in range(NG):
                            itemsB = []
                            for kc in range(NT):
                                for mp in range(2):
                                    itemsB.append((kc, mp, scnt % 4))
                                    scnt += 1

                            def fB1(it, qg=qg):
                                kc, mp, sl = it
                                off = kc * 128 - qg * 512
                                near = not (off + 127 <= -FAR or off - 511 >= FAR)
                                ps_s = psum[sl]
                                rr = slice(mp * 64, mp * 64 + 64)
                                fns = [lambda e, rr=rr, kc=kc, qg=qg, ps_s=ps_s, near=near: e.matmul(
                                    ps_s[:], lhsT=kT[:, kc * 128:(kc + 1) * 128], rhs=qT[:, mp, qg * 512:(qg + 1) * 512], start=True, stop=not near)]
                                rds = [T_qT, T_kT]
                                if near:
                                    oi = offs.index(off)
                                    fns.append(lambda e, oi=oi, ps_s=ps_s: e.matmul(ps_s[:], lhsT=anti_b[:], rhs=BMB[:, oi, :], start=False, stop=True))
                                    rds += [T_BMB, T_const]
                                S_.group("pe", fns, reads=rds, writes=[PS[sl]])
                                if near:
                                    S_.op("act", lambda e, sl=sl, ps_s=ps_s: e.activation(out=pTb[sl][:], in_=ps_s[:], func=AF.Exp), reads=[PS[sl]], writes=[T_pTb[sl]])
                                else:
                                    sg = 0 if off < 0 else 1
                                    S_.op("act", lambda e, sl=sl, ps_s=ps_s, sg=sg: e.activation(out=pTb[sl][:], in_=ps_s[:], func=AF.Exp, bias=biasfar[:, hb, sg:sg + 1]),
                                          reads=[PS[sl], T_const], writes=[T_pTb[sl]])

                            def fB2(it):
                                kc, mp, sl = it
                                S_.group("pe", [
                                    lambda e, sl=sl, kc=kc, mp=mp: e.matmul(psum[4 + 2 * mp][:], lhsT=vB[:, kc, :], rhs=pTb[sl][:], start=(kc == 0), stop=(kc == NT - 1)),
                                    lambda e, sl=sl, kc=kc, mp=mp: e.matmul(psum[5 + 2 * mp][:], lhsT=ones_b[:], rhs=pTb[sl][:], start=(kc == 0), stop=(kc == NT - 1)),
                                ], reads=[T_vB, T_pTb[sl], T_const], writes=[PS[4 + 2 * mp], PS[5 + 2 * mp]])
                            pipeline(itemsB, fB1, fB2, depth=3)
                            S_.op("dve", lambda e: e.reciprocal(out=fa[0][:], in_=psum[5][:]), reads=[PS[5]], writes=[T_fa[0]])
                            S_.op("dve", lambda e: e.tensor_tensor(out=fa[1][:], in0=psum[4][:], in1=fa[0][:], op=ALU.mult), reads=[PS[4], T_fa[0]], writes=[T_fa[1]])
                            S_.op("dve", lambda e: e.reciprocal(out=fa[0][:], in_=psum[7][:]), reads=[PS[7], T_fa[0]], writes=[T_fa[0]])
                            S_.op("dve", lambda e: e.tensor_tensor(out=fa[2][:], in0=psum[6][:], in1=fa[0][:], op=ALU.mult), reads=[PS[6], T_fa[0]], writes=[T_fa[2]])
                            S_.op("dve", lambda e: e.scalar_tensor_tensor(out=fa[3][:], in0=fa[2][:], scalar=neglam[:, 0:1], in1=fa[1][:], op0=ALU.mult, op1=ALU.add),
                                  reads=[T_fa[1], T_fa[2], T_const], writes=[T_fa[3]])
                            S_.op("act", lambda e: e.activation(out=fa[1][:], in_=fa[3][:], func=AF.Square), reads=[T_fa[3], T_fa[1]], writes=[T_fa[1]])
                            S_.op("pe", lambda e: e.matmul(psum[0][:], lhsT=ones_f[:], rhs=fa[1][:], start=True, stop=True), reads=[T_fa[1], T_const], writes=[PS[0]])
                            S_.op("act", lambda e: e.activation(out=fa[2][:], in_=psum[0][:], func=AF.Sqrt, scale=1.0 / 128, bias=EPS), reads=[PS[0], T_fa[2]], writes=[T_fa[2]])
                            S_.op("dve", lambda e: e.reciprocal(out=fa[2][:], in_=fa[2][:]), reads=[T_fa[2]], writes=[T_fa[2]])
                            ob = qg % 2
                            S_.op("dve", lambda e, ob=ob: e.scalar_tensor_tensor(out=oTb[ob][:], in0=fa[3][:], scalar=gdiff[:, 0:1], in1=fa[2][:], op0=ALU.mult, op1=ALU.mult),
                                  reads=[T_fa[3], T_fa[2], T_const], writes=[T_oTb[ob]])
                            S_.dma("pool", oT_scr[si][512 + hb * 128:512 + (hb + 1) * 128, qg * 512:(qg + 1) * 512], oTb[ob][:], reads=[T_oTb[ob]], writes=[T_oTs[si]])

        S_.barrier()
        stA.close()
        with ExitStack() as s4:
            def sb4(name, shape, dt):
                return s4.enter_context(nc.sbuf_tensor(name, list(shape), dt))
            wout_b = sb4("wout_b", [128, KD, D], BF16)
            wqs = [sb4(f"wqs{i}", [128, KD, 128], BF16) for i in range(2)]
            T_wqs = [Dep(), Dep()]
            T_wqscr = Dep()
            KzT = sb4("KzT", [128, 2, 128], BF16)
            T_w4 = Dep()
            s4t = ExitStack()
            stg = s4t.enter_context(nc.sbuf_tensor("stg4", [128, KD, 512], F32))
            T_stg = Dep()
            wqst = s4t.enter_context(nc.sbuf_tensor("wqst", [128, KD, 512], BF16))
            T_wqst = Dep()
            for cb in range(2):
                S_.dma("sp", stg[:], w_out[:, cb * 512:(cb + 1) * 512].rearrange("(k p) c -> p k c", p=128), writes=[T_stg])
                S_.op("dve", lambda e, cb=cb: e.tensor_copy(out=wout_b[:, :, cb * 512:(cb + 1) * 512], in_=stg[:]), reads=[T_stg], writes=[T_w4])
            for cb in range(4):
                S_.dma("sp", stg[:], w_q[:, cb * 512:(cb + 1) * 512].rearrange("(k p) c -> p k c", p=128), writes=[T_stg])
                for k in range(KD):
                    S_.op("dve", lambda e, k=k, cb=cb: e.tensor_scalar(out=wqst[:, k, :], in0=stg[:, k, :], scalar1=gF[:, k:k + 1], scalar2=None, op0=ALU.mult),
                          reads=[T_stg, T_const], writes=[T_wqst])
                S_.dma("sp", wq_scr[:, :, cb * 512:(cb + 1) * 512], wqst[:], reads=[T_wqst], writes=[T_wqscr])
            for z in range(2):
                S_.dma("sp", stg[:, 0, 0:128], subk[z], writes=[T_stg])
                S_.op("pe", lambda e: e.transpose(out=psum[0][:, 0:128], in_=stg[:, 0, 0:128], identity=ident_f[:]), reads=[T_stg, T_const], writes=[PS[0]])
                S_.op("act", lambda e, z=z: e.copy(out=KzT[:, z, :], in_=psum[0][:, 0:128]), reads=[PS[0]], writes=[T_w4])

            S_.barrier()
            s4t.close()
            TB = 256
            oTl = sb4("oTl", [128, KD, TB], BF16)
            T_oTl = Dep()
            xt4 = sb4("xt4", [128, D], F32)
            T_xt4 = Dep()
            x1 = [[sb4(f"x1_{bf}_{i}", [128, D], F32) for i in range(2)] for bf in range(2)]
            T_x1 = [[Dep(), Dep()] for _ in range(2)]
            st5 = sb4("st5", [128, 8], F32)
            T_st5 = Dep()
            S_.op("pool", lambda e: e.memset(st5[:, 4:5], -0.5), writes=[T_st5])
            xtmp = sb4("xtmp", [128, 512], F32)
            T_xtmp = Dep()
            hnb = sb4("hnb", [128, D], BF16)
            T_hnb = Dep()
            hT = [sb4(f"hT{bf}", [128, KD, TB], BF16) for bf in range(2)]
            T_hT = [Dep(), Dep()]
            qpT = sb4("qpT", [128, 16, TB], BF16)
            T_qpT = Dep()
            sc = [sb4(f"sc{i}", [128, 2048], F32) for i in range(2)]
            T_sc = [[Dep() for _ in range(4)] for _ in range(2)]
            sc2 = sb4("sc2", [128, 128], F32)
            T_sc2 = Dep()
            top = sb4("top", [128, 16, 16], F32)
            idx = sb4("idx", [128, 16, 16], U32)
            idxf = sb4("idxf", [128, 16, 16], F32)
            T_top, T_idx = Dep(), Dep()
            T_cand = Dep()
            cand2 = sb4("cand2", [128, 256], F32)
            T_cand2 = Dep()
            best = sb4("best", [128, 8, 16], F32)
            pos = sb4("pos", [128, 8, 16], U32)
            k12 = sb4("k12", [128, 2, 8, 16], U32)
            k12f = sb4("k12f", [128, 2, 8, 16], F32)
            T_best, T_pos, T_k12 = Dep(), Dep(), Dep()
            T_eq = T_cand
            IJg2 = sb4("IJg", [128, 2, 3, 128], F32)
            T_IJg = Dep()
            gw = sb4("gw", [128, 8, 16], F32)
            zz = sb4("zz", [128, 8], F32)
            T_gw = Dep()
            IJgT = sb4("IJgT", [128, 3, TB], F32)
            T_IJgT = Dep()
            NRING = 12
            AB = sb4("AB", [128, NRING, 2, 128], BF16)
            T_AB = [Dep() for _ in range(NRING)]
            T_BB = [Dep() for _ in range(NRING)]
            Gsb = sb4("Gsb", [128, TB, 128], BF16)
            T_G = Dep()
            NUV = 8
            UTb = [sb4(f"UTb{i}", [128, KD, 128], BF16) for i in range(NUV)]
            Vb = [sb4(f"Vb{i}", [128, D], BF16) for i in range(NUV)]
            T_UTb = [Dep() for _ in range(NUV)]
            T_Vb = [Dep() for _ in range(NUV)]
            asb = [sb4(f"asb{i}", [128, TB], BF16) for i in range(3)]
            wsb = [sb4(f"wsb{i}", [128, TB], BF16) for i in range(3)]
            T_asb = [Dep(), Dep(), Dep()]
            T_wsb = [Dep(), Dep(), Dep()]
            yt = sb4("yt", [128, 512], F32)
            T_yt = Dep()
            iota16 = iota_f[:, 0:16]

            blocks = [(si, blk * TB) for si, S in enumerate(seq_lens) for blk in range(S // TB)]

            def gen_pe(bi, pbks):
                si, t0 = blocks[bi]
                bf = bi % 2
                cnt_ = [0]

                def nb_():
                    cnt_[0] += 1
                    return pbks[cnt_[0] % len(pbks)]
                S_.dma("sp", oTl[:], oT_scr[si][:, t0:t0 + TB].rearrange("(k p) t -> p k t", p=128), reads=[T_oTs[si]], writes=[T_oTl])
                for t2 in range(2):
                    S_.dma("sp", xt4[:], xs[si][t0 + t2 * 128:t0 + (t2 + 1) * 128, :], writes=[T_xt4])
                    for hf in range(2):
                        pbk = nb_()
                        S_.group("pe", [lambda e, k=k, t2=t2, hf=hf, pbk=pbk: e.matmul(psum[pbk][:], lhsT=oTl[:, k, t2 * 128:(t2 + 1) * 128], rhs=wout_b[:, k, hf * 512:(hf + 1) * 512], start=(k == 0), stop=(k == KD - 1)) for k in range(KD)],
                                 reads=[T_oTl, T_w4], writes=[PS[pbk]])
                        S_.op("act", lambda e, pbk=pbk: e.copy(out=xtmp[:], in_=psum[pbk][:]), reads=[PS[pbk]], writes=[T_xtmp])
                        S_.op("pool", lambda e, t2=t2, hf=hf: e.tensor_tensor(out=x1[bf][t2][:, hf * 512:(hf + 1) * 512], in0=xtmp[:], in1=xt4[:, hf * 512:(hf + 1) * 512], op=ALU.add),
                              reads=[T_xtmp, T_xt4], writes=[T_x1[bf][t2]])
                        yield
                    if debug:
                        S_.dma("pool", dbg[f"dbg_x1_{si}"][t0 + t2 * 128:t0 + (t2 + 1) * 128, :], x1[bf][t2][:], reads=[T_x1[bf][t2]], writes=[Dep()])
                    S_.op("act", lambda e, t2=t2: e.activation(out=hnb[:], in_=x1[bf][t2][:], func=AF.Square, accum_out=st5[:, 0:1]), reads=[T_x1[bf][t2]], writes=[T_hnb, T_st5])
                    S_.op("pool", lambda e: e.tensor_scalar(out=st5[:, 1:2], in0=st5[:, 0:1], scalar1=1.0 / D, scalar2=EPS, op0=ALU.mult, op1=ALU.add), reads=[T_st5], writes=[T_st5])
                    S_.op("pool", lambda e: e.tensor_tensor(out=st5[:, 2:3], in0=st5[:, 1:2], in1=st5[:, 4:5], op=ALU.pow), reads=[T_st5], writes=[T_st5])
                    S_.op("act", lambda e, t2=t2: e.activation(out=hnb[:], in_=x1[bf][t2][:], func=AF.Copy, scale=st5[:, 2:3]), reads=[T_x1[bf][t2], T_st5], writes=[T_hnb])
                    yield
                    pbk = nb_()
                    pst = psum[pbk][:].bitcast(BF16)
                    S_.group("pe", [lambda e, k=k, pst=pst: e.transpose(out=pst[:, k * 128:(k + 1) * 128], in_=hnb[:, k * 128:(k + 1) * 128], identity=ident_b[:]) for k in range(KD)],
                             reads=[T_hnb, T_const], writes=[PS[pbk]])
                    S_.op("act", lambda e, t2=t2, pst=pst: e.copy(out=hT[bf][:, :, t2 * 128:(t2 + 1) * 128], in_=pst.rearrange("p (k t) -> p k t", k=KD)), reads=[PS[pbk]], writes=[T_hT[bf]])
                    yield
                for pz in range(16):
                    b = pz % 2
                    pbk = nb_()
                    S_.dma("sp", wqs[b][:], wq_scr[:, :, pz * 128:(pz + 1) * 128], reads=[T_wqscr], writes=[T_wqs[b]])
                    S_.group("pe", [lambda e, k=k, pz=pz, b=b, pbk=pbk: e.matmul(psum[pbk][:, 0:TB], lhsT=wqs[b][:, k, :], rhs=hT[bf][:, k, :], start=(k == 0), stop=(k == KD - 1)) for k in range(KD)],
                             reads=[T_hT[bf], T_wqs[b]], writes=[PS[pbk]])
                    S_.op("act", lambda e, pz=pz, b=b, pbk=pbk: e.copy(out=qpT[:, pz, :], in_=psum[pbk][:, 0:TB]), reads=[PS[pbk]], writes=[T_qpT])
                    yield
                for t2 in range(2):
                    for bk in range(4):
                        pbk = nb_()
                        S_.group("pe", [lambda e, pz=pz, t2=t2, pbk=pbk: e.matmul(psum[pbk][:, (pz % 4) * 128:(pz % 4 + 1) * 128], lhsT=qpT[:, pz, t2 * 128:(t2 + 1) * 128], rhs=KzT[:, pz % 2, :], start=True, stop=True) for pz in range(bk * 4, bk * 4 + 4)],
                                 reads=[T_qpT, T_w4], writes=[PS[pbk]])
                        S_.op("act", lambda e, t2=t2, bk=bk, pbk=pbk: e.copy(out=sc[t2][:, bk * 512:(bk + 1) * 512], in_=psum[pbk][:]), reads=[PS[pbk]], writes=[T_sc[t2][bk], T_cand])
                        yield

            def gen_dve(bi):
                si, t0 = blocks[bi]
                bf = bi % 2
                for t2 in range(2):
                    for bk in range(4):
                        for g in range(bk * 4, bk * 4 + 4):
                            gs = slice(g * 128, (g + 1) * 128)
                            S_.op("dve", lambda e, g=g, gs=gs, t2=t2: e.max(out=top[:, g, 0:8], in_=sc[t2][:, gs]), reads=[T_sc[t2][bk]], writes=[T_top])
                            S_.op("dve", lambda e, g=g, gs=gs, t2=t2: e.max_index(out=idx[:, g, 0:8], in_max=top[:, g, 0:8], in_values=sc[t2][:, gs]), reads=[T_sc[t2][bk], T_top], writes=[T_idx])
                            S_.op("dve", lambda e, g=g, gs=gs, t2=t2: e.match_replace(out=sc2[:], in_to_replace=top[:, g, 0:8], in_values=sc[t2][:, gs], imm_value=-1e30), reads=[T_sc[t2][bk], T_top], writes=[T_sc2])
                            yield
                            S_.op("dve", lambda e, g=g: e.max(out=top[:, g, 8:16], in_=sc2[:]), reads=[T_sc2], writes=[T_top])
                            S_.op("dve", lambda e, g=g: e.max_index(out=idx[:, g, 8:16], in_max=top[:, g, 8:16], in_values=sc2[:]), reads=[T_sc2, T_top], writes=[T_idx])
                            yield
                    S_.op("dve", lambda e: e.tensor_copy(out=idxf[:], in_=idx[:]), reads=[T_idx], writes=[T_idx])
                    t4v = top[:].rearrange("p (h z) k -> p h z k", z=2)
                    i4v = idxf[:].rearrange("p (h z) k -> p h z k", z=2)
                    cand = sc[t2][:].rearrange("p (h c) -> p h c", h=8)
                    eq = sc[t2][:].rearrange("p (h s k) -> p h s k", h=8, s=16)
                    c4 = sc[t2][:].rearrange("p (h a b) -> p h a b", h=8, a=16)
                    S_.op("dve", lambda e: e.tensor_tensor(out=c4, in0=t4v[:, :, 0, :].unsqueeze(3).to_broadcast([128, 8, 16, 16]), in1=t4v[:, :, 1, :].unsqueeze(2).to_broadcast([128, 8, 16, 16]), op=ALU.add),
                          reads=[T_top], writes=[T_cand] + T_sc[t2])
                    yield
                    for p in range(8):
                        S_.op("dve", lambda e, p=p: e.max(out=best[:, p, 0:8], in_=cand[:, p, :]), reads=[T_cand], writes=[T_best])
                        S_.op("dve", lambda e, p=p: e.max_index(out=pos[:, p, 0:8], in_max=best[:, p, 0:8], in_values=cand[:, p, :]), reads=[T_cand, T_best], writes=[T_pos])
                        S_.op("dve", lambda e, p=p: e.match_replace(out=cand2[:], in_to_replace=best[:, p, 0:8], in_values=cand[:, p, :], imm_value=-1e30), reads=[T_cand, T_best], writes=[T_cand2])
                        yield
                        S_.op("dve", lambda e, p=p: e.max(out=best[:, p, 8:16], in_=cand2[:]), reads=[T_cand2], writes=[T_best])
                        S_.op("dve", lambda e, p=p: e.max_index(out=pos[:, p, 8:16], in_max=best[:, p, 8:16], in_values=cand2[:]), reads=[T_cand2, T_best], writes=[T_pos])
                        yield
                    S_.op("dve", lambda e: e.tensor_tensor(out=gw[:], in0=best[:], in1=best[:, :, 0:1].to_broadcast([128, 8, 16]), op=ALU.subtract), reads=[T_best], writes=[T_gw])
                    S_.op("act", lambda e: e.activation(out=gw[:], in_=gw[:], func=AF.Exp), reads=[T_gw], writes=[T_gw])
                    S_.op("dve", lambda e: e.reduce_sum(out=zz[:], in_=gw[:], axis=AX.X), reads=[T_gw], writes=[T_gw])
                    S_.op("dve", lambda e: e.reciprocal(out=zz[:], in_=zz[:]), reads=[T_gw], writes=[T_gw])
                    S_.op("dve", lambda e, t2=t2: e.tensor_tensor(out=IJg2[:, t2, 2, :].rearrange("p (h s) -> p h s", h=8), in0=gw[:], in1=zz[:].unsqueeze(2).to_broadcast([128, 8, 16]), op=ALU.mult),
                          reads=[T_gw], writes=[T_IJg])
                    yield
                    S_.op("dve", lambda e: e.tensor_single_scalar(out=k12[:, 0], in_=pos[:], scalar=4, op=ALU.logical_shift_right), reads=[T_pos], writes=[T_k12])
                    S_.op("dve", lambda e: e.tensor_single_scalar(out=k12[:, 1], in_=pos[:], scalar=15, op=ALU.bitwise_and), reads=[T_pos, T_k12], writes=[T_k12])
                    S_.op("dve", lambda e: e.tensor_copy(out=k12f[:], in_=k12[:]), reads=[T_k12], writes=[T_k12])
                    yield
                    for z in range(2):
                        S_.op("dve", lambda e, z=z: e.tensor_tensor(out=eq[:], in0=k12f[:, z].unsqueeze(3).to_broadcast([128, 8, 16, 16]),
                                                                      in1=iota16.unsqueeze(1).unsqueeze(1).to_broadcast([128, 8, 16, 16]), op=ALU.is_equal),
                              reads=[T_k12, T_const, T_eq], writes=[T_eq])
                        yield
                        S_.op("dve", lambda e, z=z: e.tensor_tensor(out=eq[:], in0=eq[:], in1=i4v[:, :, z, :].unsqueeze(2).to_broadcast([128, 8, 16, 16]), op=ALU.mult),
                              reads=[T_eq, T_idx], writes=[T_eq])
                        yield
                        S_.op("dve", lambda e, z=z, t2=t2: e.reduce_sum(out=IJg2[:, t2, z, :], in_=eq[:].rearrange("p h s k -> p (h s) k"), axis=AX.X), reads=[T_eq], writes=[T_IJg])
                        yield

            def emit_B6(gpe=None):
                for t2 in range(2):
                    S_.group("pe", [lambda e, j=j, t2=t2: e.transpose(out=psum[5][:, j * 128:(j + 1) * 128], in_=IJg2[:, t2, j, :], identity=ident_f[:]) for j in range(3)],
                             reads=[T_IJg, T_const], writes=[PS[5]])
                    S_.op("act", lambda e, t2=t2: e.copy(out=IJgT[:, :, t2 * 128:(t2 + 1) * 128], in_=psum[5][:, 0:384].rearrange("p (j t) -> p j t", j=3)), reads=[PS[5]], writes=[T_IJgT])
                for tq in range(TB // 4):
                    gb = 5 + (tq % 2)
                    for tt in range(4):
                        t = tq * 4 + tt
                        rs = t % NRING
                        S_.op("dve", lambda e, t=t, rs=rs: e.tensor_scalar(out=AB[:, rs, 0, :], in0=iota_f[:], scalar1=IJgT[:, 0, t:t + 1], scalar2=IJgT[:, 2, t:t + 1], op0=ALU.is_equal, op1=ALU.mult),
                              reads=[T_IJgT, T_const], writes=[T_AB[rs]])
                        S_.op("dve", lambda e, t=t, rs=rs: e.tensor_scalar(out=AB[:, rs, 1, :], in0=iota_f[:], scalar1=IJgT[:, 1, t:t + 1], scalar2=None, op0=ALU.is_equal),
                              reads=[T_IJgT, T_const], writes=[T_BB[rs]])
                        S_.op("pe", lambda e, tt=tt, rs=rs, gb=gb: e.matmul(psum[gb][:, tt * 128:(tt + 1) * 128], lhsT=AB[:, rs, 1, :], rhs=AB[:, rs, 0, :], start=True, stop=True),
                              reads=[T_AB[rs], T_BB[rs]], writes=[PS[gb]])
                    S_.op("act", lambda e, tq=tq, gb=gb: e.copy(out=Gsb[:, tq * 4:tq * 4 + 4, :].rearrange("j t i -> j (t i)"), in_=psum[gb][:]),
                          reads=[PS[gb]], writes=[T_G])
                    if gpe is not None:
                        next(gpe, None)
                if gpe is not None:
                    for _ in gpe:
                        pass

            def emit_B7(bi, filler):
                bf = bi % 2

                def f71(i):
                    u = i % NUV
                    a = i % 3
                    S_.dma("sp", UTb[u][:], UT_scr[i], reads=[T_UV], writes=[T_UTb[u]])
                    S_.dma("sp", Vb[u][:], V_scr[i * 128:(i + 1) * 128, :], reads=[T_UV], writes=[T_Vb[u]])
                    pa_ = psum[(4, 6, 7)[a]][:, 0:TB]
                    S_.group("pe", [lambda e, k=k, u=u, pa_=pa_: e.matmul(pa_, lhsT=UTb[u][:, k, :], rhs=hT[bf][:, k, :], start=(k == 0), stop=(k == KD - 1)) for k in range(KD)],
                             reads=[T_UTb[u], T_hT[bf]], writes=[PSA[a]])
                    S_.op("act", lambda e, a=a, pa_=pa_: e.activation(out=asb[a][:], in_=pa_, func=AF.Gelu), reads=[PSA[a]], writes=[T_asb[a]])
                    S_.op("pool", lambda e, a=a, i=i: e.tensor_tensor(out=wsb[a][:], in0=asb[a][:], in1=Gsb[:, :, i], op=ALU.mult), reads=[T_asb[a], T_G], writes=[T_wsb[a]])

                def f72(i):
                    u = i % NUV
                    a = i % 3
                    S_.group("pe", [lambda e, t2=t2, hf=hf, a=a, u=u, i=i: e.matmul(psum[t2 * 2 + hf][:], lhsT=wsb[a][:, t2 * 128:(t2 + 1) * 128], rhs=Vb[u][:, hf * 512:(hf + 1) * 512], start=(i == 0), stop=(i == n_exp_chunks - 1)) for t2 in range(2) for hf in range(2)],
                             reads=[T_wsb[a], T_Vb[u]], writes=[PS[0], PS[1], PS[2], PS[3]])
                    if filler is not None:
                        next(filler, None)
                pipeline(range(n_exp_chunks), f71, f72, depth=2)
                if filler is not None:
                    for _ in filler:
                        pass

            def emit_B8(bi):
                si, t0 = blocks[bi]
                bf = bi % 2
                for t2 in range(2):
                    for hf in range(2):
                        S_.op("dve", lambda e, t2=t2, hf=hf: e.tensor_tensor(out=yt[:], in0=psum[t2 * 2 + hf][:], in1=x1[bf][t2][:, hf * 512:(hf + 1) * 512], op=ALU.add),
                              reads=[PS[t2 * 2 + hf], T_x1[bf][t2]], writes=[T_yt])
                        S_.dma("pool", ys[si][t0 + t2 * 128:t0 + (t2 + 1) * 128, hf * 512:(hf + 1) * 512], yt[:], reads=[T_yt], writes=[T_y])

            for _ in gen_pe(0, (5, 6)):
                pass
            for _ in gen_dve(0):
                pass
            for bi in range(len(blocks)):
                emit_B6(gen_pe(bi + 1, (7, 4)) if bi + 1 < len(blocks) else None)
                filler = gen_dve(bi + 1) if bi + 1 < len(blocks) else None
                emit_B7(bi, filler)
                emit_B8(bi)
        S_.finish()
    print("instructions:", S_.ninst)
    return nc


_CONSTS = None


def _in_maps(inputs, seq_lens_per_core, core_seqs):
    global _CONSTS
    if _CONSTS is None:
        _CONSTS = host_consts()
    c = _CONSTS
    base = {
        "rel_bias": inputs["rel_bias"], "attn_norm_g": inputs["attn_norm_g"][0], "w_in": inputs["w_in"][0],
        "q_norm_a": inputs["q_norm_a"][0], "k_norm_a": inputs["k_norm_a"][0], "q_norm_b": inputs["q_norm_b"][0],
        "k_norm_b": inputs["k_norm_b"][0], "lambda_q1": inputs["lambda_q1"][0], "lambda_k1": inputs["lambda_k1"][0],
        "lambda_q2": inputs["lambda_q2"][0], "lambda_k2": inputs["lambda_k2"][0], "diff_norm_g": inputs["diff_norm_g"][0],
        "w_out": inputs["w_out"][0], "ffn_norm_g": inputs["ffn_norm_g"][0], "peer_w_q": inputs["peer_w_q"][0],
        "peer_sub_keys": inputs["peer_sub_keys"][0], "peer_u": inputs["peer_u"][0], "peer_v": inputs["peer_v"][0],
        "c_ident": c["ident"], "c_antiI": c["antiI"], "c_blockones": c["blockones"], "c_iota": c["iota"],
        "c_ohb": c["ohb"], "c_oha": c["oha"], "c_sel65": c["sel65"],
    }
    base = {k: np.ascontiguousarray(np.asarray(v, dtype=np.float32)) for k, v in base.items()}
    maps = []
    for seqs in core_seqs:
        m = dict(base)
        for i, xarr in enumerate(seqs):
            m[f"x{i}"] = np.ascontiguousarray(np.asarray(xarr, dtype=np.float32))
        maps.append(m)
    return maps


def kernel(**inputs):
    inputs = {k: np.asarray(v) for k, v in inputs.items()}
    xp = inputs["x_prompt"]
    xsmp = inputs["x_sample"]
    n = 8
    seq_lens = [xp.shape[1], xsmp.shape[1], xsmp.shape[1]]
    core_seqs = [[xp[c], xsmp[2 * c], xsmp[2 * c + 1]] for c in range(n)]
    nc = build(seq_lens)
    maps = _in_maps(inputs, seq_lens, core_seqs)
    res = run_bass_kernel_spmd(nc, maps, core_ids=list(range(n)))
    yp = np.stack([res.results[c]["y0"] for c in range(n)], axis=0).astype(np.float32)
    ysm = np.empty(xsmp.shape, np.float32)
    for c in range(n):
        ysm[2 * c] = res.results[c]["y1"]
        ysm[2 * c + 1] = res.results[c]["y2"]
    return (yp, ysm)
```
